# Optimizing a Trainium2 kernel written in Bass

```python
import math
import jax, jax.numpy as jnp
from jax import lax
import numpy as np

D_MODEL = 1024
BATCH = 8
SEQ = 2048
DEPTH = 1
DEC_BATCH = 32
DEC_SEQ = 8
PAST_LEN = 16384
PAGE_SIZE = 128

N_META = 16
D_FF = 2816
EPS = 1e-6
D_INNER = 2 * D_MODEL
SSM_HEAD_DIM = 64
SSM_HEADS = D_INNER // SSM_HEAD_DIM
SSM_GROUPS = 4
HEADS_PER_GROUP = SSM_HEADS // SSM_GROUPS
D_STATE = 128
CONV_K = 4
CONV_DIM = D_INNER + 2 * SSM_GROUPS * D_STATE
CHUNK = 128
MLA_HEADS = 16
Q_LORA = D_MODEL // 2
KV_LORA = D_MODEL // 4
QK_NOPE = 64
QK_ROPE = 32
V_HEAD = 64
ROPE_THETA = 10000.0
ATTN_SCALE = (QK_NOPE + QK_ROPE) ** -0.5
Q_BLOCK = 128
IN_SIZES = (D_INNER, CONV_DIM, SSM_HEADS, Q_LORA, KV_LORA, QK_ROPE, D_MODEL, D_MODEL)
N_IN = D_INNER + CONV_DIM + SSM_HEADS + Q_LORA + KV_LORA + QK_ROPE + 2 * D_MODEL

kernel_name = "hybrid_ssd_mla_gated_decoder_step"

F32 = jnp.float32


def rmsnorm(x, g):
    xf = x.astype(F32)
    y = xf * lax.rsqrt(jnp.mean(xf * xf, axis=-1, keepdims=True) + EPS)
    return (y * g.astype(F32)).astype(x.dtype)


def group_rmsnorm(x, g, groups):
    shp = x.shape
    xf = x.astype(F32).reshape(*shp[:-1], groups, shp[-1] // groups)
    y = xf * lax.rsqrt(jnp.mean(xf * xf, axis=-1, keepdims=True) + EPS)
    return (y.reshape(shp) * g.astype(F32)).astype(x.dtype)


def swiglu(x, w_gate, w_up, w_down):
    return (jax.nn.silu(x @ w_gate) * (x @ w_up)) @ w_down


def rope_tables(pos):
    inv = ROPE_THETA ** (-jnp.arange(0, QK_ROPE, 2, dtype=F32) / QK_ROPE)
    ang = pos.astype(F32)[:, None] * inv[None, :]
    return jnp.cos(ang), jnp.sin(ang)


def apply_rope(t, cos, sin):
    tf = t.astype(F32)
    half = QK_ROPE // 2
    t1, t2 = tf[..., :half], tf[..., half:]
    return jnp.concatenate([t1 * cos - t2 * sin, t1 * sin + t2 * cos], axis=-1).astype(t.dtype)


def causal_dwconv(x_hist, w, b):
    out = lax.conv_general_dilated(
        x_hist, w.astype(x_hist.dtype)[:, None, :], window_strides=(1,), padding='VALID',
        dimension_numbers=('NWC', 'WIO', 'NWC'), feature_group_count=x_hist.shape[-1])
    return out + b.astype(x_hist.dtype)


def ssd_scan(x, dt, b_in, c_in, a, d_skip, h0, lead):
    bsz, L = x.shape[0], x.shape[1]
    trail = (-(lead + L)) % CHUNK
    def pad(t):
        return jnp.pad(t.astype(F32), [(0, 0), (lead, trail)] + [(0, 0)] * (t.ndim - 2))
    xp, dtp, bp, cp = pad(x), pad(dt), pad(b_in), pad(c_in)
    nc = xp.shape[1] // CHUNK
    G, E, P, N = SSM_GROUPS, HEADS_PER_GROUP, SSM_HEAD_DIM, D_STATE
    xp = xp.reshape(bsz, nc, CHUNK, G, E, P)
    dtp = dtp.reshape(bsz, nc, CHUNK, G, E)
    bp = bp.reshape(bsz, nc, CHUNK, G, N)
    cp = cp.reshape(bsz, nc, CHUNK, G, N)
    a_cs = jnp.cumsum(dtp * a.astype(F32).reshape(G, E), axis=2)
    xdt = xp * dtp[..., None]
    a_t = jnp.moveaxis(a_cs, 2, -1)
    seg = a_t[..., :, None] - a_t[..., None, :]
    causal = jnp.tril(jnp.ones((CHUNK, CHUNK), dtype=bool))
    decay_in = jnp.exp(jnp.where(causal, seg, -jnp.inf))
    cb = jnp.einsum('bclgn,bcsgn->bcgls', cp, bp)
    y_diag = jnp.einsum('bcgels,bcsgep->bclgep', cb[:, :, :, None] * decay_in, xdt)
    decay_states = jnp.exp(a_cs[:, :, -1:] - a_cs)
    states = jnp.einsum('bclgn,bclge,bclgep->bcgepn', bp, decay_states, xdt)
    chunk_decay = jnp.exp(a_cs[:, :, -1])
    def step(h, inp):
        s, dcy = inp
        return h * dcy[..., None, None] + s, h
    h_init = h0.astype(F32).reshape(bsz, G, E, P, N)
    h_final, h_prev = lax.scan(step, h_init, (jnp.moveaxis(states, 1, 0), jnp.moveaxis(chunk_decay, 1, 0)))
    h_prev = jnp.moveaxis(h_prev, 0, 1)
    y_off = jnp.einsum('bclgn,bcgepn,bclge->bclgep', cp, h_prev, jnp.exp(a_cs))
    y = y_diag + y_off + xp * d_skip.astype(F32).reshape(G, E)[..., None]
    y = y.reshape(bsz, nc * CHUNK, SSM_HEADS, P)[:, lead:lead + L]
    return y.astype(x.dtype), h_final.reshape(bsz, SSM_HEADS, P, N).astype(x.dtype)


def mla_scores(q_lat, q_pe, c_kv, k_pe):
    s = jnp.einsum('bthc,bsc->bhts', q_lat, c_kv) + jnp.einsum('bthr,bsr->bhts', q_pe, k_pe)
    return s.astype(F32) * ATTN_SCALE


def attend_prompt(q_lat, q_pe, c_kv, k_pe):
    bsz, L = q_lat.shape[0], q_lat.shape[1]
    nb = -(-L // Q_BLOCK)
    lq = nb * Q_BLOCK
    def blocks(t):
        t = jnp.pad(t, [(0, 0), (0, lq - L), (0, 0), (0, 0)])
        return jnp.swapaxes(t.reshape(bsz, nb, Q_BLOCK, *t.shape[2:]), 0, 1)
    qpos = jnp.arange(lq).reshape(nb, Q_BLOCK)
    kpos = jnp.arange(L)
    def one_block(args):
        ql, qr, qp = args
        s = mla_scores(ql, qr, c_kv, k_pe)
        s = jnp.where(kpos[None, :] <= qp[:, None], s, -jnp.inf)
        pr = jax.nn.softmax(s, axis=-1).astype(c_kv.dtype)
        return jnp.einsum('bhts,bsc->bthc', pr, c_kv)
    out = lax.map(one_block, (blocks(q_lat), blocks(q_pe), qpos))
    return jnp.swapaxes(out, 0, 1).reshape(bsz, lq, MLA_HEADS, KV_LORA)[:, :L]


def make_attend_sample(past_c, past_kr):
    def attend(q_lat, q_pe, c_kv, k_pe):
        t = q_lat.shape[1]
        n_past = past_c.shape[1]
        s_past = mla_scores(q_lat, q_pe, past_c, past_kr)
        s_new = mla_scores(q_lat, q_pe, c_kv, k_pe)
        s_new = jnp.where(jnp.tril(jnp.ones((t, t), dtype=bool)), s_new, -jnp.inf)
        pr = jax.nn.softmax(jnp.concatenate([s_past, s_new], axis=-1), axis=-1).astype(c_kv.dtype)
        return (jnp.einsum('bhts,bsc->bthc', pr[..., :n_past], past_c)
                + jnp.einsum('bhts,bsc->bthc', pr[..., n_past:], c_kv))
    return attend


def token_mixer(u, conv_hist, h0, pos, lead, attend, p):
    bsz, L = u.shape[0], u.shape[1]
    proj = u @ p['w_in']
    offs, acc = [], 0
    for s in IN_SIZES[:-1]:
        acc += s
        offs.append(acc)
    z, xbc, dt_raw, q_a, kv_a, k_pe, gate_a, gate_b = jnp.split(proj, offs, axis=-1)
    xbc_hist = jnp.concatenate([conv_hist.astype(xbc.dtype), xbc], axis=1)
    conv_new = xbc_hist[:, -(CONV_K - 1):]
    xbc_c = jax.nn.silu(causal_dwconv(xbc_hist, p['conv_w'], p['conv_b']))
    xs, b_in, c_in = jnp.split(xbc_c, [D_INNER, D_INNER + SSM_GROUPS * D_STATE], axis=-1)
    dt = jax.nn.softplus(dt_raw.astype(F32) + p['dt_bias'].astype(F32))
    a = -jnp.exp(p['a_log'].astype(F32))
    y, h_new = ssd_scan(xs.reshape(bsz, L, SSM_HEADS, SSM_HEAD_DIM), dt,
                        b_in.reshape(bsz, L, SSM_GROUPS, D_STATE), c_in.reshape(bsz, L, SSM_GROUPS, D_STATE),
                        a, p['d_skip'], h0, lead)
    y = y.reshape(bsz, L, D_INNER) * jax.nn.silu(z)
    branch_a = group_rmsnorm(y, p['ssm_norm_g'], SSM_GROUPS) @ p['w_a_out']
    q = (rmsnorm(q_a, p['q_a_norm_g']) @ p['w_q_b']).reshape(bsz, L, MLA_HEADS, QK_NOPE + QK_ROPE)
    q_nope, q_pe = q[..., :QK_NOPE], q[..., QK_NOPE:]
    cos, sin = rope_tables(pos)
    q_pe = apply_rope(q_pe, cos[:, None, :], sin[:, None, :])
    k_pe = apply_rope(k_pe, cos, sin)
    c_kv = rmsnorm(kv_a, p['kv_a_norm_g'])
    q_lat = jnp.einsum('bthn,chn->bthc', q_nope, p['w_uk'])
    o_lat = attend(q_lat, q_pe, c_kv, k_pe)
    o = jnp.einsum('bthc,chv->bthv', o_lat, p['w_uv']).reshape(bsz, L, MLA_HEADS * V_HEAD)
    branch_b = o @ p['w_b_out']
    merged = jax.nn.sigmoid(gate_a) * branch_a + jax.nn.sigmoid(gate_b) * branch_b
    return merged @ p['w_o'], c_kv, k_pe, h_new, conv_new


def decoder_layer(x, conv_hist, h0, pos, lead, attend, p):
    h = x + 0.5 * rmsnorm(swiglu(rmsnorm(x, p['ffn1_pre_g']), p['ffn1_w_gate'], p['ffn1_w_up'], p['ffn1_w_down']),
                          p['ffn1_post_g'])
    mix, c_kv, k_pe, h_new, conv_new = token_mixer(rmsnorm(h, p['mix_pre_g']), conv_hist, h0, pos, lead, attend, p)
    h = h + rmsnorm(mix, p['mix_post_g'])
    h = h + 0.5 * rmsnorm(swiglu(rmsnorm(h, p['ffn2_pre_g']), p['ffn2_w_gate'], p['ffn2_w_up'], p['ffn2_w_down']),
                          p['ffn2_post_g'])
    return h, c_kv, k_pe, h_new, conv_new


def setup_inputs(seed: int = 0) -> dict:
    key = jax.random.key(seed)
    ks = iter(jax.random.split(key, 64))
    def nrm(shape, scale):
        return jax.random.normal(next(ks), shape, F32) * scale
    def gain(shape):
        return 1.0 + 0.05 * jax.random.normal(next(ks), shape, F32)
    n_pages = PAST_LEN // PAGE_SIZE
    n_used = DEC_BATCH * n_pages
    n_pool = n_used + n_used // 4
    perm = jax.random.permutation(next(ks), n_pool).astype(jnp.int32)
    page_table = perm[:n_used].reshape(DEC_BATCH, n_pages)
    dt0 = jnp.exp(jax.random.uniform(next(ks), (DEPTH, SSM_HEADS), F32, math.log(1e-3), math.log(1e-1)))
    dt_bias = dt0 + jnp.log(-jnp.expm1(-dt0))
    a_log = jnp.log(jax.random.uniform(next(ks), (DEPTH, SSM_HEADS), F32, 1.0, 16.0))
    return {
        'x_prompt': nrm((BATCH, SEQ, D_MODEL), 1.0),
        'x_sample': nrm((DEC_BATCH, DEC_SEQ, D_MODEL), 1.0),
        'cache_kv_latent': nrm((DEPTH, n_pool, PAGE_SIZE, KV_LORA), 1.0),
        'cache_k_rope': nrm((DEPTH, n_pool, PAGE_SIZE, QK_ROPE), 1.0),
        'state_ssm': nrm((DEPTH, DEC_BATCH, SSM_HEADS, SSM_HEAD_DIM, D_STATE), 0.3),
        'state_conv': nrm((DEPTH, DEC_BATCH, CONV_K - 1, CONV_DIM), 1.0),
        'page_table': page_table,
        'meta_tokens': nrm((N_META, D_MODEL), 1.0),
        'ffn1_pre_g': gain((DEPTH, D_MODEL)),
        'ffn1_w_gate': nrm((DEPTH, D_MODEL, D_FF), D_MODEL ** -0.5),
        'ffn1_w_up': nrm((DEPTH, D_MODEL, D_FF), D_MODEL ** -0.5),
        'ffn1_w_down': nrm((DEPTH, D_FF, D_MODEL), D_FF ** -0.5),
        'ffn1_post_g': gain((DEPTH, D_MODEL)),
        'mix_pre_g': gain((DEPTH, D_MODEL)),
        'w_in': nrm((DEPTH, D_MODEL, N_IN), D_MODEL ** -0.5),
        'conv_w': nrm((DEPTH, CONV_K, CONV_DIM), CONV_K ** -0.5),
        'conv_b': nrm((DEPTH, CONV_DIM), 0.02),
        'dt_bias': dt_bias,
        'a_log': a_log,
        'd_skip': gain((DEPTH, SSM_HEADS)),
        'ssm_norm_g': gain((DEPTH, D_INNER)),
        'q_a_norm_g': gain((DEPTH, Q_LORA)),
        'w_q_b': nrm((DEPTH, Q_LORA, MLA_HEADS * (QK_NOPE + QK_ROPE)), Q_LORA ** -0.5),
        'kv_a_norm_g': gain((DEPTH, KV_LORA)),
        'w_uk': nrm((DEPTH, KV_LORA, MLA_HEADS, QK_NOPE), KV_LORA ** -0.5),
        'w_uv': nrm((DEPTH, KV_LORA, MLA_HEADS, V_HEAD), KV_LORA ** -0.5),
        'w_a_out': nrm((DEPTH, D_INNER, D_MODEL), D_INNER ** -0.5),
        'w_b_out': nrm((DEPTH, MLA_HEADS * V_HEAD, D_MODEL), (MLA_HEADS * V_HEAD) ** -0.5),
        'w_o': nrm((DEPTH, D_MODEL, D_MODEL), D_MODEL ** -0.5),
        'mix_post_g': gain((DEPTH, D_MODEL)),
        'ffn2_pre_g': gain((DEPTH, D_MODEL)),
        'ffn2_w_gate': nrm((DEPTH, D_MODEL, D_FF), D_MODEL ** -0.5),
        'ffn2_w_up': nrm((DEPTH, D_MODEL, D_FF), D_MODEL ** -0.5),
        'ffn2_w_down': nrm((DEPTH, D_FF, D_MODEL), D_FF ** -0.5),
        'ffn2_post_g': gain((DEPTH, D_MODEL)),
    }


def reference(x_prompt, x_sample, cache_kv_latent, cache_k_rope, state_ssm, state_conv, page_table, meta_tokens,
              ffn1_pre_g, ffn1_w_gate, ffn1_w_up, ffn1_w_down, ffn1_post_g,
              mix_pre_g, w_in, conv_w, conv_b, dt_bias, a_log, d_skip, ssm_norm_g,
              q_a_norm_g, w_q_b, kv_a_norm_g, w_uk, w_uv, w_a_out, w_b_out, w_o, mix_post_g,
              ffn2_pre_g, ffn2_w_gate, ffn2_w_up, ffn2_w_down, ffn2_post_g):
    bp, seq = x_prompt.shape[0], x_prompt.shape[1]
    bs, tdec = x_sample.shape[0], x_sample.shape[1]
    past_len = page_table.shape[1] * cache_kv_latent.shape[2]
    hp = jnp.concatenate([jnp.broadcast_to(meta_tokens.astype(x_prompt.dtype)[None], (bp, N_META, D_MODEL)),
                          x_prompt], axis=1)
    pos_p = jnp.arange(N_META + seq)
    lead_p = (-N_META) % CHUNK
    conv0_p = jnp.zeros((bp, CONV_K - 1, CONV_DIM), x_prompt.dtype)
    ssm0_p = jnp.zeros((bp, SSM_HEADS, SSM_HEAD_DIM, D_STATE), x_prompt.dtype)
    hs = x_sample
    pos_s = past_len + jnp.arange(tdec)
    kvp, krp, ssp, cvp, kvs, krs, sss, cvs = [], [], [], [], [], [], [], []
    for l in range(DEPTH):
        p = dict(ffn1_pre_g=ffn1_pre_g[l], ffn1_w_gate=ffn1_w_gate[l], ffn1_w_up=ffn1_w_up[l],
                 ffn1_w_down=ffn1_w_down[l], ffn1_post_g=ffn1_post_g[l], mix_pre_g=mix_pre_g[l],
                 w_in=w_in[l], conv_w=conv_w[l], conv_b=conv_b[l], dt_bias=dt_bias[l], a_log=a_log[l],
                 d_skip=d_skip[l], ssm_norm_g=ssm_norm_g[l], q_a_norm_g=q_a_norm_g[l], w_q_b=w_q_b[l],
                 kv_a_norm_g=kv_a_norm_g[l], w_uk=w_uk[l], w_uv=w_uv[l], w_a_out=w_a_out[l],
                 w_b_out=w_b_out[l], w_o=w_o[l], mix_post_g=mix_post_g[l], ffn2_pre_g=ffn2_pre_g[l],
                 ffn2_w_gate=ffn2_w_gate[l], ffn2_w_up=ffn2_w_up[l], ffn2_w_down=ffn2_w_down[l],
                 ffn2_post_g=ffn2_post_g[l])
        hp, c_p, k_p, h_p, cv_p = decoder_layer(hp, conv0_p, ssm0_p, pos_p, lead_p, attend_prompt, p)
        past_c = cache_kv_latent[l][page_table].reshape(bs, past_len, KV_LORA).astype(x_sample.dtype)
        past_kr = cache_k_rope[l][page_table].reshape(bs, past_len, QK_ROPE).astype(x_sample.dtype)
        hs, c_s, k_s, h_s, cv_s = decoder_layer(hs, state_conv[l], state_ssm[l], pos_s, 0,
                                                make_attend_sample(past_c, past_kr), p)
        kvp.append(c_p); krp.append(k_p); ssp.append(h_p); cvp.append(cv_p)
        kvs.append(c_s); krs.append(k_s); sss.append(h_s); cvs.append(cv_s)
    y_prompt = hp[:, N_META:]
    y_sample = hs
    return (y_prompt, y_sample,
            jnp.stack(kvp), jnp.stack(krp), jnp.stack(ssp), jnp.stack(cvp),
            jnp.stack(kvs), jnp.stack(krs), jnp.stack(sss), jnp.stack(cvs))
```

```python
import os
import numpy as np
import concourse.bass as bass
import concourse.mybir as mybir
from concourse.bass_utils import run_bass_kernel_spmd

F32 = mybir.dt.float32
BF16 = mybir.dt.bfloat16
I32 = mybir.dt.int32
AF = mybir.ActivationFunctionType
ALU = mybir.AluOpType
AX = mybir.AxisListType

N_DMA_SEMS = 24


class TT:
    __slots__ = ("name", "t", "last_w", "reads", "excl")

    def __init__(self, name, t=None):
        self.name = name
        self.t = t
        self.last_w = None
        self.reads = []
        self.excl = False

    def __getitem__(self, k):
        return self.t[k]


class _Op:
    __slots__ = ("fn", "waits", "tok", "dma", "needed")

    def __init__(self, fn, waits, tok, dma):
        self.fn = fn
        self.waits = waits
        self.tok = tok
        self.dma = dma
        self.needed = False


class FW:
    COMPUTE = ("pe", "act", "dve", "pool")

    def __init__(self, nc):
        self.nc = nc
        self.q = {k: [] for k in ("pe", "act", "dve", "pool", "sp")}
        self.cnt = {k: 0 for k in self.COMPUTE}
        self.dma_rr = {"sp": 0, "pool": 0, "act": 0}
        self.dma_cum = {}
        self.dma_last = {}
        self.same_engine_sync = (os.environ.get("SES", "1") == "1")
        self.raw_only = (os.environ.get("RAWONLY", "1") == "1")
        self.final_tokens = []
        self.optok = {}
        self.banks = None
        self.bank_i = 0
        self.scopes = []
        self.pending = {}
        self.reserved = 0

    def sb(self, name, shape, dt=F32):
        if self.scopes:
            return TT(name, self.scopes[-1].enter_context(self.nc.sbuf_tensor(name, list(shape), dt)))
        return TT(name, self.nc.alloc_sbuf_tensor(name, list(shape), dt))

    def push_scope(self):
        import contextlib
        self.scopes.append(contextlib.ExitStack())

    def pop_scope(self):
        self.barrier()
        self.scopes.pop().close()

    def barrier(self):
        w = {}
        for e in self.COMPUTE:
            if self.cnt[e] > 0:
                w[e] = self.cnt[e]
        for key, tok in self.dma_last.items():
            w[key] = tok[1]
        for e in self.q:
            self.pending[e] = dict(w)

    def region(self, name):
        return TT(name, None)

    def init_banks(self):
        self.banks = [TT("bank%d" % i, self.nc.alloc_psum_tensor("bank%d" % i, [128, 512], F32)) for i in range(8)]
        for b in self.banks:
            b.excl = True

    def bank(self):
        n = 8 - self.reserved
        self.bank_i = self.bank_i % n
        b = self.banks[self.bank_i]
        self.bank_i = (self.bank_i + 1) % n
        return b

    def reserve(self, n):
        self.reserved = n
        return [self.banks[8 - 1 - i] for i in range(n)]

    def op(self, eng, fn, r=(), w=(), dma=False, final=False):
        deps = []
        deps2 = []
        for t in r:
            if t.last_w is not None:
                deps.append(t.last_w)
            if t.excl:
                deps.extend(x for x in t.reads if x[0] != eng)
        for t in w:
            if t.last_w is not None:
                deps2.append(t.last_w)
            deps2.extend(t.reads)
        if dma or not self.raw_only:
            deps.extend(deps2)
        else:
            deps.extend(x for x in deps2 if x[0] != eng)
        if dma:
            i = self.dma_rr[eng]
            self.dma_rr[eng] = (i + 1) % N_DMA_SEMS
            key = ("dma", eng, i)
            prev = self.dma_last.get(key)
            if prev is not None:
                deps.append(prev)
            val = self.dma_cum.get(key, 0) + 16
            self.dma_cum[key] = val
            tok = (key, val)
            self.dma_last[key] = tok
        else:
            self.cnt[eng] += 1
            tok = (eng, self.cnt[eng])
        waits = {}
        if self.pending.get(eng):
            for k, v in self.pending[eng].items():
                if k == eng:
                    continue
                waits[k] = v
            self.pending[eng] = None
        for (k, v) in deps:
            if k == eng and not dma:
                if eng == "pe" or not self.same_engine_sync:
                    continue
            if waits.get(k, 0) < v:
                waits[k] = v
        o = _Op(fn, waits, tok, dma)
        self.q[eng].append(o)
        if not dma:
            self.optok[tok] = o
        for t in r:
            t.reads.append(tok)
        for t in w:
            t.last_w = tok
            t.reads = []
        if final:
            self.final_tokens.append(tok)
        return tok

    def mm(self, out, lhsT, rhs, start, stop):
        self.op("pe", lambda e: e.matmul(out=out[1], lhsT=lhsT[1], rhs=rhs[1], start=start, stop=stop),
                r=[lhsT[0], rhs[0]], w=[out[0]])

    def tr(self, out, in_, ident):
        self.op("pe", lambda e: e.transpose(out=out[1], in_=in_[1], identity=ident[1]), r=[in_[0], ident[0]], w=[out[0]])

    def act(self, out, in_, func, bias=None, scale=None, accum=None, eng="act"):
        r = [in_[0]]
        kw = {}
        if bias is not None:
            if isinstance(bias, tuple):
                r.append(bias[0])
                kw["bias"] = bias[1]
            else:
                kw["bias"] = bias
        if scale is not None:
            if isinstance(scale, tuple):
                r.append(scale[0])
                kw["scale"] = scale[1]
            else:
                kw["scale"] = scale
        w = [out[0]]
        if accum is not None:
            w.append(accum[0])
            kw["accum_out"] = accum[1]
        self.op(eng, lambda e: e.activation(out=out[1], in_=in_[1], func=func, **kw), r=r, w=w)

    def ts(self, eng, out, in0, s1, op0, s2=None, op1=None):
        r = [in0[0]]
        a1 = s1
        if isinstance(s1, tuple):
            r.append(s1[0])
            a1 = s1[1]
        a2 = s2
        if isinstance(s2, tuple):
            r.append(s2[0])
            a2 = s2[1]
        if op1 is None:
            self.op(eng, lambda e: e.tensor_scalar(out=out[1], in0=in0[1], scalar1=a1, scalar2=None, op0=op0), r=r, w=[out[0]])
        else:
            self.op(eng, lambda e: e.tensor_scalar(out=out[1], in0=in0[1], scalar1=a1, scalar2=a2, op0=op0, op1=op1), r=r, w=[out[0]])

    def tt(self, eng, out, in0, in1, op):
        self.op(eng, lambda e: e.tensor_tensor(out=out[1], in0=in0[1], in1=in1[1], op=op), r=[in0[0], in1[0]], w=[out[0]])

    def stt(self, out, in0, scalar, in1, op0, op1):
        r = [in0[0], in1[0]]
        a = scalar
        if isinstance(scalar, tuple):
            r.append(scalar[0])
            a = scalar[1]
        self.op("dve", lambda e: e.scalar_tensor_tensor(out=out[1], in0=in0[1], scalar=a, in1=in1[1], op0=op0, op1=op1), r=r, w=[out[0]])

    def cp(self, eng, out, in_):
        if eng == "act":
            self.op(eng, lambda e: e.copy(out=out[1], in_=in_[1]), r=[in_[0]], w=[out[0]])
        else:
            self.op(eng, lambda e: e.tensor_copy(out=out[1], in_=in_[1]), r=[in_[0]], w=[out[0]])

    def memset(self, eng, out, val):
        self.op(eng, lambda e: e.memset(out[1], val), w=[out[0]])

    def recip(self, out, in_):
        self.op("dve", lambda e: e.reciprocal(out=out[1], in_=in_[1]), r=[in_[0]], w=[out[0]])

    def dma(self, q, out, in_, final=False, slow=False):
        if slow:
            self.op(q, lambda e: e.dma_start(out=out[1], in_=in_[1], allow_slow_non_contiguous=True), r=[in_[0]], w=[out[0]], dma=True, final=final)
        else:
            self.op(q, lambda e: e.dma_start(out=out[1], in_=in_[1]), r=[in_[0]], w=[out[0]], dma=True, final=final)

    def gather(self, out, in_ap, in_reg, idx):
        self.op("pool", lambda e: e.indirect_dma_start(out=out[1], out_offset=None, in_=in_ap,
                                                         in_offset=bass.IndirectOffsetOnAxis(ap=idx[1], axis=0)),
                r=[in_reg, idx[0]], w=[out[0]], dma=True)

    def emit(self):
        nc = self.nc
        for qn, ops in self.q.items():
            for o in ops:
                for (k, v) in o.waits.items():
                    if isinstance(k, str):
                        self.optok[(k, v)].needed = True
        for tok in self.final_tokens:
            if isinstance(tok[0], str):
                self.optok[tok].needed = True
        remap = {}
        for e in self.COMPUTE:
            c = 0
            for o in self.q[e]:
                if o.dma:
                    continue
                if o.needed:
                    c += 1
                remap[o.tok] = c
        sems = {e: nc.alloc_semaphore("s_" + e) for e in self.COMPUTE}
        dsems = {}
        for key in self.dma_cum:
            dsems[key] = nc.alloc_semaphore("d_%s_%d" % (key[1], key[2]))
        finals = list(self.final_tokens)

        def run_queue(qn, eng):
            known = {}
            for o in self.q[qn]:
                for (k, v) in o.waits.items():
                    if isinstance(k, str):
                        v2 = remap[(k, v)]
                        s = sems[k]
                    else:
                        v2 = v
                        s = dsems[k]
                    if known.get(k, 0) >= v2:
                        continue
                    known[k] = v2
                    eng.wait_ge(s, v2)
                ins = o.fn(eng)
                if o.dma:
                    ins.then_inc(dsems[o.tok[0]], 16)
                elif o.needed:
                    ins.then_inc(sems[qn], 1)
            if qn == "sp":
                for tok in finals:
                    k, v = tok
                    if isinstance(k, str):
                        eng.wait_ge(sems[k], remap[tok])
                    else:
                        eng.wait_ge(dsems[k], v)

        with nc.Block() as block:
            @block.sync
            def _(e):
                run_queue("sp", e)

            @block.tensor
            def _(e):
                run_queue("pe", e)

            @block.scalar
            def _(e):
                run_queue("act", e)

            @block.vector
            def _(e):
                run_queue("dve", e)

            @block.gpsimd
            def _(e):
                run_queue("pool", e)


class Ring:
    def __init__(self, fw, name, n, shape, dt=F32):
        self.tiles = [fw.sb("%s%d" % (name, i), shape, dt) for i in range(n)]
        self.i = 0

    def next(self):
        t = self.tiles[self.i]
        self.i = (self.i + 1) % len(self.tiles)
        return t


D = 1024
DFF = 2816
NMETA = 16
DI = 2048
NH = 32
HD = 64
NG = 4
DS = 128
CK = 4
CD = 3072
MH = 16
QL = 512
KL = 256
NOPE = 64
ROPE = 32
VH = 64
EPS = 1e-6
SCALE = (NOPE + ROPE) ** -0.5
NIN = 8000
NSAMP = 4
TS = 8
NEGV = -30000.0


def tiles_of(a, b, step):
    out = []
    c = a
    while c < b:
        out.append((c, min(step, b - c)))
        c += step
    return out


def build(SEQ, NPG, NPOOL, debug=()):
    U = 3 + NMETA + SEQ + NSAMP * 11
    P0 = 3
    PL = NMETA + SEQ
    SB0 = 3 + PL
    nc = bass.Bass("TRN2", target_bir_lowering=False)
    fw = FW(nc)
    fw.init_banks()

    def din(name, shape, dt=F32):
        return nc.dram_tensor(name, list(shape), dt, kind="ExternalInput").ap()

    def dout(name, shape, dt=F32):
        return nc.dram_tensor(name, list(shape), dt, kind="ExternalOutput").ap()

    def dscr(name, shape, dt=F32):
        kind = "ExternalOutput" if name in debug else "Internal"
        return nc.dram_tensor(name, list(shape), dt, kind=kind).ap()

    xin = din("xin", [U, D])
    W = {}
    for nm, shp in [("ffn1_w_gate", [D, DFF]), ("ffn1_w_up", [D, DFF]), ("ffn1_w_down", [DFF, D]),
                    ("ffn2_w_gate", [D, DFF]), ("ffn2_w_up", [D, DFF]), ("ffn2_w_down", [DFF, D]),
                    ("w_in", [D, NIN]), ("w_kpe_sw", [D, ROPE]), ("w_q_b", [QL, MH * 96]), ("w_q_b_sw", [QL, MH * 96]),
                    ("w_uk", [KL, MH * NOPE]), ("w_ukT", [NOPE, MH * KL]), ("w_uv", [KL, MH * VH]),
                    ("w_a_out", [DI, D]), ("w_b_out", [MH * VH, D]), ("w_o", [D, D])]:
        W[nm] = din(nm, shp)
    gcols = din("gcols", [128, 3, 8])
    grows = din("grows", [1, 3 * D + DI + QL + KL + 3 * NH])
    gncol = din("gncol", [128, 16])
    convw = din("convw", [128, 24 * 4])
    convb = din("convb", [128, 24])
    cf32 = din("cf32", [128, 128 * 4 + NH * 128 + 1])
    cbf = din("cbf", [128, 128 + 4 * 512 + 128])
    sel3_d = din("sel3", [96, NH * 128])
    cs96 = din("cs96", [96, 2, U])
    csU = din("csU", [U, 2, 32])
    cache_kv = din("cache_kv", [NPOOL * 32, 4 * KL])
    cache_kr = din("cache_kr", [NPOOL * 32, 4 * ROPE])
    ptab = din("ptab", [1, NSAMP * NPG], I32)
    st_ssm = din("st_ssm", [NSAMP, DI, DS])
    st_conv = din("st_conv", [NSAMP * 3, CD])

    yout = dout("yout", [U, D])
    kvlat = dout("kvlat", [U, KL])
    krope = dout("krope", [U, ROPE])
    ssm_o = dout("ssm_o", [1 + NSAMP, DI, DS])
    conv_o = dout("conv_o", [47, CD])

    res1 = dscr("res1", [U, D])
    res2 = dscr("res2", [U, D])
    zs_d = dscr("zs_d", [U, DI], BF16)
    xc_d = dscr("xc_d", [24, 128, U], BF16)
    dtr_d = dscr("dtr_d", [U, NH])
    gT_d = dscr("gT_d", [16, 128, U], BF16)
    mT_d = dscr("mT_d", [8, 128, U], BF16)
    oT_d = dscr("oT_d", [MH, 64, U], BF16)
    R = {k: fw.region(k) for k in ["xin", "w", "res1", "res2", "zs", "xc", "dtr", "gT", "mT", "oT", "yout", "kvlat", "krope", "ssm_o", "conv_o", "consts", "cache"]}

    identf = fw.sb("identf", [128, 128])
    tri = fw.sb("tri", [128, 128])
    tris = fw.sb("tris", [128, 128])
    onesf = fw.sb("onesf", [128, 128])
    identb = fw.sb("identb", [128, 128], BF16)
    cbf_sb = fw.sb("cbf_sb", [128, 128 + 4 * 512 + 128], BF16)
    gcol_sb = fw.sb("gcol_sb", [128, 3, 8])
    eps_sb = fw.sb("eps_sb", [128, 1])
    fw.dma("sp", (identf, identf[:]), (R["consts"], cf32[:, 0:128]))
    fw.dma("sp", (tri, tri[:]), (R["consts"], cf32[:, 128:256]))
    fw.dma("sp", (tris, tris[:]), (R["consts"], cf32[:, 256:384]))
    fw.dma("sp", (onesf, onesf[:]), (R["consts"], cf32[:, 384:512]))
    fw.dma("pool", (cbf_sb, cbf_sb[:]), (R["consts"], cbf))
    fw.dma("sp", (gcol_sb, gcol_sb[:]), (R["consts"], gcols))
    fw.cp("dve", (identb, identb[:]), (identf, identf[:]))
    fw.memset("dve", (eps_sb, eps_sb[:]), EPS)
    negsl = cbf_sb[:, 0:128]
    qaT = fw.sb("qaT", [128, 4, U], BF16)
    ckvT = fw.sb("ckvT", [128, 2, U], BF16)
    kpeT = fw.sb("kpeT", [96, U], BF16)

    def negm(r):
        return cbf_sb[:, 128 + r * 512:128 + (r + 1) * 512]
    neg8 = cbf_sb[0:8, 128 + 2048:128 + 2048 + 128]

    def bcast_row(name, off, n, dt=F32):
        t = fw.sb(name, [128, n], dt)
        fw.dma("sp" if dt == F32 else "pool", (t, t[:]), (R["consts"], grows[0, off:off + n].partition_broadcast(128)))
        return t

    rowtiles = tiles_of(0, U, 128)

    def rstd_from(ss_tt, ss_ap, n, rows, out_tt, out_ap):
        fw.act((out_tt, out_ap), (ss_tt, ss_ap), AF.Sqrt, bias=(eps_sb, eps_sb[0:rows, :]), scale=1.0 / n)
        fw.recip((out_tt, out_ap), (out_tt, out_ap))

    class _H:
        pass
    NR = _H()
    st_ring = Ring(fw, "stat", 4, [128, 8])
    nr_cnt = [0]

    def alloc_norm_rings():
        nr_cnt[0] += 1
        NR.xrow = Ring(fw, "xrow%d_" % nr_cnt[0], 2, [128, D])
        NR.junk = Ring(fw, "junk%d_" % nr_cnt[0], 2, [128, D])
        NR.xn = Ring(fw, "xnb%d_" % nr_cnt[0], 2, [128, D], BF16)

    def norm_transpose(src_ap, src_reg, which, xT, c_lo, c_hi):
        for (r0, rn) in tiles_of(c_lo, c_hi, 128):
            xr = NR.xrow.next()
            fw.dma("sp", (xr, xr[0:rn, :]), (src_reg, src_ap[r0:r0 + rn, :]))
            jk = NR.junk.next()
            st = st_ring.next()
            fw.act((jk, jk[0:rn, :]), (xr, xr[0:rn, :]), AF.Square, accum=(st, st[0:rn, 0:1]))
            rstd_from(st, st[0:rn, 0:1], D, rn, st, st[0:rn, 1:2])
            xn = NR.xn.next()
            fw.ts("dve", (xn, xn[0:rn, :]), (xr, xr[0:rn, :]), (st, st[0:rn, 1:2]), ALU.mult)
            for half in range(2):
                bk = fw.bank()
                bv = bk.t[:].bitcast(BF16)
                for kk in range(4):
                    k = half * 4 + kk
                    fw.tr((bk, bv[:, kk * 128:kk * 128 + rn]), (xn, xn[0:rn, k * 128:(k + 1) * 128]), (identb, identb[0:rn, 0:rn]))
                for kk in range(4):
                    k = half * 4 + kk
                    eng = "dve" if kk % 2 == 0 else "pool"
                    if eng == "pool":
                        fw.act((xT, xT[:, k, r0 - c_lo:r0 - c_lo + rn]), (bk, bv[:, kk * 128:kk * 128 + rn]), AF.Copy,
                               scale=(gcol_sb, gcol_sb[:, which, k:k + 1]))
                    else:
                        fw.ts("dve", (xT, xT[:, k, r0 - c_lo:r0 - c_lo + rn]), (bk, bv[:, kk * 128:kk * 128 + rn]),
                              (gcol_sb, gcol_sb[:, which, k:k + 1]), ALU.mult)

    def postnorm_residual(banks, rn, g_bc, res_ap, res_reg, dst_ap, dst_reg, r0, final=False):
        st = st_ring.next()
        for half in range(2):
            jk = NR.junk.next()
            fw.act((jk, jk[0:rn, 0:512]), (banks[half], banks[half].t[0:rn, :]), AF.Square, accum=(st, st[0:rn, half:half + 1]))
        fw.tt("dve", (st, st[0:rn, 2:3]), (st, st[0:rn, 0:1]), (st, st[0:rn, 1:2]), ALU.add)
        rstd_from(st, st[0:rn, 2:3], D, rn, st, st[0:rn, 3:4])
        xr = NR.xrow.next()
        fw.dma("sp", (xr, xr[0:rn, :]), (res_reg, res_ap[r0:r0 + rn, :]))
        o = NR.junk.next()
        for half in range(2):
            fw.stt((o, o[0:rn, half * 512:(half + 1) * 512]), (banks[half], banks[half].t[0:rn, :]), (st, st[0:rn, 3:4]),
                   (g_bc, g_bc[0:rn, half * 512:(half + 1) * 512]), ALU.mult, ALU.mult)
        fw.tt("pool", (o, o[0:rn, :]), (o, o[0:rn, :]), (xr, xr[0:rn, :]), ALU.add)
        fw.dma("sp", (dst_reg, dst_ap[r0:r0 + rn, :]), (o, o[0:rn, :]), final=final)

    wring_h = [None]

    def load_w(w_ap, c0, cn, kchunks=8):
        t = wring_h[0].next()
        fw.dma("pool", (t, t[:, 0:kchunks, 0:cn]), (R["w"], w_ap[:, c0:c0 + cn].rearrange("(k p) n -> p k n", p=128)))
        return t

    def ffn(pfx, which_pre, gpost_off, src_ap, src_reg, dst_ap, dst_reg, final):
        fw.push_scope()
        alloc_norm_rings()
        wring_h[0] = Ring(fw, "wst_" + pfx, 4, [128, 8, 512], BF16)
        gpost = bcast_row("gpost_" + pfx, gpost_off, D)
        fw.ts("dve", (gpost, gpost[:]), (gpost, gpost[:]), 0.5, ALU.mult)
        supers = tiles_of(0, U, 768)
        supers = [(a, a + n) for (a, n) in supers]
        maxc = max(b - a for a, b in supers)
        xT = fw.sb("xT_" + pfx, [128, 8, maxc], BF16)
        hT = fw.sb("hT_" + pfx, [128, 22, maxc], BF16)
        wd = fw.sb("wd_" + pfx, [128, 22, D], BF16)
        for kk in range(0, 22, 2):
            fw.dma("pool", (wd, wd[:, kk:kk + 2, :]), (R["w"], W[pfx + "_w_down"][kk * 128:(kk + 2) * 128, :].rearrange("(k p) n -> p k n", p=128)))
        sg_ring = Ring(fw, "sg_" + pfx, 2, [128, 512])
        for (c_lo, c_hi) in supers:
            norm_transpose(src_ap, src_reg, which_pre, xT, c_lo, c_hi)
            for (g0, gn) in tiles_of(0, DFF, 512):
                wg = load_w(W[pfx + "_w_gate"], g0, gn)
                wu = load_w(W[pfx + "_w_up"], g0, gn)
                for jj in range(gn // 128):
                    j = g0 // 128 + jj
                    for (t0, tn) in tiles_of(0, c_hi - c_lo, 512):
                        bg = fw.bank()
                        bu = fw.bank()
                        for k in range(8):
                            fw.mm((bg, bg.t[:, 0:tn]), (wg, wg[:, k, jj * 128:(jj + 1) * 128]), (xT, xT[:, k, t0:t0 + tn]), k == 0, k == 7)
                        for k in range(8):
                            fw.mm((bu, bu.t[:, 0:tn]), (wu, wu[:, k, jj * 128:(jj + 1) * 128]), (xT, xT[:, k, t0:t0 + tn]), k == 0, k == 7)
                        sg = sg_ring.next()
                        fw.act((sg, sg[:, 0:tn]), (bg, bg.t[:, 0:tn]), AF.Silu)
                        fw.tt("dve", (hT, hT[:, j, t0:t0 + tn]), (sg, sg[:, 0:tn]), (bu, bu.t[:, 0:tn]), ALU.mult)
            for (r0, rn) in tiles_of(c_lo, c_hi, 128):
                bks = [fw.bank(), fw.bank()]
                for half in range(2):
                    for j in range(22):
                        fw.mm((bks[half], bks[half].t[0:rn, :]), (hT, hT[:, j, r0 - c_lo:r0 - c_lo + rn]),
                              (wd, wd[:, j, half * 512:(half + 1) * 512]), j == 0, j == 21)
                postnorm_residual(bks, rn, gpost, src_ap, src_reg, dst_ap, dst_reg, r0, final=final)
        fw.pop_scope()


    OFF_Z, OFF_X, OFF_DT, OFF_QA, OFF_KV, OFF_KPE, OFF_GA = 0, DI, DI + CD, DI + CD + NH, DI + CD + NH + QL, DI + CD + NH + QL + KL, DI + CD + NH + QL + KL + ROPE
    GOFF = {"ssm": 3 * D, "qa": 3 * D + DI, "kv": 3 * D + DI + QL, "dtb": 3 * D + DI + QL + KL, "alog": 3 * D + DI + QL + KL + NH, "dsk": 3 * D + DI + QL + KL + 2 * NH}
    SB = [SB0 + 11 * j for j in range(NSAMP)]

    def mix_proj():
        fw.push_scope()
        alloc_norm_rings()
        wring_h[0] = Ring(fw, "wst_mp", 4, [128, 8, 512], BF16)
        xT = fw.sb("xTm", [128, 8, U], BF16)
        norm_transpose(res1, R["res1"], 1, xT, 0, U)
        o512 = Ring(fw, "o512", 3, [128, 512], BF16)
        for (g0, gn) in tiles_of(OFF_Z, OFF_X, 512):
            wt = load_w(W["w_in"], g0, gn)
            for (r0, rn) in rowtiles:
                b = fw.bank()
                for k in range(8):
                    fw.mm((b, b.t[0:rn, :]), (xT, xT[:, k, r0:r0 + rn]), (wt, wt[:, k, :]), k == 0, k == 7)
                o = o512.next()
                fw.act((o, o[0:rn, :]), (b, b.t[0:rn, :]), AF.Silu)
                fw.dma("sp", (R["zs"], zs_d[r0:r0 + rn, g0:g0 + gn]), (o, o[0:rn, :]))
        convw_sb = fw.sb("convw_sb", [128, 96])
        convb_sb = fw.sb("convb_sb", [128, 24])
        fw.dma("sp", (convw_sb, convw_sb[:]), (R["consts"], convw))
        fw.dma("sp", (convb_sb, convb_sb[:]), (R["consts"], convb))
        diag = fw.sb("diag", [128, 24, 4, 128], BF16)
        for c in range(24):
            for k in range(4):
                fw.ts("dve" if (c + k) % 2 == 0 else "pool", (diag, diag[:, c, k, :]), (identf, identf[:]), (convw_sb, convw_sb[:, c * 4 + k:c * 4 + k + 1]), ALU.mult)
        stc = fw.sb("stc", [12, CD])
        fw.dma("sp", (stc, stc[:]), (R["consts"], st_conv))
        scT = fw.sb("scT", [128, 24, 12], BF16)
        b = fw.bank()
        for c in range(24):
            fw.tr((b, b.t[:, c * 12:(c + 1) * 12]), (stc, stc[0:12, c * 128:(c + 1) * 128]), (identf, identf[0:12, 0:12]))
        fw.cp("dve", (scT, scT[:].rearrange("p a b -> p (a b)")), (b, b.t[:, 0:288]))
        xpre_ring = Ring(fw, "xpre", 3, [128, U], BF16)
        pend_conv = [None]
        cn_ring = Ring(fw, "cn", 2, [47, 512])
        for gi, (g0, gn) in enumerate(tiles_of(OFF_X, OFF_DT, 512)):
            wt = load_w(W["w_in"], g0, gn)
            b = fw.bank()
            for k in range(8):
                fw.mm((b, b.t[0:47, :]), (xT, xT[:, k, U - 47:U]), (wt, wt[:, k, :]), k == 0, k == 7)
            cn = cn_ring.next()
            fw.cp("dve", (cn, cn[:]), (b, b.t[0:47, :]))
            fw.dma("sp", (R["conv_o"], conv_o[:, gi * 512:(gi + 1) * 512]), (cn, cn[:]), final=True)
            for jj in range(4):
                c = gi * 4 + jj
                xp = xpre_ring.next()
                for ti, (t0, tn) in enumerate(tiles_of(0, U, 512)):
                    b = fw.bank()
                    for k in range(8):
                        fw.mm((b, b.t[:, 0:tn]), (wt, wt[:, k, jj * 128:(jj + 1) * 128]), (xT, xT[:, k, t0:t0 + tn]), k == 0, k == 7)
                    fw.cp("dve" if ti % 2 == 0 else "act", (xp, xp[:, t0:t0 + tn]), (b, b.t[:, 0:tn]))
                for j in range(NSAMP):
                    fw.cp("pool", (xp, xp[:, SB[j]:SB[j] + 3]), (scT, scT[:, c, 3 * j:3 * j + 3]))

                def conv(c=c, xp=xp):
                    for (t0, tn) in tiles_of(3, U, 512):
                        b = fw.bank()
                        for k in range(4):
                            fw.mm((b, b.t[:, 0:tn]), (diag, diag[:, c, k, :]), (xp, xp[:, t0 - 3 + k:t0 - 3 + k + tn]), k == 0, k == 3)
                        o = o512.next()
                        fw.act((o, o[:, 0:tn]), (b, b.t[:, 0:tn]), AF.Silu, bias=(convb_sb, convb_sb[:, c:c + 1]))
                        fw.dma("sp", (R["xc"], xc_d[c, :, t0:t0 + tn]), (o, o[:, 0:tn]))
                if pend_conv[0] is not None:
                    pend_conv[0]()
                pend_conv[0] = conv
        pend_conv[0]()
        wq = load_w(W["w_in"], OFF_QA, QL)
        wk = load_w(W["w_in"], OFF_KV, KL + ROPE)
        fw.dma("pool", (wk, wk[:, :, KL + ROPE:KL + 2 * ROPE]), (R["w"], W["w_kpe_sw"].rearrange("(k p) n -> p k n", p=128)))
        wdt = load_w(W["w_in"], OFF_DT, NH)
        gq = bcast_row("gq_bc", GOFF["qa"], QL)
        gkv = bcast_row("gkv_bc", GOFF["kv"], KL)
        cs_ring = Ring(fw, "csr", 2, [128, 2, 32])
        sm_ring = Ring(fw, "smr", 3, [128, 512])
        kpad = fw.sb("kpad", [128, 96], BF16)
        fw.memset("dve", (kpad, kpad[:]), 0.0)
        def small_mm(r0, rn):
            bq = fw.bank()
            bk2 = fw.bank()
            bd = fw.bank()
            for k in range(8):
                fw.mm((bq, bq.t[0:rn, :]), (xT, xT[:, k, r0:r0 + rn]), (wq, wq[:, k, 0:QL]), k == 0, k == 7)
            for k in range(8):
                fw.mm((bk2, bk2.t[0:rn, 0:320]), (xT, xT[:, k, r0:r0 + rn]), (wk, wk[:, k, 0:320]), k == 0, k == 7)
            for k in range(8):
                fw.mm((bd, bd.t[0:rn, 0:NH]), (xT, xT[:, k, r0:r0 + rn]), (wdt, wdt[:, k, 0:NH]), k == 0, k == 7)
            return bq, bk2, bd

        nxt = small_mm(*rowtiles[0])
        for ri, (r0, rn) in enumerate(rowtiles):
            bq, bk2, bd = nxt
            if ri + 1 < len(rowtiles):
                nxt = small_mm(*rowtiles[ri + 1])
            sm = sm_ring.next()
            fw.cp("dve", (sm, sm[0:rn, 0:NH]), (bd, bd.t[0:rn, 0:NH]))
            fw.dma("sp", (R["dtr"], dtr_d[r0:r0 + rn, :]), (sm, sm[0:rn, 0:NH]))
            st = st_ring.next()
            jk = NR.junk.next()
            fw.act((jk, jk[0:rn, 0:QL]), (bq, bq.t[0:rn, :]), AF.Square, accum=(st, st[0:rn, 0:1]))
            rstd_from(st, st[0:rn, 0:1], QL, rn, st, st[0:rn, 1:2])
            fw.act((jk, jk[0:rn, 512:512 + KL]), (bk2, bk2.t[0:rn, 0:KL]), AF.Square, accum=(st, st[0:rn, 2:3]))
            rstd_from(st, st[0:rn, 2:3], KL, rn, st, st[0:rn, 3:4])
            xn = NR.xn.next()
            fw.stt((xn, xn[0:rn, 0:QL]), (bq, bq.t[0:rn, :]), (st, st[0:rn, 1:2]), (gq, gq[0:rn, :]), ALU.mult, ALU.mult)
            ck = sm_ring.next()
            fw.stt((ck, ck[0:rn, 0:KL]), (bk2, bk2.t[0:rn, 0:KL]), (st, st[0:rn, 3:4]), (gkv, gkv[0:rn, :]), ALU.mult, ALU.mult)
            fw.dma("sp", (R["kvlat"], kvlat[r0:r0 + rn, :]), (ck, ck[0:rn, 0:KL]), final=True)
            fw.cp("pool", (xn, xn[0:rn, 512:512 + KL]), (ck, ck[0:rn, 0:KL]))
            cs = cs_ring.next()
            fw.dma("sp", (cs, cs[0:rn]), (R["consts"], csU[r0:r0 + rn]))
            kr = sm_ring.next()
            fw.tt("dve", (kr, kr[0:rn, 0:32]), (bk2, bk2.t[0:rn, KL:KL + 32]), (cs, cs[0:rn, 0, :]), ALU.mult)
            fw.tt("dve", (kr, kr[0:rn, 32:64]), (bk2, bk2.t[0:rn, KL + 32:KL + 64]), (cs, cs[0:rn, 1, :]), ALU.mult)
            fw.tt("dve", (kr, kr[0:rn, 0:32]), (kr, kr[0:rn, 0:32]), (kr, kr[0:rn, 32:64]), ALU.add)
            fw.dma("sp", (R["krope"], krope[r0:r0 + rn, :]), (kr, kr[0:rn, 0:32]), final=True)
            fw.cp("dve", (kpad, kpad[0:rn, 64:96]), (kr, kr[0:rn, 0:32]))
            bt = fw.bank()
            bv = bt.t[:].bitcast(BF16)
            for kk in range(4):
                fw.tr((bt, bv[:, kk * 128:kk * 128 + rn]), (xn, xn[0:rn, kk * 128:(kk + 1) * 128]), (identb, identb[0:rn, 0:rn]))
            for kk in range(2):
                fw.tr((bt, bv[:, (4 + kk) * 128:(4 + kk) * 128 + rn]), (xn, xn[0:rn, 512 + kk * 128:512 + (kk + 1) * 128]), (identb, identb[0:rn, 0:rn]))
            fw.tr((bt, bv[0:96, 6 * 128:6 * 128 + rn]), (kpad, kpad[0:rn, :]), (identb, identb[0:rn, 0:rn]))
            for kk in range(4):
                fw.cp("dve" if kk % 2 == 0 else "act", (qaT, qaT[:, kk, r0:r0 + rn]), (bt, bv[:, kk * 128:kk * 128 + rn]))
            for kk in range(2):
                fw.cp("act" if kk % 2 == 0 else "dve", (ckvT, ckvT[:, kk, r0:r0 + rn]), (bt, bv[:, (4 + kk) * 128:(4 + kk) * 128 + rn]))
            fw.cp("dve", (kpeT, kpeT[64:96, r0:r0 + rn]), (bt, bv[64:96, 6 * 128:6 * 128 + rn]))
        for gi, (g0, gn) in enumerate(tiles_of(OFF_GA, NIN, 512)):
            wt = load_w(W["w_in"], g0, gn)
            for jj in range(4):
                for (t0, tn) in tiles_of(0, U, 512):
                    b = fw.bank()
                    for k in range(8):
                        fw.mm((b, b.t[:, 0:tn]), (wt, wt[:, k, jj * 128:(jj + 1) * 128]), (xT, xT[:, k, t0:t0 + tn]), k == 0, k == 7)
                    o = o512.next()
                    fw.act((o, o[:, 0:tn]), (b, b.t[:, 0:tn]), AF.Sigmoid)
                    fw.dma("sp", (R["gT"], gT_d[gi * 4 + jj, :, t0:t0 + tn]), (o, o[:, 0:tn]))
        fw.pop_scope()


    def ssd():
        fw.push_scope()
        sel3 = fw.sb("sel3_sb", [96, NH, 128], BF16)
        fw.dma("pool", (sel3, sel3[:].rearrange("p a b -> p (a b)")), (R["consts"], sel3_d))
        s3_ring = Ring(fw, "s3r", 2, [96, 128], BF16)
        r1_ring = Ring(fw, "r1r", 2, [96, 128])
        mt_ring = Ring(fw, "mtr", 2, [96, 128], BF16)
        da3_ring = Ring(fw, "da3r", 2, [128, 3, NH])
        A_bc = bcast_row("A_bc", GOFF["alog"], NH)
        fw.act((A_bc, A_bc[:]), (A_bc, A_bc[:]), AF.Exp)
        fw.ts("dve", (A_bc, A_bc[:]), (A_bc, A_bc[:]), -1.0, ALU.mult)
        dtb_bc = bcast_row("dtb_bc", GOFF["dtb"], NH)
        Drep = bcast_row("Drep", GOFF["dsk"], NH)
        wa = fw.sb("wa", [128, 16, D], BF16)
        for kk in range(0, 16, 4):
            fw.dma("pool", (wa, wa[:, kk:kk + 4, :]), (R["w"], W["w_a_out"][kk * 128:(kk + 4) * 128, :].rearrange("(k p) n -> p k n", p=128)))
        gncol_sb = fw.sb("gncol_sb", [128, 16])
        fw.dma("sp", (gncol_sb, gncol_sb[:]), (R["consts"], gncol))
        for kk in range(16):
            fw.ts("pool" if kk % 2 else "dve", (wa, wa[:, kk, :]), (wa, wa[:, kk, :]), (gncol_sb, gncol_sb[:, kk:kk + 1]), ALU.mult)
        hT = fw.sb("hT", [128, DI])
        hTb = fw.sb("hTb", [128, DI], BF16)
        xc_ring = Ring(fw, "xcr", 2, [128, 24, 128], BF16)
        zs_ring = Ring(fw, "zsr", 2, [128, DI], BF16)
        sm = Ring(fw, "ssm_sm", 2, [128, 8, NH])
        cd_ring = Ring(fw, "cdr", 2, [128, NH])
        xs_ring = Ring(fw, "xstm", 2, [128, DI], BF16)
        xw_ring = Ring(fw, "xwtm", 2, [128, DI], BF16)
        xsD_ring = Ring(fw, "xsD", 2, [128, DI], BF16)
        B_ring = Ring(fw, "Btm", 2, [128, 512], BF16)
        cb_ring = Ring(fw, "cbT", 2, [128, 4, 128], BF16)
        dexp_ring = Ring(fw, "dexp", 3, [128, 128], BF16)
        MT_ring = Ring(fw, "MTg", 2, [128, 8, 128], BF16)
        y_ring = Ring(fw, "yall", 1, [128, DI])
        t_ring = Ring(fw, "ytmp", 2, [128, 512])
        ynb_ring = Ring(fw, "ynb", 1, [128, DI], BF16)
        ynT_ring = Ring(fw, "ynT", 1, [128, 16, 128], BF16)
        ga_ring = Ring(fw, "gar", 2, [128, 8, 128], BF16)
        mA_ring = Ring(fw, "mAr", 2, [128, 8, 128], BF16)
        so_ring = Ring(fw, "sor", 2, [128, 4, 128])

        STG = int(os.environ.get("SSD_STG", "9"))

        def front(c0, cl):
            xc = xc_ring.next()
            for q6 in range(0, 24, 4):
                fw.dma("sp", (xc, xc[:, q6:q6 + 4, 0:cl]), (R["xc"], xc_d[q6:q6 + 4, :, c0:c0 + cl].rearrange("c p u -> p c u")))
            zs = zs_ring.next()
            fw.dma("sp", (zs, zs[0:cl, :]), (R["zs"], zs_d[c0:c0 + cl, :]))
            s_ = sm.next()
            fw.dma("sp", (s_, s_[0:cl, 0, :]), (R["dtr"], dtr_d[c0:c0 + cl, :]))
            fw.tt("dve", (s_, s_[0:cl, 0, :]), (s_, s_[0:cl, 0, :]), (dtb_bc, dtb_bc[0:cl, :]), ALU.add)
            fw.act((s_, s_[0:cl, 0, :]), (s_, s_[0:cl, 0, :]), AF.Exp)
            fw.act((s_, s_[0:cl, 0, :]), (s_, s_[0:cl, 0, :]), AF.Ln, bias=1.0)
            da3 = da3_ring.next()
            fw.tt("dve", (da3, da3[0:cl]), (s_, s_[0:cl, 0:1, :].to_broadcast([cl, 3, NH])), (A_bc, A_bc[0:cl, :].unsqueeze(1).to_broadcast([cl, 3, NH])), ALU.mult)
            bs = fw.bank()
            da = (da3, da3[0:cl, 0, :])
            fw.mm((bs, bs.t[0:cl, 0:32]), (tri, tri[0:cl, 0:cl]), da, True, True)
            fw.mm((bs, bs.t[0:cl, 32:64]), (tris, tris[0:cl, 0:cl]), da, True, True)
            fw.mm((bs, bs.t[:, 64:96]), (onesf, onesf[0:cl, :]), da, True, True)
            fw.mm((bs, bs.t[0:96, 128:128 + cl]), (da3, da3[0:cl].rearrange("p a b -> p (a b)")), (tri, tri[0:cl, 0:cl]), True, True)
            fw.ts("dve", (s_, s_[0:cl, 2, :]), (bs, bs.t[0:cl, 0:32]), -1.0, ALU.mult)
            fw.act((s_, s_[0:cl, 3, :]), (bs, bs.t[0:cl, 0:32]), AF.Exp)
            fw.act((s_, s_[0:cl, 4, :]), (bs, bs.t[0:cl, 32:64]), AF.Exp)
            fw.tt("dve", (s_, s_[0:cl, 4, :]), (s_, s_[0:cl, 4, :]), (s_, s_[0:cl, 0, :]), ALU.mult)
            cd = cd_ring.next()
            fw.act((cd, cd[:]), (bs, bs.t[:, 64:96]), AF.Exp)
            acsT = s3_ring.next()
            r1 = r1_ring.next()
            mtp = mt_ring.next()
            fw.cp("dve", (acsT, acsT[0:96, 0:cl]), (bs, bs.t[0:96, 128:128 + cl]))
            fw.tt("dve", (r1, r1[32:64, 0:cl]), (bs, bs.t[32:64, 128:128 + cl]), (acsT, acsT[32:64, 0:cl]), ALU.subtract)
            fw.tt("dve", (r1, r1[64:96, 0:cl]), (bs, bs.t[64:96, 128:128 + cl]), (acsT, acsT[64:96, 0:cl]), ALU.subtract)
            fw.cp("dve", (acsT, acsT[32:64, 0:cl]), (r1, r1[32:64, 0:cl]))
            fw.cp("dve", (mtp, mtp[64:96, 0:cl]), (r1, r1[64:96, 0:cl]))
            fw.tt("dve", (r1, r1[64:96, 0:cl]), (r1, r1[64:96, 0:cl]), (mtp, mtp[64:96, 0:cl]), ALU.subtract)
            fw.cp("dve", (acsT, acsT[64:96, 0:cl]), (r1, r1[64:96, 0:cl]))
            xs = xs_ring.next()
            for half in range(2):
                bt = fw.bank()
                bv = bt.t[:].bitcast(BF16)
                for kk in range(8):
                    fw.tr((bt, bv[0:cl, kk * 128:(kk + 1) * 128]), (xc, xc[:, half * 8 + kk, 0:cl]), (identb, identb[:]))
                fw.cp("dve" if half == 0 else "act", (xs, xs[0:cl, half * 1024:(half + 1) * 1024]), (bt, bv[0:cl, :]))
            Bt = B_ring.next()
            bt = fw.bank()
            bv = bt.t[:].bitcast(BF16)
            for g in range(4):
                fw.tr((bt, bv[0:cl, g * 128:(g + 1) * 128]), (xc, xc[:, 16 + g, 0:cl]), (identb, identb[:]))
            fw.cp("dve", (Bt, Bt[0:cl, :]), (bt, bv[0:cl, 0:512]))
            xw = xw_ring.next()
            fw.tt("pool", (xw, xw[0:cl, :].rearrange("p (h d) -> p h d", h=NH)), (xs, xs[0:cl, :].rearrange("p (h d) -> p h d", h=NH)),
                  (s_, s_[0:cl, 4, :].unsqueeze(2).to_broadcast([cl, NH, HD])), ALU.mult)
            xsD = xsD_ring.next()
            fw.tt("pool", (xsD, xsD[0:cl, :].rearrange("p (h d) -> p h d", h=NH)), (xs, xs[0:cl, :].rearrange("p (h d) -> p h d", h=NH)),
                  (Drep, Drep[0:cl, :].unsqueeze(2).to_broadcast([cl, NH, HD])), ALU.mult)
            bc = fw.bank()
            for g in range(4):
                fw.mm((bc, bc.t[0:cl, g * 128:g * 128 + cl]), (xc, xc[:, 16 + g, 0:cl]), (xc, xc[:, 20 + g, 0:cl]), True, True)
            cbT = cb_ring.next()
            fw.cp("act", (cbT, cbT[0:cl].rearrange("p a b -> p (a b)")), (bc, bc.t[0:cl, :]))
            C = _H()
            C.c0, C.cl, C.xc, C.zs, C.s_, C.cd, C.acsT, C.xs, C.Bt, C.xw, C.cbT = c0, cl, xc, zs, s_, cd, acsT, xs, Bt, xw, cbT
            C.xsD = xsD
            C.MT = {}
            return C

        def stageA(C, g):
            c0, cl, xc, zs, s_, cd, acsT, xs, Bt, xw, cbT = C.c0, C.cl, C.xc, C.zs, C.s_, C.cd, C.acsT, C.xs, C.Bt, C.xw, C.cbT
            MT = MT_ring.next()
            C.MT[g] = MT
            for half in range(2):
                bm = fw.bank()
                for hh in range(4):
                    h = g * 8 + half * 4 + hh
                    fw.mm((bm, bm.t[0:cl, hh * 128:hh * 128 + cl]), (sel3, sel3[:, h, 0:cl]), (acsT, acsT[0:96, 0:cl]), True, False)
                    fw.mm((bm, bm.t[0:cl, hh * 128:hh * 128 + cl]), (identb, identb[0:cl, 0:cl]), (cbf_sb, negsl[0:cl, 0:cl]), False, True)
                for hh in range(4):
                    h = g * 8 + half * 4 + hh
                    de = dexp_ring.next()
                    fw.act((de, de[0:cl, 0:cl]), (bm, bm.t[0:cl, hh * 128:hh * 128 + cl]), AF.Exp, bias=(s_, s_[0:cl, 2, h:h + 1]))
                    fw.stt((MT, MT[0:cl, half * 4 + hh, 0:cl]), (de, de[0:cl, 0:cl]), (s_, s_[0:cl, 0, h:h + 1]), (cbT, cbT[0:cl, g, 0:cl]), ALU.mult, ALU.mult)

        def stageB(C, g):
            c0, cl, xc, zs, s_, cd, acsT, xs, Bt, xw, cbT = C.c0, C.cl, C.xc, C.zs, C.s_, C.cd, C.acsT, C.xs, C.Bt, C.xw, C.cbT
            MT = C.MT[g]
            if g == 0:
                C.yall = y_ring.next()
            yall = C.yall
            by = fw.bank()
            xsD = C.xsD
            fw.mm((by, by.t[0:cl, :]), (identb, identb[0:cl, 0:cl]), (xsD, xsD[0:cl, g * 512:(g + 1) * 512]), True, False)
            for hh in range(8):
                h = g * 8 + hh
                fw.mm((by, by.t[0:cl, hh * 64:(hh + 1) * 64]), (MT, MT[0:cl, hh, 0:cl]), (xs, xs[0:cl, h * 64:(h + 1) * 64]), False, hh == 7)
            bo = fw.bank()
            fw.mm((bo, bo.t[0:cl, :]), (xc, xc[:, 20 + g, 0:cl]), (hTb, hTb[:, g * 512:(g + 1) * 512]), True, True)
            yg = (yall, yall[0:cl, g * 512:(g + 1) * 512])
            fw.tt("dve", (yall, yall[0:cl, g * 512:(g + 1) * 512].rearrange("p (h d) -> p h d", h=8)), (bo, bo.t[0:cl, :].rearrange("p (h d) -> p h d", h=8)),
                  (s_, s_[0:cl, 3, g * 8:(g + 1) * 8].unsqueeze(2).to_broadcast([cl, 8, HD])), ALU.mult)
            fw.tt("dve", yg, yg, (by, by.t[0:cl, :]), ALU.add)
            tmp = t_ring.next()
            fw.tt("pool", yg, yg, (zs, zs[0:cl, g * 512:(g + 1) * 512]), ALU.mult)
            fw.act((tmp, tmp[0:cl, :]), yg, AF.Square, accum=(s_, s_[0:cl, 5, g:g + 1]))
            bst = fw.bank()
            fw.mm((bst, bst.t[:, :]), (Bt, Bt[0:cl, g * 128:(g + 1) * 128]), (xw, xw[0:cl, g * 512:(g + 1) * 512]), True, True)
            hg = (hT, hT[:, g * 512:(g + 1) * 512])
            fw.tt("pool", (hT, hT[:, g * 512:(g + 1) * 512].rearrange("p (h d) -> p h d", h=8)), (hT, hT[:, g * 512:(g + 1) * 512].rearrange("p (h d) -> p h d", h=8)),
                  (cd, cd[:, g * 8:(g + 1) * 8].unsqueeze(2).to_broadcast([128, 8, HD])), ALU.mult)
            fw.tt("dve", hg, hg, (bst, bst.t[:, :]), ALU.add)
            fw.cp("act", (hTb, hTb[:, g * 512:(g + 1) * 512]), hg)

        def tail(C):
            c0, cl, xc, zs, s_, cd, acsT, xs, Bt, xw, cbT = C.c0, C.cl, C.xc, C.zs, C.s_, C.cd, C.acsT, C.xs, C.Bt, C.xw, C.cbT
            yall = C.yall
            fw.act((s_, s_[0:cl, 5, 4:8]), (s_, s_[0:cl, 5, 0:4]), AF.Sqrt, bias=(eps_sb, eps_sb[0:cl, :]), scale=1.0 / 512)
            fw.recip((s_, s_[0:cl, 5, 4:8]), (s_, s_[0:cl, 5, 4:8]))
            ynb = ynb_ring.next()
            for g in range(4):
                fw.act((ynb, ynb[0:cl, g * 512:(g + 1) * 512]), (yall, yall[0:cl, g * 512:(g + 1) * 512]), AF.Copy, scale=(s_, s_[0:cl, 5, 4 + g:5 + g]))
            ynT = ynT_ring.next()
            for half in range(2):
                bt = fw.bank()
                bv = bt.t[:].bitcast(BF16)
                for kk in range(8):
                    fw.tr((bt, bv[:, kk * 128:kk * 128 + cl]), (ynb, ynb[0:cl, (half * 8 + kk) * 128:(half * 8 + kk + 1) * 128]), (identb, identb[0:cl, 0:cl]))
                fw.cp("dve" if half == 0 else "act", (ynT, ynT[:, half * 8:(half + 1) * 8, 0:cl]), (bt, bv[:, :].rearrange("p (a b) -> p a b", a=8)[:, :, 0:cl]))
            ga = ga_ring.next()
            fw.dma("sp", (ga, ga[:, :, 0:cl]), (R["gT"], gT_d[0:8, :, c0:c0 + cl].rearrange("c p u -> p c u")))
            mA = mA_ring.next()
            for half in range(2):
                bw = fw.bank()
                for oo in range(4):
                    oc = half * 4 + oo
                    for k in range(16):
                        fw.mm((bw, bw.t[:, oo * 128:oo * 128 + cl]), (wa, wa[:, k, oc * 128:(oc + 1) * 128]), (ynT, ynT[:, k, 0:cl]), k == 0, k == 15)
                fw.tt("dve", (mA, mA[:, half * 4:(half + 1) * 4, 0:cl]), (bw, bw.t[:, :].rearrange("p (a b) -> p a b", a=4)[:, :, 0:cl]), (ga, ga[:, half * 4:(half + 1) * 4, 0:cl]), ALU.mult)
            fw.dma("sp", (R["mT"], mT_d[:, :, c0:c0 + cl].rearrange("c p u -> p c u")), (mA, mA[:, :, 0:cl]))


        def run_seq(chunks):
            Cs = {}
            seq = [(ci, g) for ci in range(len(chunks)) for g in range(4)]
            Cs[0] = front(*chunks[0])
            stageA(Cs[0], 0)
            for n, (ci, g) in enumerate(seq):
                if n + 1 < len(seq):
                    ci2, g2 = seq[n + 1]
                    if g2 == 0:
                        Cs[ci2] = front(*chunks[ci2])
                    stageA(Cs[ci2], g2)
                stageB(Cs[ci], g)
                if g == 3:
                    tail(Cs[ci])
                    del Cs[ci]

        def store_state(idx):
            if STG < 1 and STG != -1:
                return
            for q4 in range(4):
                bt = fw.bank()
                for kk in range(4):
                    k = q4 * 4 + kk
                    fw.tr((bt, bt.t[:, kk * 128:(kk + 1) * 128]), (hT, hT[:, k * 128:(k + 1) * 128]), (identf, identf[:]))
                so = so_ring.next()
                fw.cp("dve", (so, so[:].rearrange("p a b -> p (a b)")), (bt, bt.t[:, :]))
                fw.dma("sp", (R["ssm_o"], ssm_o[idx, q4 * 512:(q4 + 1) * 512, :].rearrange("(k p) n -> p k n", p=128)), (so, so[:]), final=True)

        fw.memset("dve", (hT, hT[:]), 0.0)
        fw.memset("pool", (hTb, hTb[:]), 0.0)
        run_seq([(P0, NMETA)] + [(P0 + NMETA + 128 * i, 128) for i in range(SEQ // 128)])
        store_state(0)
        for j in range(NSAMP if STG != -2 else 0):
            for q4 in range(4):
                si = so_ring.next()
                fw.dma("sp", (si, si[:]), (R["consts"], st_ssm[j, q4 * 512:(q4 + 1) * 512, :].rearrange("(k p) n -> p k n", p=128)))
                if STG == -3:
                    continue
                bt = fw.bank()
                for kk in range(4):
                    fw.tr((bt, bt.t[:, kk * 128:(kk + 1) * 128]), (si, si[:, kk, :]), (identf, identf[:]))
                if STG == -4:
                    continue
                fw.cp("dve", (hT, hT[:, q4 * 512:(q4 + 1) * 512]), (bt, bt.t[:, :]))
                if STG == -5:
                    continue
                fw.cp("act", (hTb, hTb[:, q4 * 512:(q4 + 1) * 512]), (bt, bt.t[:, :]))
            run_seq([(SB[j] + 3, TS)])
            store_state(1 + j)
        fw.pop_scope()


    def attn():
        fw.push_scope()
        ASTG = int(os.environ.get("ATT_STG", "9"))
        NST = (PL + 127) // 128
        NPG4 = NPG // 4
        qlatT = fw.sb("qlatT", [128, 2, NSAMP, 128], BF16)
        qpeT = fw.sb("qpeT", [96, NSAMP, 128], BF16)
        wuv = fw.sb("wuv", [128, 2, MH * VH], BF16)
        fw.dma("pool", (wuv, wuv[:]), (R["w"], W["w_uv"].rearrange("(k p) n -> p k n", p=128)))
        fw.push_scope()
        wqb = fw.sb("wqb", [128, 4, MH * 96], BF16)
        wqbs = fw.sb("wqbs", [128, 4, MH * 96], BF16)
        fw.dma("pool", (wqb, wqb[:]), (R["w"], W["w_q_b"].rearrange("(k p) n -> p k n", p=128)))
        fw.dma("pool", (wqbs, wqbs[:]), (R["w"], W["w_q_b_sw"].rearrange("(k p) n -> p k n", p=128)))
        wukp = fw.sb("wukp", [128, 2, MH, 96], BF16)
        fw.memset("dve", (wukp, wukp[:]), 0.0)
        for kc in range(2):
            fw.dma("pool", (wukp, wukp[:, kc, :, 0:NOPE]), (R["w"], W["w_uk"][kc * 128:(kc + 1) * 128, :].rearrange("p (h n) -> p h n", h=MH)))
        wukT = fw.sb("wukT", [NOPE, MH, KL], BF16)
        fw.dma("pool", (wukT, wukT[:]), (R["w"], W["w_ukT"].rearrange("n (h c) -> n h c", h=MH)))
        cs96_sb = fw.sb("cs96_sb", [96, 2, U])
        fw.dma("sp", (cs96_sb, cs96_sb[:]), (R["consts"], cs96))
        V_sb = fw.sb("V_sb", [128, NST, MH, VH + 1], BF16)
        fw.memset("pool", (V_sb, V_sb[:]), 1.0)
        QT = fw.sb("QT", [96, 4, U], BF16)
        KT = fw.sb("KT", [96, 4, U], BF16)
        PT_ring = Ring(fw, "PT", 5, [128, 512], BF16)
        tq_ring = Ring(fw, "tq", 2, [96, 2, 512])
        rd_ring = Ring(fw, "rd", 2, [65, 512])
        bc_ring = Ring(fw, "bcr", 2, [64, 512])
        o_ring = Ring(fw, "osb", 2, [64, 512], BF16)
        for st in range(NST):
            s0 = P0 + 128 * st
            sn = min(128, P0 + PL - s0)
            for half in range(2):
                b = fw.bank()
                for kc in range(2):
                    fw.mm((b, b.t[0:sn, :]), (ckvT, ckvT[:, kc, s0:s0 + sn]), (wuv, wuv[:, kc, half * 512:(half + 1) * 512]), kc == 0, kc == 1)
                fw.cp("dve" if half == 0 else "act", (V_sb, V_sb[0:sn, st, half * 8:(half + 1) * 8, 0:VH]), (b, b.t[0:sn, :].rearrange("p (h v) -> p h v", h=8)))
        accs = fw.reserve(2)
        acc_i = 0
        pend_fin = [None]
        qtiles = tiles_of(0, PL, 512)
        for hg in range(MH // 4):
            for hh in range(4):
                h = hg * 4 + hh
                for (t0, tn) in tiles_of(0, U, 512):
                    bA = fw.bank()
                    bB = fw.bank()
                    for k in range(4):
                        fw.mm((bA, bA.t[0:96, 0:tn]), (wqb, wqb[:, k, h * 96:(h + 1) * 96]), (qaT, qaT[:, k, t0:t0 + tn]), k == 0, k == 3)
                    for k in range(4):
                        fw.mm((bB, bB.t[0:96, 0:tn]), (wqbs, wqbs[:, k, h * 96:(h + 1) * 96]), (qaT, qaT[:, k, t0:t0 + tn]), k == 0, k == 3)
                    tq = tq_ring.next()
                    fw.tt("dve", (tq, tq[:, 0, 0:tn]), (bA, bA.t[0:96, 0:tn]), (cs96_sb, cs96_sb[:, 0, t0:t0 + tn]), ALU.mult)
                    fw.tt("dve", (tq, tq[:, 1, 0:tn]), (bB, bB.t[0:96, 0:tn]), (cs96_sb, cs96_sb[:, 1, t0:t0 + tn]), ALU.mult)
                    fw.tt("pool", (QT, QT[:, hh, t0:t0 + tn]), (tq, tq[:, 0, 0:tn]), (tq, tq[:, 1, 0:tn]), ALU.add)
                    bK = fw.bank()
                    for kc in range(2):
                        fw.mm((bK, bK.t[0:96, 0:tn]), (wukp, wukp[:, kc, h, :]), (ckvT, ckvT[:, kc, t0:t0 + tn]), kc == 0, False)
                    fw.mm((bK, bK.t[0:96, 0:tn]), (identb, identb[64:96, 0:96]), (kpeT, kpeT[64:96, t0:t0 + tn]), False, True)
                    fw.cp("act", (KT, KT[:, hh, t0:t0 + tn]), (bK, bK.t[0:96, 0:tn]))
            for j in range(NSAMP if ASTG >= 2 else 0):
                cs8 = SB[j] + 3
                bq = fw.bank()
                for kc in range(2):
                    for hh in range(4):
                        h = hg * 4 + hh
                        fw.mm((bq, bq.t[:, kc * 32 + hh * 8:kc * 32 + hh * 8 + 8]), (wukT, wukT[:, h, kc * 128:(kc + 1) * 128]), (QT, QT[0:NOPE, hh, cs8:cs8 + 8]), True, True)
                fw.cp("dve", (qlatT, qlatT[:, :, j, hg * 32:(hg + 1) * 32]), (bq, bq.t[:, 0:64].rearrange("p (a b) -> p a b", a=2)))
                fw.cp("pool", (qpeT, qpeT[64:96, j, hg * 32:(hg + 1) * 32].rearrange("p (a b) -> p a b", a=4)), (QT, QT[64:96, :, cs8:cs8 + 8]))
            for hh in range(4 if ASTG >= 3 else 0):
                h = hg * 4 + hh
                for qi, (qp0, qn) in enumerate(qtiles):
                    q0 = P0 + qp0
                    bO = accs[acc_i]
                    acc_i = (acc_i + 1) % 2
                    sts = [st for st in range(NST) if st * 128 <= qp0 + qn - 1]
                    prevq = []

                    def do_pv(pv, bO=bO, qn=qn, h=h, nlast=len(sts) - 1):
                        PT_, st_, si_, sn_ = pv
                        fw.mm((bO, bO.t[0:VH + 1, 0:qn]), (V_sb, V_sb[0:sn_, st_, h, :]), (PT_, PT_[0:sn_, 0:qn]), si_ == 0, si_ == nlast)

                    for si, st in enumerate(sts):
                        s0 = P0 + 128 * st
                        sn = min(128, P0 + PL - s0)
                        diag = (st * 128 + sn - 1) > qp0
                        bS = fw.bank()
                        fw.mm((bS, bS.t[0:sn, 0:qn]), (KT, KT[:, hh, s0:s0 + sn]), (QT, QT[:, hh, q0:q0 + qn]), True, not diag)
                        if diag:
                            r = st - 4 * qi
                            fw.mm((bS, bS.t[0:sn, 0:qn]), (identb, identb[0:sn, 0:sn]), (cbf_sb, negm(r)[0:sn, 0:qn]), False, True)
                        PT = PT_ring.next()
                        fw.act((PT, PT[0:sn, 0:qn]), (bS, bS.t[0:sn, 0:qn]), AF.Exp)
                        if si == 1 and pend_fin[0] is not None:
                            pend_fin[0]()
                            pend_fin[0] = None
                        prevq.append((PT, st, si, sn))
                        if len(prevq) > 2:
                            do_pv(prevq.pop(0))
                    if pend_fin[0] is not None:
                        pend_fin[0]()
                        pend_fin[0] = None
                    while prevq:
                        do_pv(prevq.pop(0))

                    def fin(bO=bO, qn=qn, h=h, q0=q0):
                        rd = rd_ring.next()
                        fw.recip((rd, rd[64:65, 0:qn]), (bO, bO.t[64:65, 0:qn]))
                        bB = fw.bank()
                        fw.mm((bB, bB.t[0:64, 0:qn]), (onesf, onesf[64:65, 0:64]), (rd, rd[64:65, 0:qn]), True, True)
                        bcs = bc_ring.next()
                        fw.cp("act", (bcs, bcs[:, 0:qn]), (bB, bB.t[0:64, 0:qn]))
                        osb = o_ring.next()
                        fw.tt("dve", (osb, osb[:, 0:qn]), (bO, bO.t[0:64, 0:qn]), (bcs, bcs[:, 0:qn]), ALU.mult)
                        fw.dma("sp", (R["oT"], oT_d[h, :, q0:q0 + qn]), (osb, osb[:, 0:qn]))
                    pend_fin[0] = fin
        if pend_fin[0] is not None:
            pend_fin[0]()
            pend_fin[0] = None
        fw.pop_scope()
        fw.push_scope()
        ptb = fw.sb("ptb", [128, NSAMP * NPG], I32)
        fw.dma("sp", (ptb, ptb[:]), (R["consts"], ptab[0, :].partition_broadcast(128)))
        pm = fw.sb("pm", [128, 1])
        fw.dma("sp", (pm, pm[:]), (R["consts"], cf32[:, 512 + NH * 128:512 + NH * 128 + 1]), slow=True)
        idx = fw.sb("idx", [128, NSAMP * NPG4], I32)
        for qd in range(4):
            fw.ts("dve", (idx, idx[32 * qd:32 * qd + 32, :]), (ptb, ptb[32 * qd:32 * qd + 32, :].rearrange("p (i q) -> p i q", q=4)[:, :, qd]),
                  32.0, ALU.mult, s2=(pm, pm[32 * qd:32 * qd + 32, :]), op1=ALU.add)
        kvp_ring = Ring(fw, "kvp", 4, [128, 4, KL], BF16)
        krg_ring = Ring(fw, "krg", 3, [128, 4, ROPE])
        kv32_ring = Ring(fw, "kv32", 2, [128, 4, KL])
        krp_ring = Ring(fw, "krp", 4, [128, 4, 96], BF16)
        ones_b = fw.sb("ones_b", [128, 8], BF16)
        fw.memset("dve", (ones_b, ones_b[:]), 1.0)
        for t_ in krp_ring.tiles:
            fw.memset("dve", (t_, t_[:]), 0.0)
        KTp_ring = Ring(fw, "KTp", 4, [128, 3, 128], BF16)
        PTs_ring = Ring(fw, "PTs", 4, [128, 128], BF16)
        ckn_ring = Ring(fw, "ckn", 2, [8, KL + 4], BF16)
        for t_ in ckn_ring.tiles:
            fw.memset("dve", (t_, t_[:]), 1.0)
        ol_ring = Ring(fw, "olat", 2, [128, KL], BF16)
        olT_ring = Ring(fw, "olatT", 2, [128, 2, 128], BF16)
        oS_ring = Ring(fw, "oS", 2, [64, MH, 8], BF16)
        sst = Ring(fw, "sst", 2, [128, 2])
        accs = fw.reserve(2)
        ckr = cache_kr.rearrange("r (a b) -> r a b", a=4)
        SUB = int(os.environ.get("SUB", "9"))
        for j in range(NSAMP if ASTG >= 4 else 0):
            cs8 = SB[j] + 3
            bO = accs[j % 2]
            grp = {}

            def do_gather(i, j=j, grp=grp):
                kvp = kvp_ring.next()
                krp = krp_ring.next()
                krg = krg_ring.next()
                kv32 = kv32_ring.next()
                ic = j * NPG4 + i
                fw.gather((kv32, kv32[:].rearrange("p a b -> p (a b)")), cache_kv, R["cache"], (idx, idx[:, ic:ic + 1]))
                fw.gather((krg, krg[:].rearrange("p a b -> p (a b)")), cache_kr, R["cache"], (idx, idx[:, ic:ic + 1]))
                fw.cp("act", (kvp, kvp[:]), (kv32, kv32[:]))
                fw.cp("pool", (krp, krp[:, :, 64:96]), (krg, krg[:]))
                grp[i] = (kvp, krp)

            items = [(i, r) for i in range(NPG4) for r in range(4)]
            st1 = {}
            st2 = {}

            def stage1(k, grp=grp, st1=st1):
                i, r = items[k]
                if r == 0:
                    if i == 0:
                        do_gather(0)
                    if i + 1 < NPG4:
                        do_gather(i + 1)
                kvp, krp = grp[i]
                bt = fw.bank()
                bv = bt.t[:].bitcast(BF16)
                fw.tr((bt, bv[:, 0:128]), (kvp, kvp[:, r, 0:128]), (identb, identb[:]))
                fw.tr((bt, bv[:, 128:256]), (kvp, kvp[:, r, 128:256]), (identb, identb[:]))
                fw.tr((bt, bv[0:96, 256:384]), (krp, krp[:, r, :]), (identb, identb[:]))
                KTp = KTp_ring.next()
                fw.cp("dve", (KTp, KTp[:, 0:2, :]), (bt, bv[:, 0:256].rearrange("p (a b) -> p a b", a=2)))
                fw.cp("dve", (KTp, KTp[0:96, 2, :]), (bt, bv[0:96, 256:384]))
                st1[k] = KTp

            def stage2(k, j=j, st1=st1, st2=st2):
                KTp = st1.pop(k)
                bS = fw.bank()
                fw.mm((bS, bS.t[:, 0:128]), (KTp, KTp[:, 0, :]), (qlatT, qlatT[:, 0, j, :]), True, False)
                fw.mm((bS, bS.t[:, 0:128]), (KTp, KTp[:, 1, :]), (qlatT, qlatT[:, 1, j, :]), False, False)
                fw.mm((bS, bS.t[:, 0:128]), (KTp, KTp[64:96, 2, :]), (qpeT, qpeT[64:96, j, :]), False, True)
                PTs = PTs_ring.next()
                fw.act((PTs, PTs[:]), (bS, bS.t[:, 0:128]), AF.Exp)
                st2[k] = PTs

            def stage3(k, bO=bO, grp=grp, st2=st2):
                i, r = items[k]
                PTs = st2.pop(k)
                kvp, krp = grp[i]
                fw.mm((bO, bO.t[:, KL:KL + 1]), (PTs, PTs[:]), (ones_b, ones_b[:, 0:1]), k == 0, False)
                fw.mm((bO, bO.t[:, 0:KL]), (PTs, PTs[:]), (kvp, kvp[:, r, :]), False, False)

            n_it = len(items) if SUB >= 2 else 0
            for k in range(n_it + 2):
                if k < n_it:
                    stage1(k)
                if 0 <= k - 1 < n_it:
                    stage2(k - 1)
                if 0 <= k - 2 < n_it:
                    stage3(k - 2)
            if SUB < 3:
                continue
            bS = fw.bank()
            fw.mm((bS, bS.t[0:8, 0:128]), (ckvT, ckvT[:, 0, cs8:cs8 + 8]), (qlatT, qlatT[:, 0, j, :]), True, False)
            fw.mm((bS, bS.t[0:8, 0:128]), (ckvT, ckvT[:, 1, cs8:cs8 + 8]), (qlatT, qlatT[:, 1, j, :]), False, False)
            fw.mm((bS, bS.t[0:8, 0:128]), (identb, identb[:, 0:8]), (cbf_sb, cbf_sb[:, 128 + 2048:128 + 2048 + 128]), False, False)
            fw.mm((bS, bS.t[0:8, 0:128]), (kpeT, kpeT[64:96, cs8:cs8 + 8]), (qpeT, qpeT[64:96, j, :]), False, True)
            PTs = PTs_ring.next()
            fw.act((PTs, PTs[0:8, :]), (bS, bS.t[0:8, 0:128]), AF.Exp)
            if SUB < 4:
                continue
            ckn = ckn_ring.next()
            fw.dma("pool", (ckn, ckn[:, 0:KL]), (R["kvlat"], kvlat[cs8:cs8 + 8, :]))
            fw.mm((bO, bO.t[:, 0:KL + 1]), (PTs, PTs[0:8, :]), (ckn, ckn[0:8, 0:KL + 1]), False, True)
            if SUB < 5:
                continue
            ss_ = sst.next()
            fw.recip((ss_, ss_[:, 0:1]), (bO, bO.t[:, KL:KL + 1]))
            ol = ol_ring.next()
            fw.ts("dve", (ol, ol[:]), (bO, bO.t[:, 0:KL]), (ss_, ss_[:, 0:1]), ALU.mult)
            if "dbg_ol" in debug and j == 0:
                dol = nc.dram_tensor("dbg_ol", [128, KL], BF16, kind="ExternalOutput").ap()
                dq = nc.dram_tensor("dbg_q", [128, 2, NSAMP, 128], BF16, kind="ExternalOutput").ap()
                dqp = nc.dram_tensor("dbg_qp", [96, NSAMP, 128], BF16, kind="ExternalOutput").ap()
                dden = nc.dram_tensor("dbg_den", [128, 2], F32, kind="ExternalOutput").ap()
                fw.dma("sp", (R["yout"], dol), (ol, ol[:]), final=True)
                fw.dma("sp", (R["yout"], dq), (qlatT, qlatT[:]), final=True)
                fw.dma("sp", (R["yout"], dqp), (qpeT, qpeT[:]), final=True)
                fw.dma("sp", (R["yout"], dden), (ss_, ss_[:]), final=True)
            bt = fw.bank()
            bv = bt.t[:].bitcast(BF16)
            for kc in range(2):
                fw.tr((bt, bv[:, kc * 128:(kc + 1) * 128]), (ol, ol[:, kc * 128:(kc + 1) * 128]), (identb, identb[:]))
            olT = olT_ring.next()
            fw.cp("dve", (olT, olT[:].rearrange("p a b -> p (a b)")), (bt, bv[:, 0:256]))
            if SUB < 6:
                continue
            b2 = fw.bank()
            for h in range(MH):
                for kc in range(2):
                    fw.mm((b2, b2.t[0:VH, h * 8:(h + 1) * 8]), (wuv, wuv[:, kc, h * VH:(h + 1) * VH]), (olT, olT[:, kc, h * 8:(h + 1) * 8]), kc == 0, kc == 1)
            oS = oS_ring.next()
            fw.cp("dve", (oS, oS[:].rearrange("p a b -> p (a b)")), (b2, b2.t[0:VH, 0:128]))
            fw.dma("sp", (R["oT"], oT_d[:, :, cs8:cs8 + 8].rearrange("h v u -> v h u")), (oS, oS[:]))
        fw.reserve(0)
        fw.pop_scope()
        fw.pop_scope()
        fw.push_scope()
        alloc_norm_rings()
        wbo = fw.sb("wbo", [VH, MH, D], BF16)
        for hq in range(0, MH, 4):
            fw.dma("pool", (wbo, wbo[:, hq:hq + 4, :]), (R["w"], W["w_b_out"][hq * VH:(hq + 4) * VH, :].rearrange("(h v) n -> v h n", v=VH)))
        wo = fw.sb("wo", [128, 8, D], BF16)
        for kk in range(0, 8, 4):
            fw.dma("pool", (wo, wo[:, kk:kk + 4, :]), (R["w"], W["w_o"][kk * 128:(kk + 4) * 128, :].rearrange("(k p) n -> p k n", p=128)))
        gmix = bcast_row("gmix", D, D)
        oT_ring = Ring(fw, "oTr", 2, [VH, MH, 512], BF16)
        gb_ring = Ring(fw, "gbr", 2, [128, 8, 512], BF16)
        mA2_ring = Ring(fw, "mA2", 2, [128, 8, 512], BF16)
        mS_ring = Ring(fw, "mS", 2, [128, 8, 512], BF16)
        tm_ring = Ring(fw, "tmr", 2, [128, 512])
        for (t0, tn) in (tiles_of(0, U, 512) if ASTG >= 5 else []):
            oTt = oT_ring.next()
            for hq in range(0, MH, 8):
                fw.dma("sp", (oTt, oTt[:, hq:hq + 8, 0:tn]), (R["oT"], oT_d[hq:hq + 8, :, t0:t0 + tn].rearrange("h v u -> v h u")))
            gb = gb_ring.next()
            fw.dma("sp", (gb, gb[:, :, 0:tn]), (R["gT"], gT_d[8:16, :, t0:t0 + tn].rearrange("c p u -> p c u")))
            mA2 = mA2_ring.next()
            fw.dma("sp", (mA2, mA2[:, :, 0:tn]), (R["mT"], mT_d[:, :, t0:t0 + tn].rearrange("c p u -> p c u")))
            mS = mS_ring.next()
            for oc in range(8):
                b = fw.bank()
                for h in range(MH):
                    fw.mm((b, b.t[:, 0:tn]), (wbo, wbo[:, h, oc * 128:(oc + 1) * 128]), (oTt, oTt[:, h, 0:tn]), h == 0, h == MH - 1)
                tm = tm_ring.next()
                fw.tt("dve", (tm, tm[:, 0:tn]), (b, b.t[:, 0:tn]), (gb, gb[:, oc, 0:tn]), ALU.mult)
                fw.tt("pool", (mS, mS[:, oc, 0:tn]), (tm, tm[:, 0:tn]), (mA2, mA2[:, oc, 0:tn]), ALU.add)
            for (r0, rn) in tiles_of(t0, t0 + tn, 128):
                bks = [fw.bank(), fw.bank()]
                for half in range(2):
                    for k in range(8):
                        fw.mm((bks[half], bks[half].t[0:rn, :]), (mS, mS[:, k, r0 - t0:r0 - t0 + rn]), (wo, wo[:, k, half * 512:(half + 1) * 512]), k == 0, k == 7)
                postnorm_residual(bks, rn, gmix, res1, R["res1"], res2, R["res2"], r0)
        fw.pop_scope()

    PH = build.phases
    if "ffn1" in PH:
        ffn("ffn1", 0, 0, xin, R["xin"], res1, R["res1"], final=("proj" not in PH))
    if "proj" in PH:
        mix_proj()
    if "ssd" in PH:
        ssd()
    if "attn" in PH:
        attn()
    if "ffn2" in PH:
        src2, reg2 = (res2, R["res2"]) if "attn" in PH else (res1, R["res1"])
        ffn("ffn2", 2, 2 * D, src2, reg2, yout, R["yout"], final=True)
    fw.emit()
    return nc


build.phases = ("ffn1", "proj", "ssd", "attn", "ffn2")


def host_consts(SEQ, PAST):
    U = 3 + NMETA + SEQ + NSAMP * 11
    ident = np.eye(128, dtype=np.float32)
    t = np.arange(128)
    tri = (t[:, None] <= t[None, :]).astype(np.float32)
    tris = (t[:, None] > t[None, :]).astype(np.float32)
    ones = np.ones((128, 128), np.float32)
    sel = np.zeros((128, NH, 128), np.float32)
    for h in range(NH):
        sel[h, h, :] = 1.0
    cf32 = np.concatenate([ident, tri, tris, ones, sel.reshape(128, NH * 128), (t % 32).astype(np.float32)[:, None]], axis=1)
    neg = np.where(t[None, :] < t[:, None], NEGV, 0.0).astype(np.float32)
    q = np.arange(512)
    negm = [np.where(q[None, :] < 128 * r + t[:, None], NEGV, 0.0).astype(np.float32) for r in range(4)]
    neg8 = np.zeros((128, 128), np.float32)
    for s_ in range(8):
        for h in range(MH):
            for tq in range(8):
                if tq < s_:
                    neg8[s_, h * 8 + tq] = NEGV
    cbf = np.concatenate([neg] + negm + [neg8], axis=1)
    pos = np.zeros(U, np.float32)
    pos[3:3 + NMETA + SEQ] = np.arange(NMETA + SEQ)
    for j in range(NSAMP):
        b0 = 3 + NMETA + SEQ + 11 * j
        pos[b0 + 3:b0 + 11] = PAST + np.arange(8)
    inv = (np.float32(10000.0) ** (-np.arange(0, ROPE, 2, dtype=np.float32) / np.float32(ROPE))).astype(np.float32)
    ang = pos[:, None].astype(np.float32) * inv[None, :]
    cos = np.cos(ang).astype(np.float32)
    sin = np.sin(ang).astype(np.float32)
    csU = np.stack([np.concatenate([cos, cos], 1), np.concatenate([-sin, sin], 1)], axis=1)
    cs96 = np.zeros((96, 2, U), np.float32)
    cs96[0:64, 0, :] = SCALE
    cs96[64:80, 0, :] = cos.T * SCALE
    cs96[80:96, 0, :] = cos.T * SCALE
    cs96[64:80, 1, :] = -sin.T * SCALE
    cs96[80:96, 1, :] = sin.T * SCALE
    sel3 = np.zeros((96, NH, 128), np.float32)
    for h in range(NH):
        for part in range(3):
            sel3[32 * part + h, h, :] = 1.0
    return dict(cf32=cf32, cbf=cbf, csU=np.ascontiguousarray(csU), cs96=cs96, sel3=sel3.reshape(96, NH * 128))


def swap_rope_cols(w, head_w, rope_off):
    w2 = w.copy()
    n = w.shape[1] // head_w
    for h in range(n):
        a = h * head_w + rope_off
        w2[:, a:a + 16] = w[:, a + 16:a + 32]
        w2[:, a + 16:a + 32] = w[:, a:a + 16]
    return w2


def host_shared(inp, SEQ, PAST):
    sh = host_consts(SEQ, PAST)
    for nm in ["ffn1_w_gate", "ffn1_w_up", "ffn1_w_down", "ffn2_w_gate", "ffn2_w_up", "ffn2_w_down", "w_in", "w_q_b",
               "w_a_out", "w_b_out", "w_o"]:
        sh[nm] = np.ascontiguousarray(inp[nm][0])
    w_in = inp["w_in"][0]
    kpe0 = DI + CD + NH + QL + KL
    sh["w_kpe_sw"] = np.ascontiguousarray(np.concatenate([w_in[:, kpe0 + 16:kpe0 + 32], w_in[:, kpe0:kpe0 + 16]], axis=1))
    sh["w_q_b_sw"] = swap_rope_cols(inp["w_q_b"][0], 96, 64)
    w_uk = inp["w_uk"][0]
    sh["w_uk"] = np.ascontiguousarray(w_uk.reshape(KL, MH * NOPE))
    sh["w_ukT"] = np.ascontiguousarray(w_uk.transpose(2, 1, 0).reshape(NOPE, MH * KL))
    sh["w_uv"] = np.ascontiguousarray(inp["w_uv"][0].reshape(KL, MH * VH))
    gc = np.stack([inp[k][0].reshape(8, 128).T for k in ["ffn1_pre_g", "mix_pre_g", "ffn2_pre_g"]], axis=1)
    sh["gcols"] = np.ascontiguousarray(gc.astype(np.float32))
    sh["grows"] = np.concatenate([inp[k][0].reshape(-1) for k in
                                  ["ffn1_post_g", "mix_post_g", "ffn2_post_g", "ssm_norm_g", "q_a_norm_g", "kv_a_norm_g",
                                   "dt_bias", "a_log", "d_skip"]]).astype(np.float32)[None, :]
    cw = inp["conv_w"][0]
    sh["convw"] = np.ascontiguousarray(cw.reshape(4, 24, 128).transpose(2, 1, 0).reshape(128, 96))
    sh["convb"] = np.ascontiguousarray(inp["conv_b"][0].reshape(24, 128).T)
    sh["gncol"] = np.ascontiguousarray(inp["ssm_norm_g"][0].reshape(16, 128).T)
    npool = inp["cache_kv_latent"].shape[1]
    sh["cache_kv"] = inp["cache_kv_latent"][0].reshape(npool * 32, 4 * KL)
    sh["cache_kr"] = inp["cache_k_rope"][0].reshape(npool * 32, 4 * ROPE)
    return sh


def host_core(inp, sh, b, SEQ):
    U = 3 + NMETA + SEQ + NSAMP * 11
    m = dict(sh)
    xin = np.zeros((U, D), np.float32)
    xin[3:3 + NMETA] = inp["meta_tokens"]
    xin[3 + NMETA:3 + NMETA + SEQ] = inp["x_prompt"][b]
    stc = np.zeros((NSAMP * 3, CD), np.float32)
    for j in range(NSAMP):
        b0 = 3 + NMETA + SEQ + 11 * j
        xin[b0 + 3:b0 + 11] = inp["x_sample"][NSAMP * b + j]
        stc[3 * j:3 * j + 3] = inp["state_conv"][0, NSAMP * b + j]
    m["xin"] = xin
    m["st_conv"] = stc
    m["st_ssm"] = np.ascontiguousarray(inp["state_ssm"][0, NSAMP * b:NSAMP * b + NSAMP].reshape(NSAMP, DI, DS))
    m["ptab"] = np.ascontiguousarray(inp["page_table"][NSAMP * b:NSAMP * b + NSAMP].reshape(1, -1)).astype(np.int32)
    return m


SEQ_FULL = 2048
NPG_FULL = 128


def kernel(**inputs):
    inp = {k: np.asarray(v) for k, v in inputs.items()}
    nb = inp["x_prompt"].shape[0]
    seq = inp["x_prompt"].shape[1]
    npg = inp["page_table"].shape[1]
    npool = inp["cache_kv_latent"].shape[1]
    past = npg * inp["cache_kv_latent"].shape[2]
    nsamp_total = inp["x_sample"].shape[0]
    U = 3 + NMETA + seq + NSAMP * 11
    PL = NMETA + seq
    nc = build(seq, npg, npool)
    sh = host_shared(inp, seq, past)
    in_maps = [host_core(inp, sh, b, seq) for b in range(nb)]
    res = run_bass_kernel_spmd(nc, in_maps, core_ids=list(range(nb)))
    y_prompt = np.zeros((nb, seq, D), np.float32)
    y_sample = np.zeros((nsamp_total, TS, D), np.float32)
    kvp = np.zeros((1, nb, PL, KL), np.float32)
    krp = np.zeros((1, nb, PL, ROPE), np.float32)
    ssp = np.zeros((1, nb, NH, HD, DS), np.float32)
    cvp = np.zeros((1, nb, CK - 1, CD), np.float32)
    kvs = np.zeros((1, nsamp_total, TS, KL), np.float32)
    krs = np.zeros((1, nsamp_total, TS, ROPE), np.float32)
    sss = np.zeros((1, nsamp_total, NH, HD, DS), np.float32)
    cvs = np.zeros((1, nsamp_total, CK - 1, CD), np.float32)
    for b in range(nb):
        r = res.results[b]
        yo = np.asarray(r["yout"]); kv = np.asarray(r["kvlat"]); kr = np.asarray(r["krope"])
        so = np.asarray(r["ssm_o"]); co = np.asarray(r["conv_o"])
        y_prompt[b] = yo[3 + NMETA:3 + PL]
        kvp[0, b] = kv[3:3 + PL]
        krp[0, b] = kr[3:3 + PL]
        ssp[0, b] = so[0].reshape(NH, HD, DS)
        cvp[0, b] = co[0:3]
        for j in range(NSAMP):
            sb = 3 + PL + 11 * j
            i = NSAMP * b + j
            y_sample[i] = yo[sb + 3:sb + 11]
            kvs[0, i] = kv[sb + 3:sb + 11]
            krs[0, i] = kr[sb + 3:sb + 11]
            sss[0, i] = so[1 + j].reshape(NH, HD, DS)
            cvs[0, i] = co[11 * j + 11:11 * j + 14]
    return (y_prompt, y_sample, kvp, krp, ssp, cvp, kvs, krs, sss, cvs)
```

```python
import os
import numpy as np
import concourse.bass as bass
import concourse.mybir as mybir
from concourse.bass_utils import run_bass_kernel_spmd

F32 = mybir.dt.float32
BF16 = mybir.dt.bfloat16
I32 = mybir.dt.int32
AF = mybir.ActivationFunctionType
ALU = mybir.AluOpType
AX = mybir.AxisListType

N_DMA_SEMS = 24


class TT:
    __slots__ = ("name", "t", "last_w", "reads", "excl")

    def __init__(self, name, t=None):
        self.name = name
        self.t = t
        self.last_w = None
        self.reads = []
        self.excl = False

    def __getitem__(self, k):
        return self.t[k]


class _Op:
    __slots__ = ("fn", "waits", "tok", "dma", "needed")

    def __init__(self, fn, waits, tok, dma):
        self.fn = fn
        self.waits = waits
        self.tok = tok
        self.dma = dma
        self.needed = False


class FW:
    COMPUTE = ("pe", "act", "dve", "pool")

    def __init__(self, nc):
        self.nc = nc
        self.q = {k: [] for k in ("pe", "act", "dve", "pool", "sp")}
        self.cnt = {k: 0 for k in self.COMPUTE}
        self.dma_rr = {"sp": 0, "pool": 0, "act": 0}
        self.dma_cum = {}
        self.dma_last = {}
        self.same_engine_sync = (os.environ.get("SES", "1") == "1")
        self.raw_only = (os.environ.get("RAWONLY", "1") == "1")
        self.final_tokens = []
        self.optok = {}
        self.banks = None
        self.bank_i = 0
        self.scopes = []
        self.pending = {}
        self.reserved = 0

    def sb(self, name, shape, dt=F32):
        if self.scopes:
            return TT(name, self.scopes[-1].enter_context(self.nc.sbuf_tensor(name, list(shape), dt)))
        return TT(name, self.nc.alloc_sbuf_tensor(name, list(shape), dt))

    def push_scope(self):
        import contextlib
        self.scopes.append(contextlib.ExitStack())

    def pop_scope(self):
        self.barrier()
        self.scopes.pop().close()

    def barrier(self):
        w = {}
        for e in self.COMPUTE:
            if self.cnt[e] > 0:
                w[e] = self.cnt[e]
        for key, tok in self.dma_last.items():
            w[key] = tok[1]
        for e in self.q:
            self.pending[e] = dict(w)

    def region(self, name):
        return TT(name, None)

    def init_banks(self):
        self.banks = [TT("bank%d" % i, self.nc.alloc_psum_tensor("bank%d" % i, [128, 512], F32)) for i in range(8)]
        for b in self.banks:
            b.excl = True

    def bank(self):
        n = 8 - self.reserved
        self.bank_i = self.bank_i % n
        b = self.banks[self.bank_i]
        self.bank_i = (self.bank_i + 1) % n
        return b

    def reserve(self, n):
        self.reserved = n
        return [self.banks[8 - 1 - i] for i in range(n)]

    def op(self, eng, fn, r=(), w=(), dma=False, final=False):
        deps = []
        deps2 = []
        for t in r:
            if t.last_w is not None:
                deps.append(t.last_w)
            if t.excl:
                deps.extend(x for x in t.reads if x[0] != eng)
        for t in w:
            if t.last_w is not None:
                deps2.append(t.last_w)
            deps2.extend(t.reads)
        if dma or not self.raw_only:
            deps.extend(deps2)
        else:
            deps.extend(x for x in deps2 if x[0] != eng)
        if dma:
            i = self.dma_rr[eng]
            self.dma_rr[eng] = (i + 1) % N_DMA_SEMS
            key = ("dma", eng, i)
            prev = self.dma_last.get(key)
            if prev is not None:
                deps.append(prev)
            val = self.dma_cum.get(key, 0) + 16
            self.dma_cum[key] = val
            tok = (key, val)
            self.dma_last[key] = tok
        else:
            self.cnt[eng] += 1
            tok = (eng, self.cnt[eng])
        waits = {}
        if self.pending.get(eng):
            for k, v in self.pending[eng].items():
                if k == eng:
                    continue
                waits[k] = v
            self.pending[eng] = None
        for (k, v) in deps:
            if k == eng and not dma:
                if eng == "pe" or not self.same_engine_sync:
                    continue
            if waits.get(k, 0) < v:
                waits[k] = v
        o = _Op(fn, waits, tok, dma)
        self.q[eng].append(o)
        if not dma:
            self.optok[tok] = o
        for t in r:
            t.reads.append(tok)
        for t in w:
            t.last_w = tok
            t.reads = []
        if final:
            self.final_tokens.append(tok)
        return tok

    def mm(self, out, lhsT, rhs, start, stop):
        self.op("pe", lambda e: e.matmul(out=out[1], lhsT=lhsT[1], rhs=rhs[1], start=start, stop=stop),
                r=[lhsT[0], rhs[0]], w=[out[0]])

    def tr(self, out, in_, ident):
        self.op("pe", lambda e: e.transpose(out=out[1], in_=in_[1], identity=ident[1]), r=[in_[0], ident[0]], w=[out[0]])

    def act(self, out, in_, func, bias=None, scale=None, accum=None, eng="act"):
        r = [in_[0]]
        kw = {}
        if bias is not None:
            if isinstance(bias, tuple):
                r.append(bias[0])
                kw["bias"] = bias[1]
            else:
                kw["bias"] = bias
        if scale is not None:
            if isinstance(scale, tuple):
                r.append(scale[0])
                kw["scale"] = scale[1]
            else:
                kw["scale"] = scale
        w = [out[0]]
        if accum is not None:
            w.append(accum[0])
            kw["accum_out"] = accum[1]
        self.op(eng, lambda e: e.activation(out=out[1], in_=in_[1], func=func, **kw), r=r, w=w)

    def ts(self, eng, out, in0, s1, op0, s2=None, op1=None):
        r = [in0[0]]
        a1 = s1
        if isinstance(s1, tuple):
            r.append(s1[0])
            a1 = s1[1]
        a2 = s2
        if isinstance(s2, tuple):
            r.append(s2[0])
            a2 = s2[1]
        if op1 is None:
            self.op(eng, lambda e: e.tensor_scalar(out=out[1], in0=in0[1], scalar1=a1, scalar2=None, op0=op0), r=r, w=[out[0]])
        else:
            self.op(eng, lambda e: e.tensor_scalar(out=out[1], in0=in0[1], scalar1=a1, scalar2=a2, op0=op0, op1=op1), r=r, w=[out[0]])

    def tt(self, eng, out, in0, in1, op):
        self.op(eng, lambda e: e.tensor_tensor(out=out[1], in0=in0[1], in1=in1[1], op=op), r=[in0[0], in1[0]], w=[out[0]])

    def stt(self, out, in0, scalar, in1, op0, op1):
        r = [in0[0], in1[0]]
        a = scalar
        if isinstance(scalar, tuple):
            r.append(scalar[0])
            a = scalar[1]
        self.op("dve", lambda e: e.scalar_tensor_tensor(out=out[1], in0=in0[1], scalar=a, in1=in1[1], op0=op0, op1=op1), r=r, w=[out[0]])

    def cp(self, eng, out, in_):
        if eng == "act":
            self.op(eng, lambda e: e.copy(out=out[1], in_=in_[1]), r=[in_[0]], w=[out[0]])
        else:
            self.op(eng, lambda e: e.tensor_copy(out=out[1], in_=in_[1]), r=[in_[0]], w=[out[0]])

    def memset(self, eng, out, val):
        self.op(eng, lambda e: e.memset(out[1], val), w=[out[0]])

    def recip(self, out, in_):
        self.op("dve", lambda e: e.reciprocal(out=out[1], in_=in_[1]), r=[in_[0]], w=[out[0]])

    def dma(self, q, out, in_, final=False, slow=False):
        if slow:
            self.op(q, lambda e: e.dma_start(out=out[1], in_=in_[1], allow_slow_non_contiguous=True), r=[in_[0]], w=[out[0]], dma=True, final=final)
        else:
            self.op(q, lambda e: e.dma_start(out=out[1], in_=in_[1]), r=[in_[0]], w=[out[0]], dma=True, final=final)

    def gather(self, out, in_ap, in_reg, idx):
        self.op("pool", lambda e: e.indirect_dma_start(out=out[1], out_offset=None, in_=in_ap,
                                                         in_offset=bass.IndirectOffsetOnAxis(ap=idx[1], axis=0)),
                r=[in_reg, idx[0]], w=[out[0]], dma=True)

    def emit(self):
        nc = self.nc
        for qn, ops in self.q.items():
            for o in ops:
                for (k, v) in o.waits.items():
                    if isinstance(k, str):
                        self.optok[(k, v)].needed = True
        for tok in self.final_tokens:
            if isinstance(tok[0], str):
                self.optok[tok].needed = True
        remap = {}
        for e in self.COMPUTE:
            c = 0
            for o in self.q[e]:
                if o.dma:
                    continue
                if o.needed:
                    c += 1
                remap[o.tok] = c
        sems = {e: nc.alloc_semaphore("s_" + e) for e in self.COMPUTE}
        dsems = {}
        for key in self.dma_cum:
            dsems[key] = nc.alloc_semaphore("d_%s_%d" % (key[1], key[2]))
        finals = list(self.final_tokens)

        def run_queue(qn, eng):
            known = {}
            for o in self.q[qn]:
                for (k, v) in o.waits.items():
                    if isinstance(k, str):
                        v2 = remap[(k, v)]
                        s = sems[k]
                    else:
                        v2 = v
                        s = dsems[k]
                    if known.get(k, 0) >= v2:
                        continue
                    known[k] = v2
                    eng.wait_ge(s, v2)
                ins = o.fn(eng)
                if o.dma:
                    ins.then_inc(dsems[o.tok[0]], 16)
                elif o.needed:
                    ins.then_inc(sems[qn], 1)
            if qn == "sp":
                for tok in finals:
                    k, v = tok
                    if isinstance(k, str):
                        eng.wait_ge(sems[k], remap[tok])
                    else:
                        eng.wait_ge(dsems[k], v)

        with nc.Block() as block:
            @block.sync
            def _(e):
                run_queue("sp", e)

            @block.tensor
            def _(e):
                run_queue("pe", e)

            @block.scalar
            def _(e):
                run_queue("act", e)

            @block.vector
            def _(e):
                run_queue("dve", e)

            @block.gpsimd
            def _(e):
                run_queue("pool", e)


class Ring:
    def __init__(self, fw, name, n, shape, dt=F32):
        self.tiles = [fw.sb("%s%d" % (name, i), shape, dt) for i in range(n)]
        self.i = 0

    def next(self):
        t = self.tiles[self.i]
        self.i = (self.i + 1) % len(self.tiles)
        return t


D = 1024
DFF = 2816
NMETA = 16
DI = 2048
NH = 32
HD = 64
NG = 4
DS = 128
CK = 4
CD = 3072
MH = 16
QL = 512
KL = 256
NOPE = 64
ROPE = 32
VH = 64
EPS = 1e-6
SCALE = (NOPE + ROPE) ** -0.5
NIN = 8000
NSAMP = 4
TS = 8
NEGV = -30000.0


def tiles_of(a, b, step):
    out = []
    c = a
    while c < b:
        out.append((c, min(step, b - c)))
        c += step
    return out


def build(SEQ, NPG, NPOOL, debug=()):
    U = 3 + NMETA + SEQ + NSAMP * 11
    P0 = 3
    PL = NMETA + SEQ
    SB0 = 3 + PL
    nc = bass.Bass("TRN2", target_bir_lowering=False)
    fw = FW(nc)
    fw.init_banks()

    def din(name, shape, dt=F32):
        return nc.dram_tensor(name, list(shape), dt, kind="ExternalInput").ap()

    def dout(name, shape, dt=F32):
        return nc.dram_tensor(name, list(shape), dt, kind="ExternalOutput").ap()

    def dscr(name, shape, dt=F32):
        kind = "ExternalOutput" if name in debug else "Internal"
        return nc.dram_tensor(name, list(shape), dt, kind=kind).ap()

    xin = din("xin", [U, D])
    W = {}
    for nm, shp in [("ffn1_w_gate", [D, DFF]), ("ffn1_w_up", [D, DFF]), ("ffn1_w_down", [DFF, D]),
                    ("ffn2_w_gate", [D, DFF]), ("ffn2_w_up", [D, DFF]), ("ffn2_w_down", [DFF, D]),
                    ("w_in", [D, NIN]), ("w_kpe_sw", [D, ROPE]), ("w_q_b", [QL, MH * 96]), ("w_q_b_sw", [QL, MH * 96]),
                    ("w_uk", [KL, MH * NOPE]), ("w_ukT", [NOPE, MH * KL]), ("w_uv", [KL, MH * VH]),
                    ("w_a_out", [DI, D]), ("w_b_out", [MH * VH, D]), ("w_o", [D, D])]:
        W[nm] = din(nm, shp)
    gcols = din("gcols", [128, 3, 8])
    grows = din("grows", [1, 3 * D + DI + QL + KL + 3 * NH])
    gncol = din("gncol", [128, 16])
    convw = din("convw", [128, 24 * 4])
    convb = din("convb", [128, 24])
    cf32 = din("cf32", [128, 128 * 4 + NH * 128 + 1])
    cbf = din("cbf", [128, 128 + 4 * 512 + 128])
    sel3_d = din("sel3", [96, NH * 128])
    cs96 = din("cs96", [96, 2, U])
    csU = din("csU", [U, 2, 32])
    cache_kv = din("cache_kv", [NPOOL * 32, 4 * KL])
    cache_kr = din("cache_kr", [NPOOL * 32, 4 * ROPE])
    ptab = din("ptab", [1, NSAMP * NPG], I32)
    st_ssm = din("st_ssm", [NSAMP, DI, DS])
    st_conv = din("st_conv", [NSAMP * 3, CD])

    yout = dout("yout", [U, D])
    kvlat = dout("kvlat", [U, KL])
    krope = dout("krope", [U, ROPE])
    ssm_o = dout("ssm_o", [1 + NSAMP, DI, DS])
    conv_o = dout("conv_o", [47, CD])

    res1 = dscr("res1", [U, D])
    res2 = dscr("res2", [U, D])
    zs_d = dscr("zs_d", [U, DI], BF16)
    xc_d = dscr("xc_d", [24, 128, U], BF16)
    dtr_d = dscr("dtr_d", [U, NH])
    gT_d = dscr("gT_d", [16, 128, U], BF16)
    mT_d = dscr("mT_d", [8, 128, U], BF16)
    oT_d = dscr("oT_d", [MH, 64, U], BF16)
    R = {k: fw.region(k) for k in ["xin", "w", "res1", "res2", "zs", "xc", "dtr", "gT", "mT", "oT", "yout", "kvlat", "krope", "ssm_o", "conv_o", "consts", "cache"]}

    identf = fw.sb("identf", [128, 128])
    tri = fw.sb("tri", [128, 128])
    tris = fw.sb("tris", [128, 128])
    onesf = fw.sb("onesf", [128, 128])
    identb = fw.sb("identb", [128, 128], BF16)
    cbf_sb = fw.sb("cbf_sb", [128, 128 + 4 * 512 + 128], BF16)
    gcol_sb = fw.sb("gcol_sb", [128, 3, 8])
    eps_sb = fw.sb("eps_sb", [128, 1])
    fw.dma("sp", (identf, identf[:]), (R["consts"], cf32[:, 0:128]))
    fw.dma("sp", (tri, tri[:]), (R["consts"], cf32[:, 128:256]))
    fw.dma("sp", (tris, tris[:]), (R["consts"], cf32[:, 256:384]))
    fw.dma("sp", (onesf, onesf[:]), (R["consts"], cf32[:, 384:512]))
    fw.dma("pool", (cbf_sb, cbf_sb[:]), (R["consts"], cbf))
    fw.dma("sp", (gcol_sb, gcol_sb[:]), (R["consts"], gcols))
    fw.cp("dve", (identb, identb[:]), (identf, identf[:]))
    fw.memset("dve", (eps_sb, eps_sb[:]), EPS)
    negsl = cbf_sb[:, 0:128]
    qaT = fw.sb("qaT", [128, 4, U], BF16)
    ckvT = fw.sb("ckvT", [128, 2, U], BF16)
    kpeT = fw.sb("kpeT", [96, U], BF16)

    def negm(r):
        return cbf_sb[:, 128 + r * 512:128 + (r + 1) * 512]
    neg8 = cbf_sb[0:8, 128 + 2048:128 + 2048 + 128]

    def bcast_row(name, off, n, dt=F32):
        t = fw.sb(name, [128, n], dt)
        fw.dma("sp" if dt == F32 else "pool", (t, t[:]), (R["consts"], grows[0, off:off + n].partition_broadcast(128)))
        return t

    rowtiles = tiles_of(0, U, 128)

    def rstd_from(ss_tt, ss_ap, n, rows, out_tt, out_ap):
        fw.act((out_tt, out_ap), (ss_tt, ss_ap), AF.Sqrt, bias=(eps_sb, eps_sb[0:rows, :]), scale=1.0 / n)
        fw.recip((out_tt, out_ap), (out_tt, out_ap))

    class _H:
        pass
    NR = _H()
    st_ring = Ring(fw, "stat", 4, [128, 8])
    nr_cnt = [0]

    def alloc_norm_rings():
        nr_cnt[0] += 1
        NR.xrow = Ring(fw, "xrow%d_" % nr_cnt[0], 2, [128, D])
        NR.junk = Ring(fw, "junk%d_" % nr_cnt[0], 2, [128, D])
        NR.xn = Ring(fw, "xnb%d_" % nr_cnt[0], 2, [128, D], BF16)

    def norm_transpose(src_ap, src_reg, which, xT, c_lo, c_hi):
        for (r0, rn) in tiles_of(c_lo, c_hi, 128):
            xr = NR.xrow.next()
            fw.dma("sp", (xr, xr[0:rn, :]), (src_reg, src_ap[r0:r0 + rn, :]))
            jk = NR.junk.next()
            st = st_ring.next()
            fw.act((jk, jk[0:rn, :]), (xr, xr[0:rn, :]), AF.Square, accum=(st, st[0:rn, 0:1]))
            rstd_from(st, st[0:rn, 0:1], D, rn, st, st[0:rn, 1:2])
            xn = NR.xn.next()
            fw.ts("dve", (xn, xn[0:rn, :]), (xr, xr[0:rn, :]), (st, st[0:rn, 1:2]), ALU.mult)
            for half in range(2):
                bk = fw.bank()
                bv = bk.t[:].bitcast(BF16)
                for kk in range(4):
                    k = half * 4 + kk
                    fw.tr((bk, bv[:, kk * 128:kk * 128 + rn]), (xn, xn[0:rn, k * 128:(k + 1) * 128]), (identb, identb[0:rn, 0:rn]))
                for kk in range(4):
                    k = half * 4 + kk
                    eng = "dve" if kk % 2 == 0 else "pool"
                    if eng == "pool":
                        fw.act((xT, xT[:, k, r0 - c_lo:r0 - c_lo + rn]), (bk, bv[:, kk * 128:kk * 128 + rn]), AF.Copy,
                               scale=(gcol_sb, gcol_sb[:, which, k:k + 1]))
                    else:
                        fw.ts("dve", (xT, xT[:, k, r0 - c_lo:r0 - c_lo + rn]), (bk, bv[:, kk * 128:kk * 128 + rn]),
                              (gcol_sb, gcol_sb[:, which, k:k + 1]), ALU.mult)

    def postnorm_residual(banks, rn, g_bc, res_ap, res_reg, dst_ap, dst_reg, r0, final=False):
        st = st_ring.next()
        for half in range(2):
            jk = NR.junk.next()
            fw.act((jk, jk[0:rn, 0:512]), (banks[half], banks[half].t[0:rn, :]), AF.Square, accum=(st, st[0:rn, half:half + 1]))
        fw.tt("dve", (st, st[0:rn, 2:3]), (st, st[0:rn, 0:1]), (st, st[0:rn, 1:2]), ALU.add)
        rstd_from(st, st[0:rn, 2:3], D, rn, st, st[0:rn, 3:4])
        xr = NR.xrow.next()
        fw.dma("sp", (xr, xr[0:rn, :]), (res_reg, res_ap[r0:r0 + rn, :]))
        o = NR.junk.next()
        for half in range(2):
            fw.stt((o, o[0:rn, half * 512:(half + 1) * 512]), (banks[half], banks[half].t[0:rn, :]), (st, st[0:rn, 3:4]),
                   (g_bc, g_bc[0:rn, half * 512:(half + 1) * 512]), ALU.mult, ALU.mult)
        fw.tt("pool", (o, o[0:rn, :]), (o, o[0:rn, :]), (xr, xr[0:rn, :]), ALU.add)
        fw.dma("sp", (dst_reg, dst_ap[r0:r0 + rn, :]), (o, o[0:rn, :]), final=final)

    wring_h = [None]

    def load_w(w_ap, c0, cn, kchunks=8):
        t = wring_h[0].next()
        fw.dma("pool", (t, t[:, 0:kchunks, 0:cn]), (R["w"], w_ap[:, c0:c0 + cn].rearrange("(k p) n -> p k n", p=128)))
        return t

    def ffn(pfx, which_pre, gpost_off, src_ap, src_reg, dst_ap, dst_reg, final):
        fw.push_scope()
        alloc_norm_rings()
        wring_h[0] = Ring(fw, "wst_" + pfx, 4, [128, 8, 512], BF16)
        gpost = bcast_row("gpost_" + pfx, gpost_off, D)
        fw.ts("dve", (gpost, gpost[:]), (gpost, gpost[:]), 0.5, ALU.mult)
        supers = tiles_of(0, U, 768)
        supers = [(a, a + n) for (a, n) in supers]
        maxc = max(b - a for a, b in supers)
        xT = fw.sb("xT_" + pfx, [128, 8, maxc], BF16)
        hT = fw.sb("hT_" + pfx, [128, 22, maxc], BF16)
        wd = fw.sb("wd_" + pfx, [128, 22, D], BF16)
        for kk in range(0, 22, 2):
            fw.dma("pool", (wd, wd[:, kk:kk + 2, :]), (R["w"], W[pfx + "_w_down"][kk * 128:(kk + 2) * 128, :].rearrange("(k p) n -> p k n", p=128)))
        sg_ring = Ring(fw, "sg_" + pfx, 2, [128, 512])
        for (c_lo, c_hi) in supers:
            norm_transpose(src_ap, src_reg, which_pre, xT, c_lo, c_hi)
            for (g0, gn) in tiles_of(0, DFF, 512):
                wg = load_w(W[pfx + "_w_gate"], g0, gn)
                wu = load_w(W[pfx + "_w_up"], g0, gn)
                for jj in range(gn // 128):
                    j = g0 // 128 + jj
                    for (t0, tn) in tiles_of(0, c_hi - c_lo, 512):
                        bg = fw.bank()
                        bu = fw.bank()
                        for k in range(8):
                            fw.mm((bg, bg.t[:, 0:tn]), (wg, wg[:, k, jj * 128:(jj + 1) * 128]), (xT, xT[:, k, t0:t0 + tn]), k == 0, k == 7)
                        for k in range(8):
                            fw.mm((bu, bu.t[:, 0:tn]), (wu, wu[:, k, jj * 128:(jj + 1) * 128]), (xT, xT[:, k, t0:t0 + tn]), k == 0, k == 7)
                        sg = sg_ring.next()
                        fw.act((sg, sg[:, 0:tn]), (bg, bg.t[:, 0:tn]), AF.Silu)
                        fw.tt("dve", (hT, hT[:, j, t0:t0 + tn]), (sg, sg[:, 0:tn]), (bu, bu.t[:, 0:tn]), ALU.mult)
            for (r0, rn) in tiles_of(c_lo, c_hi, 128):
                bks = [fw.bank(), fw.bank()]
                for half in range(2):
                    for j in range(22):
                        fw.mm((bks[half], bks[half].t[0:rn, :]), (hT, hT[:, j, r0 - c_lo:r0 - c_lo + rn]),
                              (wd, wd[:, j, half * 512:(half + 1) * 512]), j == 0, j == 21)
                postnorm_residual(bks, rn, gpost, src_ap, src_reg, dst_ap, dst_reg, r0, final=final)
        fw.pop_scope()


    OFF_Z, OFF_X, OFF_DT, OFF_QA, OFF_KV, OFF_KPE, OFF_GA = 0, DI, DI + CD, DI + CD + NH, DI + CD + NH + QL, DI + CD + NH + QL + KL, DI + CD + NH + QL + KL + ROPE
    GOFF = {"ssm": 3 * D, "qa": 3 * D + DI, "kv": 3 * D + DI + QL, "dtb": 3 * D + DI + QL + KL, "alog": 3 * D + DI + QL + KL + NH, "dsk": 3 * D + DI + QL + KL + 2 * NH}
    SB = [SB0 + 11 * j for j in range(NSAMP)]

    def mix_proj():
        fw.push_scope()
        alloc_norm_rings()
        wring_h[0] = Ring(fw, "wst_mp", 4, [128, 8, 512], BF16)
        xT = fw.sb("xTm", [128, 8, U], BF16)
        norm_transpose(res1, R["res1"], 1, xT, 0, U)
        o512 = Ring(fw, "o512", 3, [128, 512], BF16)
        for (g0, gn) in tiles_of(OFF_Z, OFF_X, 512):
            wt = load_w(W["w_in"], g0, gn)
            for (r0, rn) in rowtiles:
                b = fw.bank()
                for k in range(8):
                    fw.mm((b, b.t[0:rn, :]), (xT, xT[:, k, r0:r0 + rn]), (wt, wt[:, k, :]), k == 0, k == 7)
                o = o512.next()
                fw.act((o, o[0:rn, :]), (b, b.t[0:rn, :]), AF.Silu)
                fw.dma("sp", (R["zs"], zs_d[r0:r0 + rn, g0:g0 + gn]), (o, o[0:rn, :]))
        convw_sb = fw.sb("convw_sb", [128, 96])
        convb_sb = fw.sb("convb_sb", [128, 24])
        fw.dma("sp", (convw_sb, convw_sb[:]), (R["consts"], convw))
        fw.dma("sp", (convb_sb, convb_sb[:]), (R["consts"], convb))
        diag = fw.sb("diag", [128, 24, 4, 128], BF16)
        for c in range(24):
            for k in range(4):
                fw.ts("dve" if (c + k) % 2 == 0 else "pool", (diag, diag[:, c, k, :]), (identf, identf[:]), (convw_sb, convw_sb[:, c * 4 + k:c * 4 + k + 1]), ALU.mult)
        stc = fw.sb("stc", [12, CD])
        fw.dma("sp", (stc, stc[:]), (R["consts"], st_conv))
        scT = fw.sb("scT", [128, 24, 12], BF16)
        b = fw.bank()
        for c in range(24):
            fw.tr((b, b.t[:, c * 12:(c + 1) * 12]), (stc, stc[0:12, c * 128:(c + 1) * 128]), (identf, identf[0:12, 0:12]))
        fw.cp("dve", (scT, scT[:].rearrange("p a b -> p (a b)")), (b, b.t[:, 0:288]))
        xpre_ring = Ring(fw, "xpre", 3, [128, U], BF16)
        pend_conv = [None]
        cn_ring = Ring(fw, "cn", 2, [47, 512])
        for gi, (g0, gn) in enumerate(tiles_of(OFF_X, OFF_DT, 512)):
            wt = load_w(W["w_in"], g0, gn)
            b = fw.bank()
            for k in range(8):
                fw.mm((b, b.t[0:47, :]), (xT, xT[:, k, U - 47:U]), (wt, wt[:, k, :]), k == 0, k == 7)
            cn = cn_ring.next()
            fw.cp("dve", (cn, cn[:]), (b, b.t[0:47, :]))
            fw.dma("sp", (R["conv_o"], conv_o[:, gi * 512:(gi + 1) * 512]), (cn, cn[:]), final=True)
            for jj in range(4):
                c = gi * 4 + jj
                xp = xpre_ring.next()
                for ti, (t0, tn) in enumerate(tiles_of(0, U, 512)):
                    b = fw.bank()
                    for k in range(8):
                        fw.mm((b, b.t[:, 0:tn]), (wt, wt[:, k, jj * 128:(jj + 1) * 128]), (xT, xT[:, k, t0:t0 + tn]), k == 0, k == 7)
                    fw.cp("dve" if ti % 2 == 0 else "act", (xp, xp[:, t0:t0 + tn]), (b, b.t[:, 0:tn]))
                for j in range(NSAMP):
                    fw.cp("pool", (xp, xp[:, SB[j]:SB[j] + 3]), (scT, scT[:, c, 3 * j:3 * j + 3]))

                def conv(c=c, xp=xp):
                    for (t0, tn) in tiles_of(3, U, 512):
                        b = fw.bank()
                        for k in range(4):
                            fw.mm((b, b.t[:, 0:tn]), (diag, diag[:, c, k, :]), (xp, xp[:, t0 - 3 + k:t0 - 3 + k + tn]), k == 0, k == 3)
                        o = o512.next()
                        fw.act((o, o[:, 0:tn]), (b, b.t[:, 0:tn]), AF.Silu, bias=(convb_sb, convb_sb[:, c:c + 1]))
                        fw.dma("sp", (R["xc"], xc_d[c, :, t0:t0 + tn]), (o, o[:, 0:tn]))
                if pend_conv[0] is not None:
                    pend_conv[0]()
                pend_conv[0] = conv
        pend_conv[0]()
        wq = load_w(W["w_in"], OFF_QA, QL)
        wk = load_w(W["w_in"], OFF_KV, KL + ROPE)
        fw.dma("pool", (wk, wk[:, :, KL + ROPE:KL + 2 * ROPE]), (R["w"], W["w_kpe_sw"].rearrange("(k p) n -> p k n", p=128)))
        wdt = load_w(W["w_in"], OFF_DT, NH)
        gq = bcast_row("gq_bc", GOFF["qa"], QL)
        gkv = bcast_row("gkv_bc", GOFF["kv"], KL)
        cs_ring = Ring(fw, "csr", 2, [128, 2, 32])
        sm_ring = Ring(fw, "smr", 3, [128, 512])
        kpad = fw.sb("kpad", [128, 96], BF16)
        fw.memset("dve", (kpad, kpad[:]), 0.0)
        def small_mm(r0, rn):
            bq = fw.bank()
            bk2 = fw.bank()
            bd = fw.bank()
            for k in range(8):
                fw.mm((bq, bq.t[0:rn, :]), (xT, xT[:, k, r0:r0 + rn]), (wq, wq[:, k, 0:QL]), k == 0, k == 7)
            for k in range(8):
                fw.mm((bk2, bk2.t[0:rn, 0:320]), (xT, xT[:, k, r0:r0 + rn]), (wk, wk[:, k, 0:320]), k == 0, k == 7)
            for k in range(8):
                fw.mm((bd, bd.t[0:rn, 0:NH]), (xT, xT[:, k, r0:r0 + rn]), (wdt, wdt[:, k, 0:NH]), k == 0, k == 7)
            return bq, bk2, bd

        nxt = small_mm(*rowtiles[0])
        for ri, (r0, rn) in enumerate(rowtiles):
            bq, bk2, bd = nxt
            if ri + 1 < len(rowtiles):
                nxt = small_mm(*rowtiles[ri + 1])
            sm = sm_ring.next()
            fw.cp("dve", (sm, sm[0:rn, 0:NH]), (bd, bd.t[0:rn, 0:NH]))
            fw.dma("sp", (R["dtr"], dtr_d[r0:r0 + rn, :]), (sm, sm[0:rn, 0:NH]))
            st = st_ring.next()
            jk = NR.junk.next()
            fw.act((jk, jk[0:rn, 0:QL]), (bq, bq.t[0:rn, :]), AF.Square, accum=(st, st[0:rn, 0:1]))
            rstd_from(st, st[0:rn, 0:1], QL, rn, st, st[0:rn, 1:2])
            fw.act((jk, jk[0:rn, 512:512 + KL]), (bk2, bk2.t[0:rn, 0:KL]), AF.Square, accum=(st, st[0:rn, 2:3]))
            rstd_from(st, st[0:rn, 2:3], KL, rn, st, st[0:rn, 3:4])
            xn = NR.xn.next()
            fw.stt((xn, xn[0:rn, 0:QL]), (bq, bq.t[0:rn, :]), (st, st[0:rn, 1:2]), (gq, gq[0:rn, :]), ALU.mult, ALU.mult)
            ck = sm_ring.next()
            fw.stt((ck, ck[0:rn, 0:KL]), (bk2, bk2.t[0:rn, 0:KL]), (st, st[0:rn, 3:4]), (gkv, gkv[0:rn, :]), ALU.mult, ALU.mult)
            fw.dma("sp", (R["kvlat"], kvlat[r0:r0 + rn, :]), (ck, ck[0:rn, 0:KL]), final=True)
            fw.cp("pool", (xn, xn[0:rn, 512:512 + KL]), (ck, ck[0:rn, 0:KL]))
            cs = cs_ring.next()
            fw.dma("sp", (cs, cs[0:rn]), (R["consts"], csU[r0:r0 + rn]))
            kr = sm_ring.next()
            fw.tt("dve", (kr, kr[0:rn, 0:32]), (bk2, bk2.t[0:rn, KL:KL + 32]), (cs, cs[0:rn, 0, :]), ALU.mult)
            fw.tt("dve", (kr, kr[0:rn, 32:64]), (bk2, bk2.t[0:rn, KL + 32:KL + 64]), (cs, cs[0:rn, 1, :]), ALU.mult)
            fw.tt("dve", (kr, kr[0:rn, 0:32]), (kr, kr[0:rn, 0:32]), (kr, kr[0:rn, 32:64]), ALU.add)
            fw.dma("sp", (R["krope"], krope[r0:r0 + rn, :]), (kr, kr[0:rn, 0:32]), final=True)
            fw.cp("dve", (kpad, kpad[0:rn, 64:96]), (kr, kr[0:rn, 0:32]))
            bt = fw.bank()
            bv = bt.t[:].bitcast(BF16)
            for kk in range(4):
                fw.tr((bt, bv[:, kk * 128:kk * 128 + rn]), (xn, xn[0:rn, kk * 128:(kk + 1) * 128]), (identb, identb[0:rn, 0:rn]))
            for kk in range(2):
                fw.tr((bt, bv[:, (4 + kk) * 128:(4 + kk) * 128 + rn]), (xn, xn[0:rn, 512 + kk * 128:512 + (kk + 1) * 128]), (identb, identb[0:rn, 0:rn]))
            fw.tr((bt, bv[0:96, 6 * 128:6 * 128 + rn]), (kpad, kpad[0:rn, :]), (identb, identb[0:rn, 0:rn]))
            for kk in range(4):
                fw.cp("dve" if kk % 2 == 0 else "act", (qaT, qaT[:, kk, r0:r0 + rn]), (bt, bv[:, kk * 128:kk * 128 + rn]))
            for kk in range(2):
                fw.cp("act" if kk % 2 == 0 else "dve", (ckvT, ckvT[:, kk, r0:r0 + rn]), (bt, bv[:, (4 + kk) * 128:(4 + kk) * 128 + rn]))
            fw.cp("dve", (kpeT, kpeT[64:96, r0:r0 + rn]), (bt, bv[64:96, 6 * 128:6 * 128 + rn]))
        for gi, (g0, gn) in enumerate(tiles_of(OFF_GA, NIN, 512)):
            wt = load_w(W["w_in"], g0, gn)
            for jj in range(4):
                for (t0, tn) in tiles_of(0, U, 512):
                    b = fw.bank()
                    for k in range(8):
                        fw.mm((b, b.t[:, 0:tn]), (wt, wt[:, k, jj * 128:(jj + 1) * 128]), (xT, xT[:, k, t0:t0 + tn]), k == 0, k == 7)
                    o = o512.next()
                    fw.act((o, o[:, 0:tn]), (b, b.t[:, 0:tn]), AF.Sigmoid)
                    fw.dma("sp", (R["gT"], gT_d[gi * 4 + jj, :, t0:t0 + tn]), (o, o[:, 0:tn]))
        fw.pop_scope()


    def ssd():
        fw.push_scope()
        sel3 = fw.sb("sel3_sb", [96, NH, 128], BF16)
        fw.dma("pool", (sel3, sel3[:].rearrange("p a b -> p (a b)")), (R["consts"], sel3_d))
        s3_ring = Ring(fw, "s3r", 2, [96, 128], BF16)
        r1_ring = Ring(fw, "r1r", 2, [96, 128])
        mt_ring = Ring(fw, "mtr", 2, [96, 128], BF16)
        da3_ring = Ring(fw, "da3r", 2, [128, 3, NH])
        A_bc = bcast_row("A_bc", GOFF["alog"], NH)
        fw.act((A_bc, A_bc[:]), (A_bc, A_bc[:]), AF.Exp)
        fw.ts("dve", (A_bc, A_bc[:]), (A_bc, A_bc[:]), -1.0, ALU.mult)
        dtb_bc = bcast_row("dtb_bc", GOFF["dtb"], NH)
        Drep = bcast_row("Drep", GOFF["dsk"], NH)
        wa = fw.sb("wa", [128, 16, D], BF16)
        for kk in range(0, 16, 4):
            fw.dma("pool", (wa, wa[:, kk:kk + 4, :]), (R["w"], W["w_a_out"][kk * 128:(kk + 4) * 128, :].rearrange("(k p) n -> p k n", p=128)))
        gncol_sb = fw.sb("gncol_sb", [128, 16])
        fw.dma("sp", (gncol_sb, gncol_sb[:]), (R["consts"], gncol))
        for kk in range(16):
            fw.ts("dve", (wa, wa[:, kk, :]), (wa, wa[:, kk, :]), (gncol_sb, gncol_sb[:, kk:kk + 1]), ALU.mult)
        hT = fw.sb("hT", [128, DI])
        hTb = fw.sb("hTb", [128, DI], BF16)
        xc_ring = Ring(fw, "xcr", 2, [128, 24, 128], BF16)
        zs_ring = Ring(fw, "zsr", 2, [128, DI], BF16)
        sm = Ring(fw, "ssm_sm", 2, [128, 8, NH])
        cd_ring = Ring(fw, "cdr", 2, [128, NH])
        xs_ring = Ring(fw, "xstm", 2, [128, DI], BF16)
        xw_ring = Ring(fw, "xwtm", 2, [128, DI], BF16)
        xsD_ring = Ring(fw, "xsD", 2, [128, DI], BF16)
        B_ring = Ring(fw, "Btm", 2, [128, 512], BF16)
        cb_ring = Ring(fw, "cbT", 2, [128, 4, 128], BF16)
        dexp_ring = Ring(fw, "dexp", 3, [128, 128], BF16)
        MT_ring = Ring(fw, "MTg", 2, [128, 8, 128], BF16)
        y_ring = Ring(fw, "yall", 1, [128, DI])
        t_ring = Ring(fw, "ytmp", 2, [128, 512])
        ynb_ring = Ring(fw, "ynb", 1, [128, DI], BF16)
        ynT_ring = Ring(fw, "ynT", 1, [128, 16, 128], BF16)
        ga_ring = Ring(fw, "gar", 2, [128, 8, 128], BF16)
        mA_ring = Ring(fw, "mAr", 2, [128, 8, 128], BF16)
        so_ring = Ring(fw, "sor", 2, [128, 4, 128])

        STG = int(os.environ.get("SSD_STG", "9"))

        def front(c0, cl):
            xc = xc_ring.next()
            for q6 in range(0, 24, 4):
                fw.dma("sp", (xc, xc[:, q6:q6 + 4, 0:cl]), (R["xc"], xc_d[q6:q6 + 4, :, c0:c0 + cl].rearrange("c p u -> p c u")))
            zs = zs_ring.next()
            fw.dma("sp", (zs, zs[0:cl, :]), (R["zs"], zs_d[c0:c0 + cl, :]))
            s_ = sm.next()
            fw.dma("sp", (s_, s_[0:cl, 0, :]), (R["dtr"], dtr_d[c0:c0 + cl, :]))
            fw.tt("dve", (s_, s_[0:cl, 0, :]), (s_, s_[0:cl, 0, :]), (dtb_bc, dtb_bc[0:cl, :]), ALU.add)
            fw.act((s_, s_[0:cl, 0, :]), (s_, s_[0:cl, 0, :]), AF.Exp)
            fw.act((s_, s_[0:cl, 0, :]), (s_, s_[0:cl, 0, :]), AF.Ln, bias=1.0)
            da3 = da3_ring.next()
            fw.tt("dve", (da3, da3[0:cl]), (s_, s_[0:cl, 0:1, :].to_broadcast([cl, 3, NH])), (A_bc, A_bc[0:cl, :].unsqueeze(1).to_broadcast([cl, 3, NH])), ALU.mult)
            bs = fw.bank()
            da = (da3, da3[0:cl, 0, :])
            fw.mm((bs, bs.t[0:cl, 0:32]), (tri, tri[0:cl, 0:cl]), da, True, True)
            fw.mm((bs, bs.t[0:cl, 32:64]), (tris, tris[0:cl, 0:cl]), da, True, True)
            fw.mm((bs, bs.t[:, 64:96]), (onesf, onesf[0:cl, :]), da, True, True)
            fw.mm((bs, bs.t[0:96, 128:128 + cl]), (da3, da3[0:cl].rearrange("p a b -> p (a b)")), (tri, tri[0:cl, 0:cl]), True, True)
            fw.ts("dve", (s_, s_[0:cl, 2, :]), (bs, bs.t[0:cl, 0:32]), -1.0, ALU.mult)
            fw.act((s_, s_[0:cl, 3, :]), (bs, bs.t[0:cl, 0:32]), AF.Exp)
            fw.act((s_, s_[0:cl, 4, :]), (bs, bs.t[0:cl, 32:64]), AF.Exp)
            fw.tt("dve", (s_, s_[0:cl, 4, :]), (s_, s_[0:cl, 4, :]), (s_, s_[0:cl, 0, :]), ALU.mult)
            cd = cd_ring.next()
            fw.act((cd, cd[:]), (bs, bs.t[:, 64:96]), AF.Exp)
            acsT = s3_ring.next()
            r1 = r1_ring.next()
            mtp = mt_ring.next()
            fw.cp("dve", (acsT, acsT[0:96, 0:cl]), (bs, bs.t[0:96, 128:128 + cl]))
            fw.tt("dve", (r1, r1[32:64, 0:cl]), (bs, bs.t[32:64, 128:128 + cl]), (acsT, acsT[32:64, 0:cl]), ALU.subtract)
            fw.tt("dve", (r1, r1[64:96, 0:cl]), (bs, bs.t[64:96, 128:128 + cl]), (acsT, acsT[64:96, 0:cl]), ALU.subtract)
            fw.cp("dve", (acsT, acsT[32:64, 0:cl]), (r1, r1[32:64, 0:cl]))
            fw.cp("dve", (mtp, mtp[64:96, 0:cl]), (r1, r1[64:96, 0:cl]))
            fw.tt("dve", (r1, r1[64:96, 0:cl]), (r1, r1[64:96, 0:cl]), (mtp, mtp[64:96, 0:cl]), ALU.subtract)
            fw.cp("dve", (acsT, acsT[64:96, 0:cl]), (r1, r1[64:96, 0:cl]))
            xs = xs_ring.next()
            for half in range(2):
                bt = fw.bank()
                bv = bt.t[:].bitcast(BF16)
                for kk in range(8):
                    fw.tr((bt, bv[0:cl, kk * 128:(kk + 1) * 128]), (xc, xc[:, half * 8 + kk, 0:cl]), (identb, identb[:]))
                fw.cp("dve" if half == 0 else "act", (xs, xs[0:cl, half * 1024:(half + 1) * 1024]), (bt, bv[0:cl, :]))
            Bt = B_ring.next()
            bt = fw.bank()
            bv = bt.t[:].bitcast(BF16)
            for g in range(4):
                fw.tr((bt, bv[0:cl, g * 128:(g + 1) * 128]), (xc, xc[:, 16 + g, 0:cl]), (identb, identb[:]))
            fw.cp("dve", (Bt, Bt[0:cl, :]), (bt, bv[0:cl, 0:512]))
            xw = xw_ring.next()
            fw.tt("pool", (xw, xw[0:cl, :].rearrange("p (h d) -> p h d", h=NH)), (xs, xs[0:cl, :].rearrange("p (h d) -> p h d", h=NH)),
                  (s_, s_[0:cl, 4, :].unsqueeze(2).to_broadcast([cl, NH, HD])), ALU.mult)
            xsD = xsD_ring.next()
            fw.tt("pool", (xsD, xsD[0:cl, :].rearrange("p (h d) -> p h d", h=NH)), (xs, xs[0:cl, :].rearrange("p (h d) -> p h d", h=NH)),
                  (Drep, Drep[0:cl, :].unsqueeze(2).to_broadcast([cl, NH, HD])), ALU.mult)
            bc = fw.bank()
            for g in range(4):
                fw.mm((bc, bc.t[0:cl, g * 128:g * 128 + cl]), (xc, xc[:, 16 + g, 0:cl]), (xc, xc[:, 20 + g, 0:cl]), True, True)
            cbT = cb_ring.next()
            fw.cp("act", (cbT, cbT[0:cl].rearrange("p a b -> p (a b)")), (bc, bc.t[0:cl, :]))
            C = _H()
            C.c0, C.cl, C.xc, C.zs, C.s_, C.cd, C.acsT, C.xs, C.Bt, C.xw, C.cbT = c0, cl, xc, zs, s_, cd, acsT, xs, Bt, xw, cbT
            C.xsD = xsD
            C.MT = {}
            return C

        def stageA(C, g):
            c0, cl, xc, zs, s_, cd, acsT, xs, Bt, xw, cbT = C.c0, C.cl, C.xc, C.zs, C.s_, C.cd, C.acsT, C.xs, C.Bt, C.xw, C.cbT
            MT = MT_ring.next()
            C.MT[g] = MT
            for half in range(2):
                bm = fw.bank()
                for hh in range(4):
                    h = g * 8 + half * 4 + hh
                    fw.mm((bm, bm.t[0:cl, hh * 128:hh * 128 + cl]), (sel3, sel3[:, h, 0:cl]), (acsT, acsT[0:96, 0:cl]), True, False)
                    fw.mm((bm, bm.t[0:cl, hh * 128:hh * 128 + cl]), (identb, identb[0:cl, 0:cl]), (cbf_sb, negsl[0:cl, 0:cl]), False, True)
                for hh in range(4):
                    h = g * 8 + half * 4 + hh
                    de = dexp_ring.next()
                    fw.act((de, de[0:cl, 0:cl]), (bm, bm.t[0:cl, hh * 128:hh * 128 + cl]), AF.Exp, bias=(s_, s_[0:cl, 2, h:h + 1]))
                    fw.stt((MT, MT[0:cl, half * 4 + hh, 0:cl]), (de, de[0:cl, 0:cl]), (s_, s_[0:cl, 0, h:h + 1]), (cbT, cbT[0:cl, g, 0:cl]), ALU.mult, ALU.mult)

        def stageB(C, g):
            c0, cl, xc, zs, s_, cd, acsT, xs, Bt, xw, cbT = C.c0, C.cl, C.xc, C.zs, C.s_, C.cd, C.acsT, C.xs, C.Bt, C.xw, C.cbT
            MT = C.MT[g]
            if g == 0:
                C.yall = y_ring.next()
            yall = C.yall
            by = fw.bank()
            xsD = C.xsD
            fw.mm((by, by.t[0:cl, :]), (identb, identb[0:cl, 0:cl]), (xsD, xsD[0:cl, g * 512:(g + 1) * 512]), True, False)
            for hh in range(8):
                h = g * 8 + hh
                fw.mm((by, by.t[0:cl, hh * 64:(hh + 1) * 64]), (MT, MT[0:cl, hh, 0:cl]), (xs, xs[0:cl, h * 64:(h + 1) * 64]), False, hh == 7)
            bo = fw.bank()
            fw.mm((bo, bo.t[0:cl, :]), (xc, xc[:, 20 + g, 0:cl]), (hTb, hTb[:, g * 512:(g + 1) * 512]), True, True)
            yg = (yall, yall[0:cl, g * 512:(g + 1) * 512])
            fw.tt("dve", (yall, yall[0:cl, g * 512:(g + 1) * 512].rearrange("p (h d) -> p h d", h=8)), (bo, bo.t[0:cl, :].rearrange("p (h d) -> p h d", h=8)),
                  (s_, s_[0:cl, 3, g * 8:(g + 1) * 8].unsqueeze(2).to_broadcast([cl, 8, HD])), ALU.mult)
            fw.tt("dve", yg, yg, (by, by.t[0:cl, :]), ALU.add)
            tmp = t_ring.next()
            fw.tt("dve", yg, yg, (zs, zs[0:cl, g * 512:(g + 1) * 512]), ALU.mult)
            fw.act((tmp, tmp[0:cl, :]), yg, AF.Square, accum=(s_, s_[0:cl, 5, g:g + 1]))
            bst = fw.bank()
            fw.mm((bst, bst.t[:, :]), (Bt, Bt[0:cl, g * 128:(g + 1) * 128]), (xw, xw[0:cl, g * 512:(g + 1) * 512]), True, True)
            hg = (hT, hT[:, g * 512:(g + 1) * 512])
            fw.tt("dve", (hT, hT[:, g * 512:(g + 1) * 512].rearrange("p (h d) -> p h d", h=8)), (hT, hT[:, g * 512:(g + 1) * 512].rearrange("p (h d) -> p h d", h=8)),
                  (cd, cd[:, g * 8:(g + 1) * 8].unsqueeze(2).to_broadcast([128, 8, HD])), ALU.mult)
            fw.tt("dve", hg, hg, (bst, bst.t[:, :]), ALU.add)
            fw.cp("act", (hTb, hTb[:, g * 512:(g + 1) * 512]), hg)

        def tail(C):
            c0, cl, xc, zs, s_, cd, acsT, xs, Bt, xw, cbT = C.c0, C.cl, C.xc, C.zs, C.s_, C.cd, C.acsT, C.xs, C.Bt, C.xw, C.cbT
            yall = C.yall
            fw.act((s_, s_[0:cl, 5, 4:8]), (s_, s_[0:cl, 5, 0:4]), AF.Sqrt, bias=(eps_sb, eps_sb[0:cl, :]), scale=1.0 / 512)
            fw.recip((s_, s_[0:cl, 5, 4:8]), (s_, s_[0:cl, 5, 4:8]))
            ynb = ynb_ring.next()
            for g in range(4):
                fw.act((ynb, ynb[0:cl, g * 512:(g + 1) * 512]), (yall, yall[0:cl, g * 512:(g + 1) * 512]), AF.Copy, scale=(s_, s_[0:cl, 5, 4 + g:5 + g]))
            ynT = ynT_ring.next()
            for half in range(2):
                bt = fw.bank()
                bv = bt.t[:].bitcast(BF16)
                for kk in range(8):
                    fw.tr((bt, bv[:, kk * 128:kk * 128 + cl]), (ynb, ynb[0:cl, (half * 8 + kk) * 128:(half * 8 + kk + 1) * 128]), (identb, identb[0:cl, 0:cl]))
                fw.cp("dve" if half == 0 else "act", (ynT, ynT[:, half * 8:(half + 1) * 8, 0:cl]), (bt, bv[:, :].rearrange("p (a b) -> p a b", a=8)[:, :, 0:cl]))
            ga = ga_ring.next()
            fw.dma("sp", (ga, ga[:, :, 0:cl]), (R["gT"], gT_d[0:8, :, c0:c0 + cl].rearrange("c p u -> p c u")))
            mA = mA_ring.next()
            for half in range(2):
                bw = fw.bank()
                for oo in range(4):
                    oc = half * 4 + oo
                    for k in range(16):
                        fw.mm((bw, bw.t[:, oo * 128:oo * 128 + cl]), (wa, wa[:, k, oc * 128:(oc + 1) * 128]), (ynT, ynT[:, k, 0:cl]), k == 0, k == 15)
                fw.tt("dve", (mA, mA[:, half * 4:(half + 1) * 4, 0:cl]), (bw, bw.t[:, :].rearrange("p (a b) -> p a b", a=4)[:, :, 0:cl]), (ga, ga[:, half * 4:(half + 1) * 4, 0:cl]), ALU.mult)
            fw.dma("sp", (R["mT"], mT_d[:, :, c0:c0 + cl].rearrange("c p u -> p c u")), (mA, mA[:, :, 0:cl]))


        def run_seq(chunks):
            Cs = {}
            seq = [(ci, g) for ci in range(len(chunks)) for g in range(4)]
            Cs[0] = front(*chunks[0])
            stageA(Cs[0], 0)
            for n, (ci, g) in enumerate(seq):
                if n + 1 < len(seq):
                    ci2, g2 = seq[n + 1]
                    if g2 == 0:
                        Cs[ci2] = front(*chunks[ci2])
                    stageA(Cs[ci2], g2)
                stageB(Cs[ci], g)
                if g == 3:
                    tail(Cs[ci])
                    del Cs[ci]

        def store_state(idx):
            if STG < 1 and STG != -1:
                return
            for q4 in range(4):
                bt = fw.bank()
                for kk in range(4):
                    k = q4 * 4 + kk
                    fw.tr((bt, bt.t[:, kk * 128:(kk + 1) * 128]), (hT, hT[:, k * 128:(k + 1) * 128]), (identf, identf[:]))
                so = so_ring.next()
                fw.cp("dve", (so, so[:].rearrange("p a b -> p (a b)")), (bt, bt.t[:, :]))
                fw.dma("sp", (R["ssm_o"], ssm_o[idx, q4 * 512:(q4 + 1) * 512, :].rearrange("(k p) n -> p k n", p=128)), (so, so[:]), final=True)

        fw.memset("dve", (hT, hT[:]), 0.0)
        fw.memset("pool", (hTb, hTb[:]), 0.0)
        run_seq([(P0, NMETA)] + [(P0 + NMETA + 128 * i, 128) for i in range(SEQ // 128)])
        store_state(0)
        for j in range(NSAMP if STG != -2 else 0):
            for q4 in range(4):
                si = so_ring.next()
                fw.dma("sp", (si, si[:]), (R["consts"], st_ssm[j, q4 * 512:(q4 + 1) * 512, :].rearrange("(k p) n -> p k n", p=128)))
                if STG == -3:
                    continue
                bt = fw.bank()
                for kk in range(4):
                    fw.tr((bt, bt.t[:, kk * 128:(kk + 1) * 128]), (si, si[:, kk, :]), (identf, identf[:]))
                if STG == -4:
                    continue
                fw.cp("dve", (hT, hT[:, q4 * 512:(q4 + 1) * 512]), (bt, bt.t[:, :]))
                if STG == -5:
                    continue
                fw.cp("act", (hTb, hTb[:, q4 * 512:(q4 + 1) * 512]), (bt, bt.t[:, :]))
            run_seq([(SB[j] + 3, TS)])
            store_state(1 + j)
        fw.pop_scope()


    def attn():
        fw.push_scope()
        ASTG = int(os.environ.get("ATT_STG", "9"))
        NST = (PL + 127) // 128
        NPG4 = NPG // 4
        qlatT = fw.sb("qlatT", [128, 2, NSAMP, 128], BF16)
        qpeT = fw.sb("qpeT", [96, NSAMP, 128], BF16)
        wuv = fw.sb("wuv", [128, 2, MH * VH], BF16)
        fw.dma("pool", (wuv, wuv[:]), (R["w"], W["w_uv"].rearrange("(k p) n -> p k n", p=128)))
        fw.push_scope()
        wqb = fw.sb("wqb", [128, 4, MH * 96], BF16)
        wqbs = fw.sb("wqbs", [128, 4, MH * 96], BF16)
        fw.dma("pool", (wqb, wqb[:]), (R["w"], W["w_q_b"].rearrange("(k p) n -> p k n", p=128)))
        fw.dma("pool", (wqbs, wqbs[:]), (R["w"], W["w_q_b_sw"].rearrange("(k p) n -> p k n", p=128)))
        wukp = fw.sb("wukp", [128, 2, MH, 96], BF16)
        fw.memset("dve", (wukp, wukp[:]), 0.0)
        for kc in range(2):
            fw.dma("pool", (wukp, wukp[:, kc, :, 0:NOPE]), (R["w"], W["w_uk"][kc * 128:(kc + 1) * 128, :].rearrange("p (h n) -> p h n", h=MH)))
        wukT = fw.sb("wukT", [NOPE, MH, KL], BF16)
        fw.dma("pool", (wukT, wukT[:]), (R["w"], W["w_ukT"].rearrange("n (h c) -> n h c", h=MH)))
        cs96_sb = fw.sb("cs96_sb", [96, 2, U])
        fw.dma("sp", (cs96_sb, cs96_sb[:]), (R["consts"], cs96))
        V_sb = fw.sb("V_sb", [128, NST, MH, VH + 1], BF16)
        fw.memset("pool", (V_sb, V_sb[:]), 1.0)
        QT = fw.sb("QT", [96, 4, U], BF16)
        KT = fw.sb("KT", [96, 4, U], BF16)
        PT_ring = Ring(fw, "PT", 5, [128, 512], BF16)
        tq_ring = Ring(fw, "tq", 2, [96, 2, 512])
        rd_ring = Ring(fw, "rd", 2, [65, 512])
        bc_ring = Ring(fw, "bcr", 2, [64, 512])
        o_ring = Ring(fw, "osb", 2, [64, 512], BF16)
        for st in range(NST):
            s0 = P0 + 128 * st
            sn = min(128, P0 + PL - s0)
            for half in range(2):
                b = fw.bank()
                for kc in range(2):
                    fw.mm((b, b.t[0:sn, :]), (ckvT, ckvT[:, kc, s0:s0 + sn]), (wuv, wuv[:, kc, half * 512:(half + 1) * 512]), kc == 0, kc == 1)
                fw.cp("dve" if half == 0 else "act", (V_sb, V_sb[0:sn, st, half * 8:(half + 1) * 8, 0:VH]), (b, b.t[0:sn, :].rearrange("p (h v) -> p h v", h=8)))
        accs = fw.reserve(2)
        acc_i = 0
        pend_fin = [None]
        qtiles = tiles_of(0, PL, 512)
        for hg in range(MH // 4):
            for hh in range(4):
                h = hg * 4 + hh
                for (t0, tn) in tiles_of(0, U, 512):
                    bA = fw.bank()
                    bB = fw.bank()
                    for k in range(4):
                        fw.mm((bA, bA.t[0:96, 0:tn]), (wqb, wqb[:, k, h * 96:(h + 1) * 96]), (qaT, qaT[:, k, t0:t0 + tn]), k == 0, k == 3)
                    for k in range(4):
                        fw.mm((bB, bB.t[0:96, 0:tn]), (wqbs, wqbs[:, k, h * 96:(h + 1) * 96]), (qaT, qaT[:, k, t0:t0 + tn]), k == 0, k == 3)
                    tq = tq_ring.next()
                    fw.tt("dve", (tq, tq[:, 0, 0:tn]), (bA, bA.t[0:96, 0:tn]), (cs96_sb, cs96_sb[:, 0, t0:t0 + tn]), ALU.mult)
                    fw.tt("dve", (tq, tq[:, 1, 0:tn]), (bB, bB.t[0:96, 0:tn]), (cs96_sb, cs96_sb[:, 1, t0:t0 + tn]), ALU.mult)
                    fw.tt("pool", (QT, QT[:, hh, t0:t0 + tn]), (tq, tq[:, 0, 0:tn]), (tq, tq[:, 1, 0:tn]), ALU.add)
                    bK = fw.bank()
                    for kc in range(2):
                        fw.mm((bK, bK.t[0:96, 0:tn]), (wukp, wukp[:, kc, h, :]), (ckvT, ckvT[:, kc, t0:t0 + tn]), kc == 0, False)
                    fw.mm((bK, bK.t[0:96, 0:tn]), (identb, identb[64:96, 0:96]), (kpeT, kpeT[64:96, t0:t0 + tn]), False, True)
                    fw.cp("act", (KT, KT[:, hh, t0:t0 + tn]), (bK, bK.t[0:96, 0:tn]))
            for j in range(NSAMP if ASTG >= 2 else 0):
                cs8 = SB[j] + 3
                bq = fw.bank()
                for kc in range(2):
                    for hh in range(4):
                        h = hg * 4 + hh
                        fw.mm((bq, bq.t[:, kc * 32 + hh * 8:kc * 32 + hh * 8 + 8]), (wukT, wukT[:, h, kc * 128:(kc + 1) * 128]), (QT, QT[0:NOPE, hh, cs8:cs8 + 8]), True, True)
                fw.cp("dve", (qlatT, qlatT[:, :, j, hg * 32:(hg + 1) * 32]), (bq, bq.t[:, 0:64].rearrange("p (a b) -> p a b", a=2)))
                fw.cp("pool", (qpeT, qpeT[64:96, j, hg * 32:(hg + 1) * 32].rearrange("p (a b) -> p a b", a=4)), (QT, QT[64:96, :, cs8:cs8 + 8]))
            for hh in range(4 if ASTG >= 3 else 0):
                h = hg * 4 + hh
                for qi, (qp0, qn) in enumerate(qtiles):
                    q0 = P0 + qp0
                    bO = accs[acc_i]
                    acc_i = (acc_i + 1) % 2
                    sts = [st for st in range(NST) if st * 128 <= qp0 + qn - 1]
                    prevq = []

                    def do_pv(pv, bO=bO, qn=qn, h=h, nlast=len(sts) - 1):
                        PT_, st_, si_, sn_ = pv
                        fw.mm((bO, bO.t[0:VH + 1, 0:qn]), (V_sb, V_sb[0:sn_, st_, h, :]), (PT_, PT_[0:sn_, 0:qn]), si_ == 0, si_ == nlast)

                    for si, st in enumerate(sts):
                        s0 = P0 + 128 * st
                        sn = min(128, P0 + PL - s0)
                        diag = (st * 128 + sn - 1) > qp0
                        bS = fw.bank()
                        fw.mm((bS, bS.t[0:sn, 0:qn]), (KT, KT[:, hh, s0:s0 + sn]), (QT, QT[:, hh, q0:q0 + qn]), True, not diag)
                        if diag:
                            r = st - 4 * qi
                            fw.mm((bS, bS.t[0:sn, 0:qn]), (identb, identb[0:sn, 0:sn]), (cbf_sb, negm(r)[0:sn, 0:qn]), False, True)
                        PT = PT_ring.next()
                        fw.act((PT, PT[0:sn, 0:qn]), (bS, bS.t[0:sn, 0:qn]), AF.Exp)
                        if si == 1 and pend_fin[0] is not None:
                            pend_fin[0]()
                            pend_fin[0] = None
                        prevq.append((PT, st, si, sn))
                        if len(prevq) > 2:
                            do_pv(prevq.pop(0))
                    if pend_fin[0] is not None:
                        pend_fin[0]()
                        pend_fin[0] = None
                    while prevq:
                        do_pv(prevq.pop(0))

                    def fin(bO=bO, qn=qn, h=h, q0=q0):
                        rd = rd_ring.next()
                        fw.recip((rd, rd[64:65, 0:qn]), (bO, bO.t[64:65, 0:qn]))
                        bB = fw.bank()
                        fw.mm((bB, bB.t[0:64, 0:qn]), (onesf, onesf[64:65, 0:64]), (rd, rd[64:65, 0:qn]), True, True)
                        bcs = bc_ring.next()
                        fw.cp("act", (bcs, bcs[:, 0:qn]), (bB, bB.t[0:64, 0:qn]))
                        osb = o_ring.next()
                        fw.tt("dve", (osb, osb[:, 0:qn]), (bO, bO.t[0:64, 0:qn]), (bcs, bcs[:, 0:qn]), ALU.mult)
                        fw.dma("sp", (R["oT"], oT_d[h, :, q0:q0 + qn]), (osb, osb[:, 0:qn]))
                    pend_fin[0] = fin
        if pend_fin[0] is not None:
            pend_fin[0]()
            pend_fin[0] = None
        fw.pop_scope()
        fw.push_scope()
        ptb = fw.sb("ptb", [128, NSAMP * NPG], I32)
        fw.dma("sp", (ptb, ptb[:]), (R["consts"], ptab[0, :].partition_broadcast(128)))
        pm = fw.sb("pm", [128, 1])
        fw.dma("sp", (pm, pm[:]), (R["consts"], cf32[:, 512 + NH * 128:512 + NH * 128 + 1]), slow=True)
        idx = fw.sb("idx", [128, NSAMP * NPG4], I32)
        for qd in range(4):
            fw.ts("dve", (idx, idx[32 * qd:32 * qd + 32, :]), (ptb, ptb[32 * qd:32 * qd + 32, :].rearrange("p (i q) -> p i q", q=4)[:, :, qd]),
                  32.0, ALU.mult, s2=(pm, pm[32 * qd:32 * qd + 32, :]), op1=ALU.add)
        kvp_ring = Ring(fw, "kvp", 4, [128, 4, KL], BF16)
        krg_ring = Ring(fw, "krg", 3, [128, 4, ROPE])
        kv32_ring = Ring(fw, "kv32", 2, [128, 4, KL])
        krp_ring = Ring(fw, "krp", 4, [128, 4, 96], BF16)
        ones_b = fw.sb("ones_b", [128, 8], BF16)
        fw.memset("dve", (ones_b, ones_b[:]), 1.0)
        for t_ in krp_ring.tiles:
            fw.memset("dve", (t_, t_[:]), 0.0)
        KTp_ring = Ring(fw, "KTp", 4, [128, 3, 128], BF16)
        PTs_ring = Ring(fw, "PTs", 4, [128, 128], BF16)
        ckn_ring = Ring(fw, "ckn", 2, [8, KL + 4], BF16)
        for t_ in ckn_ring.tiles:
            fw.memset("dve", (t_, t_[:]), 1.0)
        ol_ring = Ring(fw, "olat", 2, [128, KL], BF16)
        olT_ring = Ring(fw, "olatT", 2, [128, 2, 128], BF16)
        oS_ring = Ring(fw, "oS", 2, [64, MH, 8], BF16)
        sst = Ring(fw, "sst", 2, [128, 2])
        accs = fw.reserve(2)
        ckr = cache_kr.rearrange("r (a b) -> r a b", a=4)
        SUB = int(os.environ.get("SUB", "9"))
        for j in range(NSAMP if ASTG >= 4 else 0):
            cs8 = SB[j] + 3
            bO = accs[j % 2]
            grp = {}

            def do_gather(i, j=j, grp=grp):
                kvp = kvp_ring.next()
                krp = krp_ring.next()
                krg = krg_ring.next()
                kv32 = kv32_ring.next()
                ic = j * NPG4 + i
                fw.gather((kv32, kv32[:].rearrange("p a b -> p (a b)")), cache_kv, R["cache"], (idx, idx[:, ic:ic + 1]))
                fw.gather((krg, krg[:].rearrange("p a b -> p (a b)")), cache_kr, R["cache"], (idx, idx[:, ic:ic + 1]))
                fw.cp("act", (kvp, kvp[:]), (kv32, kv32[:]))
                fw.cp("pool", (krp, krp[:, :, 64:96]), (krg, krg[:]))
                grp[i] = (kvp, krp)

            items = [(i, r) for i in range(NPG4) for r in range(4)]
            st1 = {}
            st2 = {}

            def stage1(k, grp=grp, st1=st1):
                i, r = items[k]
                if r == 0:
                    if i == 0:
                        do_gather(0)
                    if i + 1 < NPG4:
                        do_gather(i + 1)
                kvp, krp = grp[i]
                bt = fw.bank()
                bv = bt.t[:].bitcast(BF16)
                fw.tr((bt, bv[:, 0:128]), (kvp, kvp[:, r, 0:128]), (identb, identb[:]))
                fw.tr((bt, bv[:, 128:256]), (kvp, kvp[:, r, 128:256]), (identb, identb[:]))
                fw.tr((bt, bv[0:96, 256:384]), (krp, krp[:, r, :]), (identb, identb[:]))
                KTp = KTp_ring.next()
                fw.cp("dve", (KTp, KTp[:, 0:2, :]), (bt, bv[:, 0:256].rearrange("p (a b) -> p a b", a=2)))
                fw.cp("dve", (KTp, KTp[0:96, 2, :]), (bt, bv[0:96, 256:384]))
                st1[k] = KTp

            def stage2(k, j=j, st1=st1, st2=st2):
                KTp = st1.pop(k)
                bS = fw.bank()
                fw.mm((bS, bS.t[:, 0:128]), (KTp, KTp[:, 0, :]), (qlatT, qlatT[:, 0, j, :]), True, False)
                fw.mm((bS, bS.t[:, 0:128]), (KTp, KTp[:, 1, :]), (qlatT, qlatT[:, 1, j, :]), False, False)
                fw.mm((bS, bS.t[:, 0:128]), (KTp, KTp[64:96, 2, :]), (qpeT, qpeT[64:96, j, :]), False, True)
                PTs = PTs_ring.next()
                fw.act((PTs, PTs[:]), (bS, bS.t[:, 0:128]), AF.Exp)
                st2[k] = PTs

            def stage3(k, bO=bO, grp=grp, st2=st2):
                i, r = items[k]
                PTs = st2.pop(k)
                kvp, krp = grp[i]
                fw.mm((bO, bO.t[:, KL:KL + 1]), (PTs, PTs[:]), (ones_b, ones_b[:, 0:1]), k == 0, False)
                fw.mm((bO, bO.t[:, 0:KL]), (PTs, PTs[:]), (kvp, kvp[:, r, :]), False, False)

            n_it = len(items) if SUB >= 2 else 0
            for k in range(n_it + 2):
                if k < n_it:
                    stage1(k)
                if 0 <= k - 1 < n_it:
                    stage2(k - 1)
                if 0 <= k - 2 < n_it:
                    stage3(k - 2)
            if SUB < 3:
                continue
            bS = fw.bank()
            fw.mm((bS, bS.t[0:8, 0:128]), (ckvT, ckvT[:, 0, cs8:cs8 + 8]), (qlatT, qlatT[:, 0, j, :]), True, False)
            fw.mm((bS, bS.t[0:8, 0:128]), (ckvT, ckvT[:, 1, cs8:cs8 + 8]), (qlatT, qlatT[:, 1, j, :]), False, False)
            fw.mm((bS, bS.t[0:8, 0:128]), (identb, identb[:, 0:8]), (cbf_sb, cbf_sb[:, 128 + 2048:128 + 2048 + 128]), False, False)
            fw.mm((bS, bS.t[0:8, 0:128]), (kpeT, kpeT[64:96, cs8:cs8 + 8]), (qpeT, qpeT[64:96, j, :]), False, True)
            PTs = PTs_ring.next()
            fw.act((PTs, PTs[0:8, :]), (bS, bS.t[0:8, 0:128]), AF.Exp)
            if SUB < 4:
                continue
            ckn = ckn_ring.next()
            fw.dma("pool", (ckn, ckn[:, 0:KL]), (R["kvlat"], kvlat[cs8:cs8 + 8, :]))
            fw.mm((bO, bO.t[:, 0:KL + 1]), (PTs, PTs[0:8, :]), (ckn, ckn[0:8, 0:KL + 1]), False, True)
            if SUB < 5:
                continue
            ss_ = sst.next()
            fw.recip((ss_, ss_[:, 0:1]), (bO, bO.t[:, KL:KL + 1]))
            ol = ol_ring.next()
            fw.ts("dve", (ol, ol[:]), (bO, bO.t[:, 0:KL]), (ss_, ss_[:, 0:1]), ALU.mult)
            if "dbg_ol" in debug and j == 0:
                dol = nc.dram_tensor("dbg_ol", [128, KL], BF16, kind="ExternalOutput").ap()
                dq = nc.dram_tensor("dbg_q", [128, 2, NSAMP, 128], BF16, kind="ExternalOutput").ap()
                dqp = nc.dram_tensor("dbg_qp", [96, NSAMP, 128], BF16, kind="ExternalOutput").ap()
                dden = nc.dram_tensor("dbg_den", [128, 2], F32, kind="ExternalOutput").ap()
                fw.dma("sp", (R["yout"], dol), (ol, ol[:]), final=True)
                fw.dma("sp", (R["yout"], dq), (qlatT, qlatT[:]), final=True)
                fw.dma("sp", (R["yout"], dqp), (qpeT, qpeT[:]), final=True)
                fw.dma("sp", (R["yout"], dden), (ss_, ss_[:]), final=True)
            bt = fw.bank()
            bv = bt.t[:].bitcast(BF16)
            for kc in range(2):
                fw.tr((bt, bv[:, kc * 128:(kc + 1) * 128]), (ol, ol[:, kc * 128:(kc + 1) * 128]), (identb, identb[:]))
            olT = olT_ring.next()
            fw.cp("dve", (olT, olT[:].rearrange("p a b -> p (a b)")), (bt, bv[:, 0:256]))
            if SUB < 6:
                continue
            b2 = fw.bank()
            for h in range(MH):
                for kc in range(2):
                    fw.mm((b2, b2.t[0:VH, h * 8:(h + 1) * 8]), (wuv, wuv[:, kc, h * VH:(h + 1) * VH]), (olT, olT[:, kc, h * 8:(h + 1) * 8]), kc == 0, kc == 1)
            oS = oS_ring.next()
            fw.cp("dve", (oS, oS[:].rearrange("p a b -> p (a b)")), (b2, b2.t[0:VH, 0:128]))
            fw.dma("sp", (R["oT"], oT_d[:, :, cs8:cs8 + 8].rearrange("h v u -> v h u")), (oS, oS[:]))
        fw.reserve(0)
        fw.pop_scope()
        fw.pop_scope()
        fw.push_scope()
        alloc_norm_rings()
        wbo = fw.sb("wbo", [VH, MH, D], BF16)
        for hq in range(0, MH, 4):
            fw.dma("pool", (wbo, wbo[:, hq:hq + 4, :]), (R["w"], W["w_b_out"][hq * VH:(hq + 4) * VH, :].rearrange("(h v) n -> v h n", v=VH)))
        wo = fw.sb("wo", [128, 8, D], BF16)
        for kk in range(0, 8, 4):
            fw.dma("pool", (wo, wo[:, kk:kk + 4, :]), (R["w"], W["w_o"][kk * 128:(kk + 4) * 128, :].rearrange("(k p) n -> p k n", p=128)))
        gmix = bcast_row("gmix", D, D)
        oT_ring = Ring(fw, "oTr", 2, [VH, MH, 512], BF16)
        gb_ring = Ring(fw, "gbr", 2, [128, 8, 512], BF16)
        mA2_ring = Ring(fw, "mA2", 2, [128, 8, 512], BF16)
        mS_ring = Ring(fw, "mS", 2, [128, 8, 512], BF16)
        tm_ring = Ring(fw, "tmr", 2, [128, 512])
        for (t0, tn) in (tiles_of(0, U, 512) if ASTG >= 5 else []):
            oTt = oT_ring.next()
            for hq in range(0, MH, 8):
                fw.dma("sp", (oTt, oTt[:, hq:hq + 8, 0:tn]), (R["oT"], oT_d[hq:hq + 8, :, t0:t0 + tn].rearrange("h v u -> v h u")))
            gb = gb_ring.next()
            fw.dma("sp", (gb, gb[:, :, 0:tn]), (R["gT"], gT_d[8:16, :, t0:t0 + tn].rearrange("c p u -> p c u")))
            mA2 = mA2_ring.next()
            fw.dma("sp", (mA2, mA2[:, :, 0:tn]), (R["mT"], mT_d[:, :, t0:t0 + tn].rearrange("c p u -> p c u")))
            mS = mS_ring.next()
            for oc in range(8):
                b = fw.bank()
                for h in range(MH):
                    fw.mm((b, b.t[:, 0:tn]), (wbo, wbo[:, h, oc * 128:(oc + 1) * 128]), (oTt, oTt[:, h, 0:tn]), h == 0, h == MH - 1)
                tm = tm_ring.next()
                fw.tt("dve", (tm, tm[:, 0:tn]), (b, b.t[:, 0:tn]), (gb, gb[:, oc, 0:tn]), ALU.mult)
                fw.tt("pool", (mS, mS[:, oc, 0:tn]), (tm, tm[:, 0:tn]), (mA2, mA2[:, oc, 0:tn]), ALU.add)
            for (r0, rn) in tiles_of(t0, t0 + tn, 128):
                bks = [fw.bank(), fw.bank()]
                for half in range(2):
                    for k in range(8):
                        fw.mm((bks[half], bks[half].t[0:rn, :]), (mS, mS[:, k, r0 - t0:r0 - t0 + rn]), (wo, wo[:, k, half * 512:(half + 1) * 512]), k == 0, k == 7)
                postnorm_residual(bks, rn, gmix, res1, R["res1"], res2, R["res2"], r0)
        fw.pop_scope()

    PH = build.phases
    if "ffn1" in PH:
        ffn("ffn1", 0, 0, xin, R["xin"], res1, R["res1"], final=("proj" not in PH))
    if "proj" in PH:
        mix_proj()
    if "ssd" in PH:
        ssd()
    if "attn" in PH:
        attn()
    if "ffn2" in PH:
        src2, reg2 = (res2, R["res2"]) if "attn" in PH else (res1, R["res1"])
        ffn("ffn2", 2, 2 * D, src2, reg2, yout, R["yout"], final=True)
    fw.emit()
    return nc


build.phases = ("ffn1", "proj", "ssd", "attn", "ffn2")


def host_consts(SEQ, PAST):
    U = 3 + NMETA + SEQ + NSAMP * 11
    ident = np.eye(128, dtype=np.float32)
    t = np.arange(128)
    tri = (t[:, None] <= t[None, :]).astype(np.float32)
    tris = (t[:, None] > t[None, :]).astype(np.float32)
    ones = np.ones((128, 128), np.float32)
    sel = np.zeros((128, NH, 128), np.float32)
    for h in range(NH):
        sel[h, h, :] = 1.0
    cf32 = np.concatenate([ident, tri, tris, ones, sel.reshape(128, NH * 128), (t % 32).astype(np.float32)[:, None]], axis=1)
    neg = np.where(t[None, :] < t[:, None], NEGV, 0.0).astype(np.float32)
    q = np.arange(512)
    negm = [np.where(q[None, :] < 128 * r + t[:, None], NEGV, 0.0).astype(np.float32) for r in range(4)]
    neg8 = np.zeros((128, 128), np.float32)
    for s_ in range(8):
        for h in range(MH):
            for tq in range(8):
                if tq < s_:
                    neg8[s_, h * 8 + tq] = NEGV
    cbf = np.concatenate([neg] + negm + [neg8], axis=1)
    pos = np.zeros(U, np.float32)
    pos[3:3 + NMETA + SEQ] = np.arange(NMETA + SEQ)
    for j in range(NSAMP):
        b0 = 3 + NMETA + SEQ + 11 * j
        pos[b0 + 3:b0 + 11] = PAST + np.arange(8)
    inv = (np.float32(10000.0) ** (-np.arange(0, ROPE, 2, dtype=np.float32) / np.float32(ROPE))).astype(np.float32)
    ang = pos[:, None].astype(np.float32) * inv[None, :]
    cos = np.cos(ang).astype(np.float32)
    sin = np.sin(ang).astype(np.float32)
    csU = np.stack([np.concatenate([cos, cos], 1), np.concatenate([-sin, sin], 1)], axis=1)
    cs96 = np.zeros((96, 2, U), np.float32)
    cs96[0:64, 0, :] = SCALE
    cs96[64:80, 0, :] = cos.T * SCALE
    cs96[80:96, 0, :] = cos.T * SCALE
    cs96[64:80, 1, :] = -sin.T * SCALE
    cs96[80:96, 1, :] = sin.T * SCALE
    sel3 = np.zeros((96, NH, 128), np.float32)
    for h in range(NH):
        for part in range(3):
            sel3[32 * part + h, h, :] = 1.0
    return dict(cf32=cf32, cbf=cbf, csU=np.ascontiguousarray(csU), cs96=cs96, sel3=sel3.reshape(96, NH * 128))


def swap_rope_cols(w, head_w, rope_off):
    w2 = w.copy()
    n = w.shape[1] // head_w
    for h in range(n):
        a = h * head_w + rope_off
        w2[:, a:a + 16] = w[:, a + 16:a + 32]
        w2[:, a + 16:a + 32] = w[:, a:a + 16]
    return w2


def host_shared(inp, SEQ, PAST):
    sh = host_consts(SEQ, PAST)
    for nm in ["ffn1_w_gate", "ffn1_w_up", "ffn1_w_down", "ffn2_w_gate", "ffn2_w_up", "ffn2_w_down", "w_in", "w_q_b",
               "w_a_out", "w_b_out", "w_o"]:
        sh[nm] = np.ascontiguousarray(inp[nm][0])
    w_in = inp["w_in"][0]
    kpe0 = DI + CD + NH + QL + KL
    sh["w_kpe_sw"] = np.ascontiguousarray(np.concatenate([w_in[:, kpe0 + 16:kpe0 + 32], w_in[:, kpe0:kpe0 + 16]], axis=1))
    sh["w_q_b_sw"] = swap_rope_cols(inp["w_q_b"][0], 96, 64)
    w_uk = inp["w_uk"][0]
    sh["w_uk"] = np.ascontiguousarray(w_uk.reshape(KL, MH * NOPE))
    sh["w_ukT"] = np.ascontiguousarray(w_uk.transpose(2, 1, 0).reshape(NOPE, MH * KL))
    sh["w_uv"] = np.ascontiguousarray(inp["w_uv"][0].reshape(KL, MH * VH))
    gc = np.stack([inp[k][0].reshape(8, 128).T for k in ["ffn1_pre_g", "mix_pre_g", "ffn2_pre_g"]], axis=1)
    sh["gcols"] = np.ascontiguousarray(gc.astype(np.float32))
    sh["grows"] = np.concatenate([inp[k][0].reshape(-1) for k in
                                  ["ffn1_post_g", "mix_post_g", "ffn2_post_g", "ssm_norm_g", "q_a_norm_g", "kv_a_norm_g",
                                   "dt_bias", "a_log", "d_skip"]]).astype(np.float32)[None, :]
    cw = inp["conv_w"][0]
    sh["convw"] = np.ascontiguousarray(cw.reshape(4, 24, 128).transpose(2, 1, 0).reshape(128, 96))
    sh["convb"] = np.ascontiguousarray(inp["conv_b"][0].reshape(24, 128).T)
    sh["gncol"] = np.ascontiguousarray(inp["ssm_norm_g"][0].reshape(16, 128).T)
    npool = inp["cache_kv_latent"].shape[1]
    sh["cache_kv"] = inp["cache_kv_latent"][0].reshape(npool * 32, 4 * KL)
    sh["cache_kr"] = inp["cache_k_rope"][0].reshape(npool * 32, 4 * ROPE)
    return sh


def host_core(inp, sh, b, SEQ):
    U = 3 + NMETA + SEQ + NSAMP * 11
    m = dict(sh)
    xin = np.zeros((U, D), np.float32)
    xin[3:3 + NMETA] = inp["meta_tokens"]
    xin[3 + NMETA:3 + NMETA + SEQ] = inp["x_prompt"][b]
    stc = np.zeros((NSAMP * 3, CD), np.float32)
    for j in range(NSAMP):
        b0 = 3 + NMETA + SEQ + 11 * j
        xin[b0 + 3:b0 + 11] = inp["x_sample"][NSAMP * b + j]
        stc[3 * j:3 * j + 3] = inp["state_conv"][0, NSAMP * b + j]
    m["xin"] = xin
    m["st_conv"] = stc
    m["st_ssm"] = np.ascontiguousarray(inp["state_ssm"][0, NSAMP * b:NSAMP * b + NSAMP].reshape(NSAMP, DI, DS))
    m["ptab"] = np.ascontiguousarray(inp["page_table"][NSAMP * b:NSAMP * b + NSAMP].reshape(1, -1)).astype(np.int32)
    return m


SEQ_FULL = 2048
NPG_FULL = 128


def kernel(**inputs):
    inp = {k: np.asarray(v) for k, v in inputs.items()}
    nb = inp["x_prompt"].shape[0]
    seq = inp["x_prompt"].shape[1]
    npg = inp["page_table"].shape[1]
    npool = inp["cache_kv_latent"].shape[1]
    past = npg * inp["cache_kv_latent"].shape[2]
    nsamp_total = inp["x_sample"].shape[0]
    U = 3 + NMETA + seq + NSAMP * 11
    PL = NMETA + seq
    nc = build(seq, npg, npool)
    sh = host_shared(inp, seq, past)
    in_maps = [host_core(inp, sh, b, seq) for b in range(nb)]
    res = run_bass_kernel_spmd(nc, in_maps, core_ids=list(range(nb)))
    y_prompt = np.zeros((nb, seq, D), np.float32)
    y_sample = np.zeros((nsamp_total, TS, D), np.float32)
    kvp = np.zeros((1, nb, PL, KL), np.float32)
    krp = np.zeros((1, nb, PL, ROPE), np.float32)
    ssp = np.zeros((1, nb, NH, HD, DS), np.float32)
    cvp = np.zeros((1, nb, CK - 1, CD), np.float32)
    kvs = np.zeros((1, nsamp_total, TS, KL), np.float32)
    krs = np.zeros((1, nsamp_total, TS, ROPE), np.float32)
    sss = np.zeros((1, nsamp_total, NH, HD, DS), np.float32)
    cvs = np.zeros((1, nsamp_total, CK - 1, CD), np.float32)
    for b in range(nb):
        r = res.results[b]
        yo = np.asarray(r["yout"]); kv = np.asarray(r["kvlat"]); kr = np.asarray(r["krope"])
        so = np.asarray(r["ssm_o"]); co = np.asarray(r["conv_o"])
        y_prompt[b] = yo[3 + NMETA:3 + PL]
        kvp[0, b] = kv[3:3 + PL]
        krp[0, b] = kr[3:3 + PL]
        ssp[0, b] = so[0].reshape(NH, HD, DS)
        cvp[0, b] = co[0:3]
        for j in range(NSAMP):
            sb = 3 + PL + 11 * j
            i = NSAMP * b + j
            y_sample[i] = yo[sb + 3:sb + 11]
            kvs[0, i] = kv[sb + 3:sb + 11]
            krs[0, i] = kr[sb + 3:sb + 11]
            sss[0, i] = so[1 + j].reshape(NH, HD, DS)
            cvs[0, i] = co[11 * j + 11:11 * j + 14]
    return (y_prompt, y_sample, kvp, krp, ssp, cvp, kvs, krs, sss, cvs)
```

```python
import os
import numpy as np
import concourse.bass as bass
import concourse.mybir as mybir
from concourse.bass_utils import run_bass_kernel_spmd

F32 = mybir.dt.float32
BF16 = mybir.dt.bfloat16
I32 = mybir.dt.int32
AF = mybir.ActivationFunctionType
ALU = mybir.AluOpType
AX = mybir.AxisListType

N_DMA_SEMS = 24


class TT:
    __slots__ = ("name", "t", "last_w", "reads", "excl")

    def __init__(self, name, t=None):
        self.name = name
        self.t = t
        self.last_w = None
        self.reads = []
        self.excl = False

    def __getitem__(self, k):
        return self.t[k]


class _Op:
    __slots__ = ("fn", "waits", "tok", "dma", "needed")

    def __init__(self, fn, waits, tok, dma):
        self.fn = fn
        self.waits = waits
        self.tok = tok
        self.dma = dma
        self.needed = False


class FW:
    COMPUTE = ("pe", "act", "dve", "pool")

    def __init__(self, nc):
        self.nc = nc
        self.q = {k: [] for k in ("pe", "act", "dve", "pool", "sp")}
        self.cnt = {k: 0 for k in self.COMPUTE}
        self.dma_rr = {"sp": 0, "pool": 0, "act": 0}
        self.dma_cum = {}
        self.dma_last = {}
        self.same_engine_sync = (os.environ.get("SES", "1") == "1")
        self.raw_only = (os.environ.get("RAWONLY", "1") == "1")
        self.final_tokens = []
        self.optok = {}
        self.banks = None
        self.bank_i = 0
        self.scopes = []
        self.pending = {}
        self.reserved = 0

    def sb(self, name, shape, dt=F32):
        if self.scopes:
            return TT(name, self.scopes[-1].enter_context(self.nc.sbuf_tensor(name, list(shape), dt)))
        return TT(name, self.nc.alloc_sbuf_tensor(name, list(shape), dt))

    def push_scope(self):
        import contextlib
        self.scopes.append(contextlib.ExitStack())

    def pop_scope(self):
        self.barrier()
        self.scopes.pop().close()

    def barrier(self):
        w = {}
        for e in self.COMPUTE:
            if self.cnt[e] > 0:
                w[e] = self.cnt[e]
        for key, tok in self.dma_last.items():
            w[key] = tok[1]
        for e in self.q:
            self.pending[e] = dict(w)

    def region(self, name):
        return TT(name, None)

    def init_banks(self):
        self.banks = [TT("bank%d" % i, self.nc.alloc_psum_tensor("bank%d" % i, [128, 512], F32)) for i in range(8)]
        for b in self.banks:
            b.excl = True

    def bank(self):
        n = 8 - self.reserved
        self.bank_i = self.bank_i % n
        b = self.banks[self.bank_i]
        self.bank_i = (self.bank_i + 1) % n
        return b

    def reserve(self, n):
        self.reserved = n
        return [self.banks[8 - 1 - i] for i in range(n)]

    def op(self, eng, fn, r=(), w=(), dma=False, final=False):
        deps = []
        deps2 = []
        for t in r:
            if t.last_w is not None:
                deps.append(t.last_w)
            if t.excl:
                deps.extend(x for x in t.reads if x[0] != eng)
        for t in w:
            if t.last_w is not None:
                deps2.append(t.last_w)
            deps2.extend(t.reads)
        if dma or not self.raw_only:
            deps.extend(deps2)
        else:
            deps.extend(x for x in deps2 if x[0] != eng)
        if dma:
            i = self.dma_rr[eng]
            self.dma_rr[eng] = (i + 1) % N_DMA_SEMS
            key = ("dma", eng, i)
            prev = self.dma_last.get(key)
            if prev is not None:
                deps.append(prev)
            val = self.dma_cum.get(key, 0) + 16
            self.dma_cum[key] = val
            tok = (key, val)
            self.dma_last[key] = tok
        else:
            self.cnt[eng] += 1
            tok = (eng, self.cnt[eng])
        waits = {}
        if self.pending.get(eng):
            for k, v in self.pending[eng].items():
                if k == eng:
                    continue
                waits[k] = v
            self.pending[eng] = None
        for (k, v) in deps:
            if k == eng and not dma:
                if eng == "pe" or not self.same_engine_sync:
                    continue
            if waits.get(k, 0) < v:
                waits[k] = v
        o = _Op(fn, waits, tok, dma)
        self.q[eng].append(o)
        if not dma:
            self.optok[tok] = o
        for t in r:
            t.reads.append(tok)
        for t in w:
            t.last_w = tok
            t.reads = []
        if final:
            self.final_tokens.append(tok)
        return tok

    def mm(self, out, lhsT, rhs, start, stop):
        self.op("pe", lambda e: e.matmul(out=out[1], lhsT=lhsT[1], rhs=rhs[1], start=start, stop=stop),
                r=[lhsT[0], rhs[0]], w=[out[0]])

    def tr(self, out, in_, ident):
        self.op("pe", lambda e: e.transpose(out=out[1], in_=in_[1], identity=ident[1]), r=[in_[0], ident[0]], w=[out[0]])

    def act(self, out, in_, func, bias=None, scale=None, accum=None, eng="act"):
        r = [in_[0]]
        kw = {}
        if bias is not None:
            if isinstance(bias, tuple):
                r.append(bias[0])
                kw["bias"] = bias[1]
            else:
                kw["bias"] = bias
        if scale is not None:
            if isinstance(scale, tuple):
                r.append(scale[0])
                kw["scale"] = scale[1]
            else:
                kw["scale"] = scale
        w = [out[0]]
        if accum is not None:
            w.append(accum[0])
            kw["accum_out"] = accum[1]
        self.op(eng, lambda e: e.activation(out=out[1], in_=in_[1], func=func, **kw), r=r, w=w)

    def ts(self, eng, out, in0, s1, op0, s2=None, op1=None):
        r = [in0[0]]
        a1 = s1
        if isinstance(s1, tuple):
            r.append(s1[0])
            a1 = s1[1]
        a2 = s2
        if isinstance(s2, tuple):
            r.append(s2[0])
            a2 = s2[1]
        if op1 is None:
            self.op(eng, lambda e: e.tensor_scalar(out=out[1], in0=in0[1], scalar1=a1, scalar2=None, op0=op0), r=r, w=[out[0]])
        else:
            self.op(eng, lambda e: e.tensor_scalar(out=out[1], in0=in0[1], scalar1=a1, scalar2=a2, op0=op0, op1=op1), r=r, w=[out[0]])

    def tt(self, eng, out, in0, in1, op):
        self.op(eng, lambda e: e.tensor_tensor(out=out[1], in0=in0[1], in1=in1[1], op=op), r=[in0[0], in1[0]], w=[out[0]])

    def stt(self, out, in0, scalar, in1, op0, op1):
        r = [in0[0], in1[0]]
        a = scalar
        if isinstance(scalar, tuple):
            r.append(scalar[0])
            a = scalar[1]
        self.op("dve", lambda e: e.scalar_tensor_tensor(out=out[1], in0=in0[1], scalar=a, in1=in1[1], op0=op0, op1=op1), r=r, w=[out[0]])

    def cp(self, eng, out, in_):
        if eng == "act":
            self.op(eng, lambda e: e.copy(out=out[1], in_=in_[1]), r=[in_[0]], w=[out[0]])
        else:
            self.op(eng, lambda e: e.tensor_copy(out=out[1], in_=in_[1]), r=[in_[0]], w=[out[0]])

    def memset(self, eng, out, val):
        self.op(eng, lambda e: e.memset(out[1], val), w=[out[0]])

    def recip(self, out, in_):
        self.op("dve", lambda e: e.reciprocal(out=out[1], in_=in_[1]), r=[in_[0]], w=[out[0]])

    def dma(self, q, out, in_, final=False, slow=False):
        if slow:
            self.op(q, lambda e: e.dma_start(out=out[1], in_=in_[1], allow_slow_non_contiguous=True), r=[in_[0]], w=[out[0]], dma=True, final=final)
        else:
            self.op(q, lambda e: e.dma_start(out=out[1], in_=in_[1]), r=[in_[0]], w=[out[0]], dma=True, final=final)

    def gather(self, out, in_ap, in_reg, idx):
        self.op("pool", lambda e: e.indirect_dma_start(out=out[1], out_offset=None, in_=in_ap,
                                                         in_offset=bass.IndirectOffsetOnAxis(ap=idx[1], axis=0)),
                r=[in_reg, idx[0]], w=[out[0]], dma=True)

    def emit(self):
        nc = self.nc
        for qn, ops in self.q.items():
            for o in ops:
                for (k, v) in o.waits.items():
                    if isinstance(k, str):
                        self.optok[(k, v)].needed = True
        for tok in self.final_tokens:
            if isinstance(tok[0], str):
                self.optok[tok].needed = True
        remap = {}
        for e in self.COMPUTE:
            c = 0
            for o in self.q[e]:
                if o.dma:
                    continue
                if o.needed:
                    c += 1
                remap[o.tok] = c
        sems = {e: nc.alloc_semaphore("s_" + e) for e in self.COMPUTE}
        dsems = {}
        for key in self.dma_cum:
            dsems[key] = nc.alloc_semaphore("d_%s_%d" % (key[1], key[2]))
        finals = list(self.final_tokens)

        def run_queue(qn, eng):
            known = {}
            for o in self.q[qn]:
                for (k, v) in o.waits.items():
                    if isinstance(k, str):
                        v2 = remap[(k, v)]
                        s = sems[k]
                    else:
                        v2 = v
                        s = dsems[k]
                    if known.get(k, 0) >= v2:
                        continue
                    known[k] = v2
                    eng.wait_ge(s, v2)
                ins = o.fn(eng)
                if o.dma:
                    ins.then_inc(dsems[o.tok[0]], 16)
                elif o.needed:
                    ins.then_inc(sems[qn], 1)
            if qn == "sp":
                for tok in finals:
                    k, v = tok
                    if isinstance(k, str):
                        eng.wait_ge(sems[k], remap[tok])
                    else:
                        eng.wait_ge(dsems[k], v)

        with nc.Block() as block:
            @block.sync
            def _(e):
                run_queue("sp", e)

            @block.tensor
            def _(e):
                run_queue("pe", e)

            @block.scalar
            def _(e):
                run_queue("act", e)

            @block.vector
            def _(e):
                run_queue("dve", e)

            @block.gpsimd
            def _(e):
                run_queue("pool", e)


class Ring:
    def __init__(self, fw, name, n, shape, dt=F32):
        self.tiles = [fw.sb("%s%d" % (name, i), shape, dt) for i in range(n)]
        self.i = 0

    def next(self):
        t = self.tiles[self.i]
        self.i = (self.i + 1) % len(self.tiles)
        return t


D = 1024
DFF = 2816
NMETA = 16
DI = 2048
NH = 32
HD = 64
NG = 4
DS = 128
CK = 4
CD = 3072
MH = 16
QL = 512
KL = 256
NOPE = 64
ROPE = 32
VH = 64
EPS = 1e-6
SCALE = (NOPE + ROPE) ** -0.5
NIN = 8000
NSAMP = 4
TS = 8
NEGV = -30000.0


def tiles_of(a, b, step):
    out = []
    c = a
    while c < b:
        out.append((c, min(step, b - c)))
        c += step
    return out


def build(SEQ, NPG, NPOOL, debug=()):
    U = 3 + NMETA + SEQ + NSAMP * 11
    P0 = 3
    PL = NMETA + SEQ
    SB0 = 3 + PL
    nc = bass.Bass("TRN2", target_bir_lowering=False)
    fw = FW(nc)
    fw.init_banks()

    def din(name, shape, dt=F32):
        return nc.dram_tensor(name, list(shape), dt, kind="ExternalInput").ap()

    def dout(name, shape, dt=F32):
        return nc.dram_tensor(name, list(shape), dt, kind="ExternalOutput").ap()

    def dscr(name, shape, dt=F32):
        kind = "ExternalOutput" if name in debug else "Internal"
        return nc.dram_tensor(name, list(shape), dt, kind=kind).ap()

    xin = din("xin", [U, D])
    W = {}
    for nm, shp in [("ffn1_w_gate", [D, DFF]), ("ffn1_w_up", [D, DFF]), ("ffn1_w_down", [DFF, D]),
                    ("ffn2_w_gate", [D, DFF]), ("ffn2_w_up", [D, DFF]), ("ffn2_w_down", [DFF, D]),
                    ("w_in", [D, NIN]), ("w_kpe_sw", [D, ROPE]), ("w_q_b", [QL, MH * 96]), ("w_q_b_sw", [QL, MH * 96]),
                    ("w_uk", [KL, MH * NOPE]), ("w_ukT", [NOPE, MH * KL]), ("w_uv", [KL, MH * VH]),
                    ("w_a_out", [DI, D]), ("w_b_out", [MH * VH, D]), ("w_o", [D, D])]:
        W[nm] = din(nm, shp)
    gcols = din("gcols", [128, 3, 8])
    grows = din("grows", [1, 3 * D + DI + QL + KL + 3 * NH])
    gncol = din("gncol", [128, 16])
    convw = din("convw", [128, 24 * 4])
    convb = din("convb", [128, 24])
    cf32 = din("cf32", [128, 128 * 4 + NH * 128 + 1])
    cbf = din("cbf", [128, 128 + 4 * 512 + 128])
    sel3_d = din("sel3", [96, NH * 128])
    cs96 = din("cs96", [96, 2, U])
    csU = din("csU", [U, 2, 32])
    cache_kv = din("cache_kv", [NPOOL * 32, 4 * KL])
    cache_kr = din("cache_kr", [NPOOL * 32, 4 * ROPE])
    ptab = din("ptab", [1, NSAMP * NPG], I32)
    st_ssm = din("st_ssm", [NSAMP, DI, DS])
    st_conv = din("st_conv", [NSAMP * 3, CD])

    yout = dout("yout", [U, D])
    kvlat = dout("kvlat", [U, KL])
    krope = dout("krope", [U, ROPE])
    ssm_o = dout("ssm_o", [1 + NSAMP, DI, DS])
    conv_o = dout("conv_o", [47, CD])

    res1 = dscr("res1", [U, D])
    res2 = dscr("res2", [U, D])
    zs_d = dscr("zs_d", [U, DI], BF16)
    xc_d = dscr("xc_d", [24, 128, U], BF16)
    dtr_d = dscr("dtr_d", [U, NH])
    gT_d = dscr("gT_d", [16, 128, U], BF16)
    mT_d = dscr("mT_d", [8, 128, U], BF16)
    oT_d = dscr("oT_d", [MH, 64, U], BF16)
    R = {k: fw.region(k) for k in ["xin", "w", "res1", "res2", "zs", "xc", "dtr", "gT", "mT", "oT", "yout", "kvlat", "krope", "ssm_o", "conv_o", "consts", "cache"]}

    identf = fw.sb("identf", [128, 128])
    tri = fw.sb("tri", [128, 128])
    tris = fw.sb("tris", [128, 128])
    onesf = fw.sb("onesf", [128, 128])
    identb = fw.sb("identb", [128, 128], BF16)
    cbf_sb = fw.sb("cbf_sb", [128, 128 + 4 * 512 + 128], BF16)
    gcol_sb = fw.sb("gcol_sb", [128, 3, 8])
    eps_sb = fw.sb("eps_sb", [128, 1])
    fw.dma("sp", (identf, identf[:]), (R["consts"], cf32[:, 0:128]))
    fw.dma("sp", (tri, tri[:]), (R["consts"], cf32[:, 128:256]))
    fw.dma("sp", (tris, tris[:]), (R["consts"], cf32[:, 256:384]))
    fw.dma("sp", (onesf, onesf[:]), (R["consts"], cf32[:, 384:512]))
    fw.dma("pool", (cbf_sb, cbf_sb[:]), (R["consts"], cbf))
    fw.dma("sp", (gcol_sb, gcol_sb[:]), (R["consts"], gcols))
    fw.cp("dve", (identb, identb[:]), (identf, identf[:]))
    fw.memset("dve", (eps_sb, eps_sb[:]), EPS)
    negsl = cbf_sb[:, 0:128]
    qaT = fw.sb("qaT", [128, 4, U], BF16)
    ckvT = fw.sb("ckvT", [128, 2, U], BF16)
    kpeT = fw.sb("kpeT", [96, U], BF16)

    def negm(r):
        return cbf_sb[:, 128 + r * 512:128 + (r + 1) * 512]
    neg8 = cbf_sb[0:8, 128 + 2048:128 + 2048 + 128]

    def bcast_row(name, off, n, dt=F32):
        t = fw.sb(name, [128, n], dt)
        fw.dma("sp" if dt == F32 else "pool", (t, t[:]), (R["consts"], grows[0, off:off + n].partition_broadcast(128)))
        return t

    rowtiles = tiles_of(0, U, 128)

    def rstd_from(ss_tt, ss_ap, n, rows, out_tt, out_ap):
        fw.act((out_tt, out_ap), (ss_tt, ss_ap), AF.Sqrt, bias=(eps_sb, eps_sb[0:rows, :]), scale=1.0 / n)
        fw.recip((out_tt, out_ap), (out_tt, out_ap))

    class _H:
        pass
    NR = _H()
    st_ring = Ring(fw, "stat", 4, [128, 8])
    nr_cnt = [0]

    def alloc_norm_rings():
        nr_cnt[0] += 1
        NR.xrow = Ring(fw, "xrow%d_" % nr_cnt[0], 2, [128, D])
        NR.junk = Ring(fw, "junk%d_" % nr_cnt[0], 2, [128, D])
        NR.xn = Ring(fw, "xnb%d_" % nr_cnt[0], 2, [128, D], BF16)

    def norm_transpose(src_ap, src_reg, which, xT, c_lo, c_hi):
        for (r0, rn) in tiles_of(c_lo, c_hi, 128):
            xr = NR.xrow.next()
            fw.dma("sp", (xr, xr[0:rn, :]), (src_reg, src_ap[r0:r0 + rn, :]))
            jk = NR.junk.next()
            st = st_ring.next()
            fw.act((jk, jk[0:rn, :]), (xr, xr[0:rn, :]), AF.Square, accum=(st, st[0:rn, 0:1]))
            rstd_from(st, st[0:rn, 0:1], D, rn, st, st[0:rn, 1:2])
            xn = NR.xn.next()
            fw.ts("dve", (xn, xn[0:rn, :]), (xr, xr[0:rn, :]), (st, st[0:rn, 1:2]), ALU.mult)
            for half in range(2):
                bk = fw.bank()
                bv = bk.t[:].bitcast(BF16)
                for kk in range(4):
                    k = half * 4 + kk
                    fw.tr((bk, bv[:, kk * 128:kk * 128 + rn]), (xn, xn[0:rn, k * 128:(k + 1) * 128]), (identb, identb[0:rn, 0:rn]))
                for kk in range(4):
                    k = half * 4 + kk
                    eng = "dve" if kk % 2 == 0 else "pool"
                    if eng == "pool":
                        fw.act((xT, xT[:, k, r0 - c_lo:r0 - c_lo + rn]), (bk, bv[:, kk * 128:kk * 128 + rn]), AF.Copy,
                               scale=(gcol_sb, gcol_sb[:, which, k:k + 1]))
                    else:
                        fw.ts("dve", (xT, xT[:, k, r0 - c_lo:r0 - c_lo + rn]), (bk, bv[:, kk * 128:kk * 128 + rn]),
                              (gcol_sb, gcol_sb[:, which, k:k + 1]), ALU.mult)

    def postnorm_residual(banks, rn, g_bc, res_ap, res_reg, dst_ap, dst_reg, r0, final=False):
        st = st_ring.next()
        for half in range(2):
            jk = NR.junk.next()
            fw.act((jk, jk[0:rn, 0:512]), (banks[half], banks[half].t[0:rn, :]), AF.Square, accum=(st, st[0:rn, half:half + 1]))
        fw.tt("dve", (st, st[0:rn, 2:3]), (st, st[0:rn, 0:1]), (st, st[0:rn, 1:2]), ALU.add)
        rstd_from(st, st[0:rn, 2:3], D, rn, st, st[0:rn, 3:4])
        xr = NR.xrow.next()
        fw.dma("sp", (xr, xr[0:rn, :]), (res_reg, res_ap[r0:r0 + rn, :]))
        o = NR.junk.next()
        for half in range(2):
            fw.stt((o, o[0:rn, half * 512:(half + 1) * 512]), (banks[half], banks[half].t[0:rn, :]), (st, st[0:rn, 3:4]),
                   (g_bc, g_bc[0:rn, half * 512:(half + 1) * 512]), ALU.mult, ALU.mult)
        fw.tt("pool", (o, o[0:rn, :]), (o, o[0:rn, :]), (xr, xr[0:rn, :]), ALU.add)
        fw.dma("sp", (dst_reg, dst_ap[r0:r0 + rn, :]), (o, o[0:rn, :]), final=final)

    wring_h = [None]

    def load_w(w_ap, c0, cn, kchunks=8):
        t = wring_h[0].next()
        fw.dma("pool", (t, t[:, 0:kchunks, 0:cn]), (R["w"], w_ap[:, c0:c0 + cn].rearrange("(k p) n -> p k n", p=128)))
        return t

    def ffn(pfx, which_pre, gpost_off, src_ap, src_reg, dst_ap, dst_reg, final):
        fw.push_scope()
        alloc_norm_rings()
        wring_h[0] = Ring(fw, "wst_" + pfx, 4, [128, 8, 512], BF16)
        gpost = bcast_row("gpost_" + pfx, gpost_off, D)
        fw.ts("dve", (gpost, gpost[:]), (gpost, gpost[:]), 0.5, ALU.mult)
        supers = tiles_of(0, U, 768)
        supers = [(a, a + n) for (a, n) in supers]
        maxc = max(b - a for a, b in supers)
        xT = fw.sb("xT_" + pfx, [128, 8, maxc], BF16)
        hT = fw.sb("hT_" + pfx, [128, 22, maxc], BF16)
        wd = fw.sb("wd_" + pfx, [128, 22, D], BF16)
        for kk in range(0, 22, 2):
            fw.dma("pool", (wd, wd[:, kk:kk + 2, :]), (R["w"], W[pfx + "_w_down"][kk * 128:(kk + 2) * 128, :].rearrange("(k p) n -> p k n", p=128)))
        sg_ring = Ring(fw, "sg_" + pfx, 2, [128, 512])
        for (c_lo, c_hi) in supers:
            norm_transpose(src_ap, src_reg, which_pre, xT, c_lo, c_hi)
            for (g0, gn) in tiles_of(0, DFF, 512):
                wg = load_w(W[pfx + "_w_gate"], g0, gn)
                wu = load_w(W[pfx + "_w_up"], g0, gn)
                for jj in range(gn // 128):
                    j = g0 // 128 + jj
                    for (t0, tn) in tiles_of(0, c_hi - c_lo, 512):
                        bg = fw.bank()
                        bu = fw.bank()
                        for k in range(8):
                            fw.mm((bg, bg.t[:, 0:tn]), (wg, wg[:, k, jj * 128:(jj + 1) * 128]), (xT, xT[:, k, t0:t0 + tn]), k == 0, k == 7)
                        for k in range(8):
                            fw.mm((bu, bu.t[:, 0:tn]), (wu, wu[:, k, jj * 128:(jj + 1) * 128]), (xT, xT[:, k, t0:t0 + tn]), k == 0, k == 7)
                        sg = sg_ring.next()
                        fw.act((sg, sg[:, 0:tn]), (bg, bg.t[:, 0:tn]), AF.Silu)
                        fw.tt("dve", (hT, hT[:, j, t0:t0 + tn]), (sg, sg[:, 0:tn]), (bu, bu.t[:, 0:tn]), ALU.mult)
            for (r0, rn) in tiles_of(c_lo, c_hi, 128):
                bks = [fw.bank(), fw.bank()]
                for half in range(2):
                    for j in range(22):
                        fw.mm((bks[half], bks[half].t[0:rn, :]), (hT, hT[:, j, r0 - c_lo:r0 - c_lo + rn]),
                              (wd, wd[:, j, half * 512:(half + 1) * 512]), j == 0, j == 21)
                postnorm_residual(bks, rn, gpost, src_ap, src_reg, dst_ap, dst_reg, r0, final=final)
        fw.pop_scope()


    OFF_Z, OFF_X, OFF_DT, OFF_QA, OFF_KV, OFF_KPE, OFF_GA = 0, DI, DI + CD, DI + CD + NH, DI + CD + NH + QL, DI + CD + NH + QL + KL, DI + CD + NH + QL + KL + ROPE
    GOFF = {"ssm": 3 * D, "qa": 3 * D + DI, "kv": 3 * D + DI + QL, "dtb": 3 * D + DI + QL + KL, "alog": 3 * D + DI + QL + KL + NH, "dsk": 3 * D + DI + QL + KL + 2 * NH}
    SB = [SB0 + 11 * j for j in range(NSAMP)]

    def mix_proj():
        fw.push_scope()
        alloc_norm_rings()
        wring_h[0] = Ring(fw, "wst_mp", 4, [128, 8, 512], BF16)
        xT = fw.sb("xTm", [128, 8, U], BF16)
        norm_transpose(res1, R["res1"], 1, xT, 0, U)
        o512 = Ring(fw, "o512", 3, [128, 512], BF16)
        for (g0, gn) in tiles_of(OFF_Z, OFF_X, 512):
            wt = load_w(W["w_in"], g0, gn)
            for (r0, rn) in rowtiles:
                b = fw.bank()
                for k in range(8):
                    fw.mm((b, b.t[0:rn, :]), (xT, xT[:, k, r0:r0 + rn]), (wt, wt[:, k, :]), k == 0, k == 7)
                o = o512.next()
                fw.act((o, o[0:rn, :]), (b, b.t[0:rn, :]), AF.Silu)
                fw.dma("sp", (R["zs"], zs_d[r0:r0 + rn, g0:g0 + gn]), (o, o[0:rn, :]))
        convw_sb = fw.sb("convw_sb", [128, 96])
        convb_sb = fw.sb("convb_sb", [128, 24])
        fw.dma("sp", (convw_sb, convw_sb[:]), (R["consts"], convw))
        fw.dma("sp", (convb_sb, convb_sb[:]), (R["consts"], convb))
        diag = fw.sb("diag", [128, 24, 4, 128], BF16)
        for c in range(24):
            for k in range(4):
                fw.ts("dve", (diag, diag[:, c, k, :]), (identf, identf[:]), (convw_sb, convw_sb[:, c * 4 + k:c * 4 + k + 1]), ALU.mult)
        stc = fw.sb("stc", [12, CD])
        fw.dma("sp", (stc, stc[:]), (R["consts"], st_conv))
        scT = fw.sb("scT", [128, 24, 12], BF16)
        b = fw.bank()
        for c in range(24):
            fw.tr((b, b.t[:, c * 12:(c + 1) * 12]), (stc, stc[0:12, c * 128:(c + 1) * 128]), (identf, identf[0:12, 0:12]))
        fw.cp("dve", (scT, scT[:].rearrange("p a b -> p (a b)")), (b, b.t[:, 0:288]))
        xpre_ring = Ring(fw, "xpre", 3, [128, U], BF16)
        pend_conv = [None]
        cn_ring = Ring(fw, "cn", 2, [47, 512])
        for gi, (g0, gn) in enumerate(tiles_of(OFF_X, OFF_DT, 512)):
            wt = load_w(W["w_in"], g0, gn)
            b = fw.bank()
            for k in range(8):
                fw.mm((b, b.t[0:47, :]), (xT, xT[:, k, U - 47:U]), (wt, wt[:, k, :]), k == 0, k == 7)
            cn = cn_ring.next()
            fw.cp("dve", (cn, cn[:]), (b, b.t[0:47, :]))
            fw.dma("sp", (R["conv_o"], conv_o[:, gi * 512:(gi + 1) * 512]), (cn, cn[:]), final=True)
            for jj in range(4):
                c = gi * 4 + jj
                xp = xpre_ring.next()
                for ti, (t0, tn) in enumerate(tiles_of(0, U, 512)):
                    b = fw.bank()
                    for k in range(8):
                        fw.mm((b, b.t[:, 0:tn]), (wt, wt[:, k, jj * 128:(jj + 1) * 128]), (xT, xT[:, k, t0:t0 + tn]), k == 0, k == 7)
                    fw.cp("dve" if ti % 2 == 0 else "act", (xp, xp[:, t0:t0 + tn]), (b, b.t[:, 0:tn]))
                for j in range(NSAMP):
                    fw.cp("pool", (xp, xp[:, SB[j]:SB[j] + 3]), (scT, scT[:, c, 3 * j:3 * j + 3]))

                def conv(c=c, xp=xp):
                    for (t0, tn) in tiles_of(3, U, 512):
                        b = fw.bank()
                        for k in range(4):
                            fw.mm((b, b.t[:, 0:tn]), (diag, diag[:, c, k, :]), (xp, xp[:, t0 - 3 + k:t0 - 3 + k + tn]), k == 0, k == 3)
                        o = o512.next()
                        fw.act((o, o[:, 0:tn]), (b, b.t[:, 0:tn]), AF.Silu, bias=(convb_sb, convb_sb[:, c:c + 1]))
                        fw.dma("sp", (R["xc"], xc_d[c, :, t0:t0 + tn]), (o, o[:, 0:tn]))
                if pend_conv[0] is not None:
                    pend_conv[0]()
                pend_conv[0] = conv
        pend_conv[0]()
        wq = load_w(W["w_in"], OFF_QA, QL)
        wk = load_w(W["w_in"], OFF_KV, KL + ROPE)
        fw.dma("pool", (wk, wk[:, :, KL + ROPE:KL + 2 * ROPE]), (R["w"], W["w_kpe_sw"].rearrange("(k p) n -> p k n", p=128)))
        wdt = load_w(W["w_in"], OFF_DT, NH)
        gq = bcast_row("gq_bc", GOFF["qa"], QL)
        gkv = bcast_row("gkv_bc", GOFF["kv"], KL)
        cs_ring = Ring(fw, "csr", 2, [128, 2, 32])
        sm_ring = Ring(fw, "smr", 3, [128, 512])
        kpad = fw.sb("kpad", [128, 96], BF16)
        fw.memset("dve", (kpad, kpad[:]), 0.0)
        def small_mm(r0, rn):
            bq = fw.bank()
            bk2 = fw.bank()
            bd = fw.bank()
            for k in range(8):
                fw.mm((bq, bq.t[0:rn, :]), (xT, xT[:, k, r0:r0 + rn]), (wq, wq[:, k, 0:QL]), k == 0, k == 7)
            for k in range(8):
                fw.mm((bk2, bk2.t[0:rn, 0:320]), (xT, xT[:, k, r0:r0 + rn]), (wk, wk[:, k, 0:320]), k == 0, k == 7)
            for k in range(8):
                fw.mm((bd, bd.t[0:rn, 0:NH]), (xT, xT[:, k, r0:r0 + rn]), (wdt, wdt[:, k, 0:NH]), k == 0, k == 7)
            return bq, bk2, bd

        nxt = small_mm(*rowtiles[0])
        for ri, (r0, rn) in enumerate(rowtiles):
            bq, bk2, bd = nxt
            if ri + 1 < len(rowtiles):
                nxt = small_mm(*rowtiles[ri + 1])
            sm = sm_ring.next()
            fw.cp("dve", (sm, sm[0:rn, 0:NH]), (bd, bd.t[0:rn, 0:NH]))
            fw.dma("sp", (R["dtr"], dtr_d[r0:r0 + rn, :]), (sm, sm[0:rn, 0:NH]))
            st = st_ring.next()
            jk = NR.junk.next()
            fw.act((jk, jk[0:rn, 0:QL]), (bq, bq.t[0:rn, :]), AF.Square, accum=(st, st[0:rn, 0:1]))
            rstd_from(st, st[0:rn, 0:1], QL, rn, st, st[0:rn, 1:2])
            fw.act((jk, jk[0:rn, 512:512 + KL]), (bk2, bk2.t[0:rn, 0:KL]), AF.Square, accum=(st, st[0:rn, 2:3]))
            rstd_from(st, st[0:rn, 2:3], KL, rn, st, st[0:rn, 3:4])
            xn = NR.xn.next()
            fw.stt((xn, xn[0:rn, 0:QL]), (bq, bq.t[0:rn, :]), (st, st[0:rn, 1:2]), (gq, gq[0:rn, :]), ALU.mult, ALU.mult)
            ck = sm_ring.next()
            fw.stt((ck, ck[0:rn, 0:KL]), (bk2, bk2.t[0:rn, 0:KL]), (st, st[0:rn, 3:4]), (gkv, gkv[0:rn, :]), ALU.mult, ALU.mult)
            fw.dma("sp", (R["kvlat"], kvlat[r0:r0 + rn, :]), (ck, ck[0:rn, 0:KL]), final=True)
            fw.cp("pool", (xn, xn[0:rn, 512:512 + KL]), (ck, ck[0:rn, 0:KL]))
            cs = cs_ring.next()
            fw.dma("sp", (cs, cs[0:rn]), (R["consts"], csU[r0:r0 + rn]))
            kr = sm_ring.next()
            fw.tt("dve", (kr, kr[0:rn, 0:32]), (bk2, bk2.t[0:rn, KL:KL + 32]), (cs, cs[0:rn, 0, :]), ALU.mult)
            fw.tt("dve", (kr, kr[0:rn, 32:64]), (bk2, bk2.t[0:rn, KL + 32:KL + 64]), (cs, cs[0:rn, 1, :]), ALU.mult)
            fw.tt("dve", (kr, kr[0:rn, 0:32]), (kr, kr[0:rn, 0:32]), (kr, kr[0:rn, 32:64]), ALU.add)
            fw.dma("sp", (R["krope"], krope[r0:r0 + rn, :]), (kr, kr[0:rn, 0:32]), final=True)
            fw.cp("dve", (kpad, kpad[0:rn, 64:96]), (kr, kr[0:rn, 0:32]))
            bt = fw.bank()
            bv = bt.t[:].bitcast(BF16)
            for kk in range(4):
                fw.tr((bt, bv[:, kk * 128:kk * 128 + rn]), (xn, xn[0:rn, kk * 128:(kk + 1) * 128]), (identb, identb[0:rn, 0:rn]))
            for kk in range(2):
                fw.tr((bt, bv[:, (4 + kk) * 128:(4 + kk) * 128 + rn]), (xn, xn[0:rn, 512 + kk * 128:512 + (kk + 1) * 128]), (identb, identb[0:rn, 0:rn]))
            fw.tr((bt, bv[0:96, 6 * 128:6 * 128 + rn]), (kpad, kpad[0:rn, :]), (identb, identb[0:rn, 0:rn]))
            for kk in range(4):
                fw.cp("dve" if kk % 2 == 0 else "act", (qaT, qaT[:, kk, r0:r0 + rn]), (bt, bv[:, kk * 128:kk * 128 + rn]))
            for kk in range(2):
                fw.cp("act" if kk % 2 == 0 else "dve", (ckvT, ckvT[:, kk, r0:r0 + rn]), (bt, bv[:, (4 + kk) * 128:(4 + kk) * 128 + rn]))
            fw.cp("dve", (kpeT, kpeT[64:96, r0:r0 + rn]), (bt, bv[64:96, 6 * 128:6 * 128 + rn]))
        for gi, (g0, gn) in enumerate(tiles_of(OFF_GA, NIN, 512)):
            wt = load_w(W["w_in"], g0, gn)
            for jj in range(4):
                for (t0, tn) in tiles_of(0, U, 512):
                    b = fw.bank()
                    for k in range(8):
                        fw.mm((b, b.t[:, 0:tn]), (wt, wt[:, k, jj * 128:(jj + 1) * 128]), (xT, xT[:, k, t0:t0 + tn]), k == 0, k == 7)
                    o = o512.next()
                    fw.act((o, o[:, 0:tn]), (b, b.t[:, 0:tn]), AF.Sigmoid)
                    fw.dma("sp", (R["gT"], gT_d[gi * 4 + jj, :, t0:t0 + tn]), (o, o[:, 0:tn]))
        fw.pop_scope()


    def ssd():
        fw.push_scope()
        sel3 = fw.sb("sel3_sb", [96, NH, 128], BF16)
        fw.dma("pool", (sel3, sel3[:].rearrange("p a b -> p (a b)")), (R["consts"], sel3_d))
        s3_ring = Ring(fw, "s3r", 2, [96, 128], BF16)
        r1_ring = Ring(fw, "r1r", 2, [96, 128])
        mt_ring = Ring(fw, "mtr", 2, [96, 128], BF16)
        da3_ring = Ring(fw, "da3r", 2, [128, 3, NH])
        A_bc = bcast_row("A_bc", GOFF["alog"], NH)
        fw.act((A_bc, A_bc[:]), (A_bc, A_bc[:]), AF.Exp)
        fw.ts("dve", (A_bc, A_bc[:]), (A_bc, A_bc[:]), -1.0, ALU.mult)
        dtb_bc = bcast_row("dtb_bc", GOFF["dtb"], NH)
        Drep = bcast_row("Drep", GOFF["dsk"], NH)
        wa = fw.sb("wa", [128, 16, D], BF16)
        for kk in range(0, 16, 4):
            fw.dma("pool", (wa, wa[:, kk:kk + 4, :]), (R["w"], W["w_a_out"][kk * 128:(kk + 4) * 128, :].rearrange("(k p) n -> p k n", p=128)))
        gncol_sb = fw.sb("gncol_sb", [128, 16])
        fw.dma("sp", (gncol_sb, gncol_sb[:]), (R["consts"], gncol))
        for kk in range(16):
            fw.ts("dve", (wa, wa[:, kk, :]), (wa, wa[:, kk, :]), (gncol_sb, gncol_sb[:, kk:kk + 1]), ALU.mult)
        hT = fw.sb("hT", [128, DI])
        hTb = fw.sb("hTb", [128, DI], BF16)
        xc_ring = Ring(fw, "xcr", 2, [128, 24, 128], BF16)
        zs_ring = Ring(fw, "zsr", 2, [128, DI], BF16)
        sm = Ring(fw, "ssm_sm", 2, [128, 8, NH])
        cd_ring = Ring(fw, "cdr", 2, [128, NH])
        xs_ring = Ring(fw, "xstm", 2, [128, DI], BF16)
        xw_ring = Ring(fw, "xwtm", 2, [128, DI], BF16)
        xsD_ring = Ring(fw, "xsD", 2, [128, DI], BF16)
        B_ring = Ring(fw, "Btm", 2, [128, 512], BF16)
        cb_ring = Ring(fw, "cbT", 2, [128, 4, 128], BF16)
        dexp_ring = Ring(fw, "dexp", 3, [128, 128], BF16)
        MT_ring = Ring(fw, "MTg", 2, [128, 8, 128], BF16)
        y_ring = Ring(fw, "yall", 1, [128, DI])
        t_ring = Ring(fw, "ytmp", 2, [128, 512])
        ynb_ring = Ring(fw, "ynb", 1, [128, DI], BF16)
        ynT_ring = Ring(fw, "ynT", 1, [128, 16, 128], BF16)
        ga_ring = Ring(fw, "gar", 2, [128, 8, 128], BF16)
        mA_ring = Ring(fw, "mAr", 2, [128, 8, 128], BF16)
        so_ring = Ring(fw, "sor", 2, [128, 4, 128])

        STG = int(os.environ.get("SSD_STG", "9"))

        def front(c0, cl):
            xc = xc_ring.next()
            for q6 in range(0, 24, 4):
                fw.dma("sp", (xc, xc[:, q6:q6 + 4, 0:cl]), (R["xc"], xc_d[q6:q6 + 4, :, c0:c0 + cl].rearrange("c p u -> p c u")))
            zs = zs_ring.next()
            fw.dma("sp", (zs, zs[0:cl, :]), (R["zs"], zs_d[c0:c0 + cl, :]))
            s_ = sm.next()
            fw.dma("sp", (s_, s_[0:cl, 0, :]), (R["dtr"], dtr_d[c0:c0 + cl, :]))
            fw.tt("dve", (s_, s_[0:cl, 0, :]), (s_, s_[0:cl, 0, :]), (dtb_bc, dtb_bc[0:cl, :]), ALU.add)
            fw.act((s_, s_[0:cl, 0, :]), (s_, s_[0:cl, 0, :]), AF.Exp)
            fw.act((s_, s_[0:cl, 0, :]), (s_, s_[0:cl, 0, :]), AF.Ln, bias=1.0)
            da3 = da3_ring.next()
            fw.tt("dve", (da3, da3[0:cl]), (s_, s_[0:cl, 0:1, :].to_broadcast([cl, 3, NH])), (A_bc, A_bc[0:cl, :].unsqueeze(1).to_broadcast([cl, 3, NH])), ALU.mult)
            bs = fw.bank()
            da = (da3, da3[0:cl, 0, :])
            fw.mm((bs, bs.t[0:cl, 0:32]), (tri, tri[0:cl, 0:cl]), da, True, True)
            fw.mm((bs, bs.t[0:cl, 32:64]), (tris, tris[0:cl, 0:cl]), da, True, True)
            fw.mm((bs, bs.t[:, 64:96]), (onesf, onesf[0:cl, :]), da, True, True)
            fw.mm((bs, bs.t[0:96, 128:128 + cl]), (da3, da3[0:cl].rearrange("p a b -> p (a b)")), (tri, tri[0:cl, 0:cl]), True, True)
            fw.ts("dve", (s_, s_[0:cl, 2, :]), (bs, bs.t[0:cl, 0:32]), -1.0, ALU.mult)
            fw.act((s_, s_[0:cl, 3, :]), (bs, bs.t[0:cl, 0:32]), AF.Exp)
            fw.act((s_, s_[0:cl, 4, :]), (bs, bs.t[0:cl, 32:64]), AF.Exp)
            fw.tt("dve", (s_, s_[0:cl, 4, :]), (s_, s_[0:cl, 4, :]), (s_, s_[0:cl, 0, :]), ALU.mult)
            cd = cd_ring.next()
            fw.act((cd, cd[:]), (bs, bs.t[:, 64:96]), AF.Exp)
            acsT = s3_ring.next()
            r1 = r1_ring.next()
            mtp = mt_ring.next()
            fw.cp("dve", (acsT, acsT[0:96, 0:cl]), (bs, bs.t[0:96, 128:128 + cl]))
            fw.tt("dve", (r1, r1[32:64, 0:cl]), (bs, bs.t[32:64, 128:128 + cl]), (acsT, acsT[32:64, 0:cl]), ALU.subtract)
            fw.tt("dve", (r1, r1[64:96, 0:cl]), (bs, bs.t[64:96, 128:128 + cl]), (acsT, acsT[64:96, 0:cl]), ALU.subtract)
            fw.cp("dve", (acsT, acsT[32:64, 0:cl]), (r1, r1[32:64, 0:cl]))
            fw.cp("dve", (mtp, mtp[64:96, 0:cl]), (r1, r1[64:96, 0:cl]))
            fw.tt("dve", (r1, r1[64:96, 0:cl]), (r1, r1[64:96, 0:cl]), (mtp, mtp[64:96, 0:cl]), ALU.subtract)
            fw.cp("dve", (acsT, acsT[64:96, 0:cl]), (r1, r1[64:96, 0:cl]))
            xs = xs_ring.next()
            for half in range(2):
                bt = fw.bank()
                bv = bt.t[:].bitcast(BF16)
                for kk in range(8):
                    fw.tr((bt, bv[0:cl, kk * 128:(kk + 1) * 128]), (xc, xc[:, half * 8 + kk, 0:cl]), (identb, identb[:]))
                fw.cp("dve" if half == 0 else "act", (xs, xs[0:cl, half * 1024:(half + 1) * 1024]), (bt, bv[0:cl, :]))
            Bt = B_ring.next()
            bt = fw.bank()
            bv = bt.t[:].bitcast(BF16)
            for g in range(4):
                fw.tr((bt, bv[0:cl, g * 128:(g + 1) * 128]), (xc, xc[:, 16 + g, 0:cl]), (identb, identb[:]))
            fw.cp("dve", (Bt, Bt[0:cl, :]), (bt, bv[0:cl, 0:512]))
            xw = xw_ring.next()
            fw.tt("pool", (xw, xw[0:cl, :].rearrange("p (h d) -> p h d", h=NH)), (xs, xs[0:cl, :].rearrange("p (h d) -> p h d", h=NH)),
                  (s_, s_[0:cl, 4, :].unsqueeze(2).to_broadcast([cl, NH, HD])), ALU.mult)
            xsD = xsD_ring.next()
            fw.tt("pool", (xsD, xsD[0:cl, :].rearrange("p (h d) -> p h d", h=NH)), (xs, xs[0:cl, :].rearrange("p (h d) -> p h d", h=NH)),
                  (Drep, Drep[0:cl, :].unsqueeze(2).to_broadcast([cl, NH, HD])), ALU.mult)
            bc = fw.bank()
            for g in range(4):
                fw.mm((bc, bc.t[0:cl, g * 128:g * 128 + cl]), (xc, xc[:, 16 + g, 0:cl]), (xc, xc[:, 20 + g, 0:cl]), True, True)
            cbT = cb_ring.next()
            fw.cp("act", (cbT, cbT[0:cl].rearrange("p a b -> p (a b)")), (bc, bc.t[0:cl, :]))
            C = _H()
            C.c0, C.cl, C.xc, C.zs, C.s_, C.cd, C.acsT, C.xs, C.Bt, C.xw, C.cbT = c0, cl, xc, zs, s_, cd, acsT, xs, Bt, xw, cbT
            C.xsD = xsD
            C.MT = {}
            return C

        def stageA(C, g):
            c0, cl, xc, zs, s_, cd, acsT, xs, Bt, xw, cbT = C.c0, C.cl, C.xc, C.zs, C.s_, C.cd, C.acsT, C.xs, C.Bt, C.xw, C.cbT
            MT = MT_ring.next()
            C.MT[g] = MT
            for half in range(2):
                bm = fw.bank()
                for hh in range(4):
                    h = g * 8 + half * 4 + hh
                    fw.mm((bm, bm.t[0:cl, hh * 128:hh * 128 + cl]), (sel3, sel3[:, h, 0:cl]), (acsT, acsT[0:96, 0:cl]), True, False)
                    fw.mm((bm, bm.t[0:cl, hh * 128:hh * 128 + cl]), (identb, identb[0:cl, 0:cl]), (cbf_sb, negsl[0:cl, 0:cl]), False, True)
                for hh in range(4):
                    h = g * 8 + half * 4 + hh
                    de = dexp_ring.next()
                    fw.act((de, de[0:cl, 0:cl]), (bm, bm.t[0:cl, hh * 128:hh * 128 + cl]), AF.Exp, bias=(s_, s_[0:cl, 2, h:h + 1]))
                    fw.stt((MT, MT[0:cl, half * 4 + hh, 0:cl]), (de, de[0:cl, 0:cl]), (s_, s_[0:cl, 0, h:h + 1]), (cbT, cbT[0:cl, g, 0:cl]), ALU.mult, ALU.mult)

        def stageB(C, g):
            c0, cl, xc, zs, s_, cd, acsT, xs, Bt, xw, cbT = C.c0, C.cl, C.xc, C.zs, C.s_, C.cd, C.acsT, C.xs, C.Bt, C.xw, C.cbT
            MT = C.MT[g]
            if g == 0:
                C.yall = y_ring.next()
            yall = C.yall
            by = fw.bank()
            xsD = C.xsD
            fw.mm((by, by.t[0:cl, :]), (identb, identb[0:cl, 0:cl]), (xsD, xsD[0:cl, g * 512:(g + 1) * 512]), True, False)
            for hh in range(8):
                h = g * 8 + hh
                fw.mm((by, by.t[0:cl, hh * 64:(hh + 1) * 64]), (MT, MT[0:cl, hh, 0:cl]), (xs, xs[0:cl, h * 64:(h + 1) * 64]), False, hh == 7)
            bo = fw.bank()
            fw.mm((bo, bo.t[0:cl, :]), (xc, xc[:, 20 + g, 0:cl]), (hTb, hTb[:, g * 512:(g + 1) * 512]), True, True)
            yg = (yall, yall[0:cl, g * 512:(g + 1) * 512])
            fw.tt("dve", (yall, yall[0:cl, g * 512:(g + 1) * 512].rearrange("p (h d) -> p h d", h=8)), (bo, bo.t[0:cl, :].rearrange("p (h d) -> p h d", h=8)),
                  (s_, s_[0:cl, 3, g * 8:(g + 1) * 8].unsqueeze(2).to_broadcast([cl, 8, HD])), ALU.mult)
            fw.tt("dve", yg, yg, (by, by.t[0:cl, :]), ALU.add)
            tmp = t_ring.next()
            fw.tt("dve", yg, yg, (zs, zs[0:cl, g * 512:(g + 1) * 512]), ALU.mult)
            fw.act((tmp, tmp[0:cl, :]), yg, AF.Square, accum=(s_, s_[0:cl, 5, g:g + 1]))
            bst = fw.bank()
            fw.mm((bst, bst.t[:, :]), (Bt, Bt[0:cl, g * 128:(g + 1) * 128]), (xw, xw[0:cl, g * 512:(g + 1) * 512]), True, True)
            hg = (hT, hT[:, g * 512:(g + 1) * 512])
            fw.tt("dve", (hT, hT[:, g * 512:(g + 1) * 512].rearrange("p (h d) -> p h d", h=8)), (hT, hT[:, g * 512:(g + 1) * 512].rearrange("p (h d) -> p h d", h=8)),
                  (cd, cd[:, g * 8:(g + 1) * 8].unsqueeze(2).to_broadcast([128, 8, HD])), ALU.mult)
            fw.tt("dve", hg, hg, (bst, bst.t[:, :]), ALU.add)
            fw.cp("act", (hTb, hTb[:, g * 512:(g + 1) * 512]), hg)

        def tail(C):
            c0, cl, xc, zs, s_, cd, acsT, xs, Bt, xw, cbT = C.c0, C.cl, C.xc, C.zs, C.s_, C.cd, C.acsT, C.xs, C.Bt, C.xw, C.cbT
            yall = C.yall
            fw.act((s_, s_[0:cl, 5, 4:8]), (s_, s_[0:cl, 5, 0:4]), AF.Sqrt, bias=(eps_sb, eps_sb[0:cl, :]), scale=1.0 / 512)
            fw.recip((s_, s_[0:cl, 5, 4:8]), (s_, s_[0:cl, 5, 4:8]))
            ynb = ynb_ring.next()
            for g in range(4):
                fw.act((ynb, ynb[0:cl, g * 512:(g + 1) * 512]), (yall, yall[0:cl, g * 512:(g + 1) * 512]), AF.Copy, scale=(s_, s_[0:cl, 5, 4 + g:5 + g]))
            ynT = ynT_ring.next()
            for half in range(2):
                bt = fw.bank()
                bv = bt.t[:].bitcast(BF16)
                for kk in range(8):
                    fw.tr((bt, bv[:, kk * 128:kk * 128 + cl]), (ynb, ynb[0:cl, (half * 8 + kk) * 128:(half * 8 + kk + 1) * 128]), (identb, identb[0:cl, 0:cl]))
                fw.cp("dve" if half == 0 else "act", (ynT, ynT[:, half * 8:(half + 1) * 8, 0:cl]), (bt, bv[:, :].rearrange("p (a b) -> p a b", a=8)[:, :, 0:cl]))
            ga = ga_ring.next()
            fw.dma("sp", (ga, ga[:, :, 0:cl]), (R["gT"], gT_d[0:8, :, c0:c0 + cl].rearrange("c p u -> p c u")))
            mA = mA_ring.next()
            for half in range(2):
                bw = fw.bank()
                for oo in range(4):
                    oc = half * 4 + oo
                    for k in range(16):
                        fw.mm((bw, bw.t[:, oo * 128:oo * 128 + cl]), (wa, wa[:, k, oc * 128:(oc + 1) * 128]), (ynT, ynT[:, k, 0:cl]), k == 0, k == 15)
                fw.tt("dve", (mA, mA[:, half * 4:(half + 1) * 4, 0:cl]), (bw, bw.t[:, :].rearrange("p (a b) -> p a b", a=4)[:, :, 0:cl]), (ga, ga[:, half * 4:(half + 1) * 4, 0:cl]), ALU.mult)
            fw.dma("sp", (R["mT"], mT_d[:, :, c0:c0 + cl].rearrange("c p u -> p c u")), (mA, mA[:, :, 0:cl]))


        def run_seq(chunks):
            Cs = {}
            seq = [(ci, g) for ci in range(len(chunks)) for g in range(4)]
            Cs[0] = front(*chunks[0])
            stageA(Cs[0], 0)
            for n, (ci, g) in enumerate(seq):
                if n + 1 < len(seq):
                    ci2, g2 = seq[n + 1]
                    if g2 == 0:
                        Cs[ci2] = front(*chunks[ci2])
                    stageA(Cs[ci2], g2)
                stageB(Cs[ci], g)
                if g == 3:
                    tail(Cs[ci])
                    del Cs[ci]

        def store_state(idx):
            if STG < 1 and STG != -1:
                return
            for q4 in range(4):
                bt = fw.bank()
                for kk in range(4):
                    k = q4 * 4 + kk
                    fw.tr((bt, bt.t[:, kk * 128:(kk + 1) * 128]), (hT, hT[:, k * 128:(k + 1) * 128]), (identf, identf[:]))
                so = so_ring.next()
                fw.cp("dve", (so, so[:].rearrange("p a b -> p (a b)")), (bt, bt.t[:, :]))
                fw.dma("sp", (R["ssm_o"], ssm_o[idx, q4 * 512:(q4 + 1) * 512, :].rearrange("(k p) n -> p k n", p=128)), (so, so[:]), final=True)

        fw.memset("dve", (hT, hT[:]), 0.0)
        fw.memset("pool", (hTb, hTb[:]), 0.0)
        run_seq([(P0, NMETA)] + [(P0 + NMETA + 128 * i, 128) for i in range(SEQ // 128)])
        store_state(0)
        for j in range(NSAMP if STG != -2 else 0):
            for q4 in range(4):
                si = so_ring.next()
                fw.dma("sp", (si, si[:]), (R["consts"], st_ssm[j, q4 * 512:(q4 + 1) * 512, :].rearrange("(k p) n -> p k n", p=128)))
                if STG == -3:
                    continue
                bt = fw.bank()
                for kk in range(4):
                    fw.tr((bt, bt.t[:, kk * 128:(kk + 1) * 128]), (si, si[:, kk, :]), (identf, identf[:]))
                if STG == -4:
                    continue
                fw.cp("dve", (hT, hT[:, q4 * 512:(q4 + 1) * 512]), (bt, bt.t[:, :]))
                if STG == -5:
                    continue
                fw.cp("act", (hTb, hTb[:, q4 * 512:(q4 + 1) * 512]), (bt, bt.t[:, :]))
            run_seq([(SB[j] + 3, TS)])
            store_state(1 + j)
        fw.pop_scope()


    def attn():
        fw.push_scope()
        ASTG = int(os.environ.get("ATT_STG", "9"))
        NST = (PL + 127) // 128
        NPG4 = NPG // 4
        qlatT = fw.sb("qlatT", [128, 2, NSAMP, 128], BF16)
        qpeT = fw.sb("qpeT", [96, NSAMP, 128], BF16)
        wuv = fw.sb("wuv", [128, 2, MH * VH], BF16)
        fw.dma("pool", (wuv, wuv[:]), (R["w"], W["w_uv"].rearrange("(k p) n -> p k n", p=128)))
        fw.push_scope()
        wqb = fw.sb("wqb", [128, 4, MH * 96], BF16)
        wqbs = fw.sb("wqbs", [128, 4, MH * 96], BF16)
        fw.dma("pool", (wqb, wqb[:]), (R["w"], W["w_q_b"].rearrange("(k p) n -> p k n", p=128)))
        fw.dma("pool", (wqbs, wqbs[:]), (R["w"], W["w_q_b_sw"].rearrange("(k p) n -> p k n", p=128)))
        wukp = fw.sb("wukp", [128, 2, MH, 96], BF16)
        fw.memset("dve", (wukp, wukp[:]), 0.0)
        for kc in range(2):
            fw.dma("pool", (wukp, wukp[:, kc, :, 0:NOPE]), (R["w"], W["w_uk"][kc * 128:(kc + 1) * 128, :].rearrange("p (h n) -> p h n", h=MH)))
        wukT = fw.sb("wukT", [NOPE, MH, KL], BF16)
        fw.dma("pool", (wukT, wukT[:]), (R["w"], W["w_ukT"].rearrange("n (h c) -> n h c", h=MH)))
        cs96_sb = fw.sb("cs96_sb", [96, 2, U])
        fw.dma("sp", (cs96_sb, cs96_sb[:]), (R["consts"], cs96))
        V_sb = fw.sb("V_sb", [128, NST, MH, VH + 1], BF16)
        fw.memset("pool", (V_sb, V_sb[:]), 1.0)
        QT = fw.sb("QT", [96, 4, U], BF16)
        KT = fw.sb("KT", [96, 4, U], BF16)
        PT_ring = Ring(fw, "PT", 5, [128, 512], BF16)
        tq_ring = Ring(fw, "tq", 2, [96, 2, 512])
        rd_ring = Ring(fw, "rd", 2, [65, 512])
        bc_ring = Ring(fw, "bcr", 2, [64, 512])
        o_ring = Ring(fw, "osb", 2, [64, 512], BF16)
        for st in range(NST):
            s0 = P0 + 128 * st
            sn = min(128, P0 + PL - s0)
            for half in range(2):
                b = fw.bank()
                for kc in range(2):
                    fw.mm((b, b.t[0:sn, :]), (ckvT, ckvT[:, kc, s0:s0 + sn]), (wuv, wuv[:, kc, half * 512:(half + 1) * 512]), kc == 0, kc == 1)
                fw.cp("dve" if half == 0 else "act", (V_sb, V_sb[0:sn, st, half * 8:(half + 1) * 8, 0:VH]), (b, b.t[0:sn, :].rearrange("p (h v) -> p h v", h=8)))
        accs = fw.reserve(2)
        acc_i = 0
        pend_fin = [None]
        qtiles = tiles_of(0, PL, 512)
        for hg in range(MH // 4):
            for hh in range(4):
                h = hg * 4 + hh
                for (t0, tn) in tiles_of(0, U, 512):
                    bA = fw.bank()
                    bB = fw.bank()
                    for k in range(4):
                        fw.mm((bA, bA.t[0:96, 0:tn]), (wqb, wqb[:, k, h * 96:(h + 1) * 96]), (qaT, qaT[:, k, t0:t0 + tn]), k == 0, k == 3)
                    for k in range(4):
                        fw.mm((bB, bB.t[0:96, 0:tn]), (wqbs, wqbs[:, k, h * 96:(h + 1) * 96]), (qaT, qaT[:, k, t0:t0 + tn]), k == 0, k == 3)
                    tq = tq_ring.next()
                    fw.tt("dve", (tq, tq[:, 0, 0:tn]), (bA, bA.t[0:96, 0:tn]), (cs96_sb, cs96_sb[:, 0, t0:t0 + tn]), ALU.mult)
                    fw.tt("dve", (tq, tq[:, 1, 0:tn]), (bB, bB.t[0:96, 0:tn]), (cs96_sb, cs96_sb[:, 1, t0:t0 + tn]), ALU.mult)
                    fw.tt("pool", (QT, QT[:, hh, t0:t0 + tn]), (tq, tq[:, 0, 0:tn]), (tq, tq[:, 1, 0:tn]), ALU.add)
                    bK = fw.bank()
                    for kc in range(2):
                        fw.mm((bK, bK.t[0:96, 0:tn]), (wukp, wukp[:, kc, h, :]), (ckvT, ckvT[:, kc, t0:t0 + tn]), kc == 0, False)
                    fw.mm((bK, bK.t[0:96, 0:tn]), (identb, identb[64:96, 0:96]), (kpeT, kpeT[64:96, t0:t0 + tn]), False, True)
                    fw.cp("act", (KT, KT[:, hh, t0:t0 + tn]), (bK, bK.t[0:96, 0:tn]))
            for j in range(NSAMP if ASTG >= 2 else 0):
                cs8 = SB[j] + 3
                bq = fw.bank()
                for kc in range(2):
                    for hh in range(4):
                        h = hg * 4 + hh
                        fw.mm((bq, bq.t[:, kc * 32 + hh * 8:kc * 32 + hh * 8 + 8]), (wukT, wukT[:, h, kc * 128:(kc + 1) * 128]), (QT, QT[0:NOPE, hh, cs8:cs8 + 8]), True, True)
                fw.cp("dve", (qlatT, qlatT[:, :, j, hg * 32:(hg + 1) * 32]), (bq, bq.t[:, 0:64].rearrange("p (a b) -> p a b", a=2)))
                fw.cp("pool", (qpeT, qpeT[64:96, j, hg * 32:(hg + 1) * 32].rearrange("p (a b) -> p a b", a=4)), (QT, QT[64:96, :, cs8:cs8 + 8]))
            for hh in range(4 if ASTG >= 3 else 0):
                h = hg * 4 + hh
                for qi, (qp0, qn) in enumerate(qtiles):
                    q0 = P0 + qp0
                    bO = accs[acc_i]
                    acc_i = (acc_i + 1) % 2
                    sts = [st for st in range(NST) if st * 128 <= qp0 + qn - 1]
                    prevq = []

                    def do_pv(pv, bO=bO, qn=qn, h=h, nlast=len(sts) - 1):
                        PT_, st_, si_, sn_ = pv
                        fw.mm((bO, bO.t[0:VH + 1, 0:qn]), (V_sb, V_sb[0:sn_, st_, h, :]), (PT_, PT_[0:sn_, 0:qn]), si_ == 0, si_ == nlast)

                    for si, st in enumerate(sts):
                        s0 = P0 + 128 * st
                        sn = min(128, P0 + PL - s0)
                        diag = (st * 128 + sn - 1) > qp0
                        bS = fw.bank()
                        fw.mm((bS, bS.t[0:sn, 0:qn]), (KT, KT[:, hh, s0:s0 + sn]), (QT, QT[:, hh, q0:q0 + qn]), True, not diag)
                        if diag:
                            r = st - 4 * qi
                            fw.mm((bS, bS.t[0:sn, 0:qn]), (identb, identb[0:sn, 0:sn]), (cbf_sb, negm(r)[0:sn, 0:qn]), False, True)
                        PT = PT_ring.next()
                        fw.act((PT, PT[0:sn, 0:qn]), (bS, bS.t[0:sn, 0:qn]), AF.Exp)
                        if si == 1 and pend_fin[0] is not None:
                            pend_fin[0]()
                            pend_fin[0] = None
                        prevq.append((PT, st, si, sn))
                        if len(prevq) > 2:
                            do_pv(prevq.pop(0))
                    if pend_fin[0] is not None:
                        pend_fin[0]()
                        pend_fin[0] = None
                    while prevq:
                        do_pv(prevq.pop(0))

                    def fin(bO=bO, qn=qn, h=h, q0=q0):
                        rd = rd_ring.next()
                        fw.recip((rd, rd[64:65, 0:qn]), (bO, bO.t[64:65, 0:qn]))
                        bB = fw.bank()
                        fw.mm((bB, bB.t[0:64, 0:qn]), (onesf, onesf[64:65, 0:64]), (rd, rd[64:65, 0:qn]), True, True)
                        bcs = bc_ring.next()
                        fw.cp("act", (bcs, bcs[:, 0:qn]), (bB, bB.t[0:64, 0:qn]))
                        osb = o_ring.next()
                        fw.tt("dve", (osb, osb[:, 0:qn]), (bO, bO.t[0:64, 0:qn]), (bcs, bcs[:, 0:qn]), ALU.mult)
                        fw.dma("sp", (R["oT"], oT_d[h, :, q0:q0 + qn]), (osb, osb[:, 0:qn]))
                    pend_fin[0] = fin
        if pend_fin[0] is not None:
            pend_fin[0]()
            pend_fin[0] = None
        fw.pop_scope()
        fw.push_scope()
        ptb = fw.sb("ptb", [128, NSAMP * NPG], I32)
        fw.dma("sp", (ptb, ptb[:]), (R["consts"], ptab[0, :].partition_broadcast(128)))
        pm = fw.sb("pm", [128, 1])
        fw.dma("sp", (pm, pm[:]), (R["consts"], cf32[:, 512 + NH * 128:512 + NH * 128 + 1]), slow=True)
        idx = fw.sb("idx", [128, NSAMP * NPG4], I32)
        for qd in range(4):
            fw.ts("dve", (idx, idx[32 * qd:32 * qd + 32, :]), (ptb, ptb[32 * qd:32 * qd + 32, :].rearrange("p (i q) -> p i q", q=4)[:, :, qd]),
                  32.0, ALU.mult, s2=(pm, pm[32 * qd:32 * qd + 32, :]), op1=ALU.add)
        kvp_ring = Ring(fw, "kvp", 4, [128, 4, KL + 1], BF16)
        for t_ in kvp_ring.tiles:
            fw.memset("dve", (t_, t_[:]), 1.0)
        krg_ring = Ring(fw, "krg", 3, [128, 4, ROPE])
        kv32_ring = Ring(fw, "kv32", 2, [128, 4, KL])
        krp_ring = Ring(fw, "krp", 4, [128, 4, 96], BF16)
        ones_b = fw.sb("ones_b", [128, 8], BF16)
        fw.memset("dve", (ones_b, ones_b[:]), 1.0)
        for t_ in krp_ring.tiles:
            fw.memset("dve", (t_, t_[:]), 0.0)
        KTp_ring = Ring(fw, "KTp", 4, [128, 3, 128], BF16)
        PTs_ring = Ring(fw, "PTs", 4, [128, 128], BF16)
        ckn_ring = Ring(fw, "ckn", 2, [8, KL + 4], BF16)
        for t_ in ckn_ring.tiles:
            fw.memset("dve", (t_, t_[:]), 1.0)
        ol_ring = Ring(fw, "olat", 2, [128, KL], BF16)
        olT_ring = Ring(fw, "olatT", 2, [128, 2, 128], BF16)
        oS_ring = Ring(fw, "oS", 2, [64, MH, 8], BF16)
        sst = Ring(fw, "sst", 2, [128, 2])
        accs = fw.reserve(2)
        ckr = cache_kr.rearrange("r (a b) -> r a b", a=4)
        SUB = int(os.environ.get("SUB", "9"))
        for j in range(NSAMP if ASTG >= 4 else 0):
            cs8 = SB[j] + 3
            bO = accs[j % 2]
            grp = {}

            def do_gather(i, j=j, grp=grp):
                kvp = kvp_ring.next()
                krp = krp_ring.next()
                krg = krg_ring.next()
                kv32 = kv32_ring.next()
                ic = j * NPG4 + i
                fw.gather((kv32, kv32[:].rearrange("p a b -> p (a b)")), cache_kv, R["cache"], (idx, idx[:, ic:ic + 1]))
                fw.gather((krg, krg[:].rearrange("p a b -> p (a b)")), cache_kr, R["cache"], (idx, idx[:, ic:ic + 1]))
                fw.cp("act", (kvp, kvp[:, :, 0:KL]), (kv32, kv32[:]))
                fw.cp("pool", (krp, krp[:, :, 64:96]), (krg, krg[:]))
                grp[i] = (kvp, krp)

            items = [(i, r) for i in range(NPG4) for r in range(4)]
            st1 = {}
            st2 = {}

            def stage1(k, grp=grp, st1=st1):
                i, r = items[k]
                if r == 0:
                    if i == 0:
                        do_gather(0)
                    if i + 1 < NPG4:
                        do_gather(i + 1)
                kvp, krp = grp[i]
                bt = fw.bank()
                bv = bt.t[:].bitcast(BF16)
                fw.tr((bt, bv[:, 0:128]), (kvp, kvp[:, r, 0:128]), (identb, identb[:]))
                fw.tr((bt, bv[:, 128:256]), (kvp, kvp[:, r, 128:256]), (identb, identb[:]))
                fw.tr((bt, bv[0:96, 256:384]), (krp, krp[:, r, :]), (identb, identb[:]))
                KTp = KTp_ring.next()
                fw.cp("dve", (KTp, KTp[:, 0:2, :]), (bt, bv[:, 0:256].rearrange("p (a b) -> p a b", a=2)))
                fw.cp("dve", (KTp, KTp[0:96, 2, :]), (bt, bv[0:96, 256:384]))
                st1[k] = KTp

            def stage2(k, j=j, st1=st1, st2=st2):
                KTp = st1.pop(k)
                bS = fw.bank()
                fw.mm((bS, bS.t[:, 0:128]), (KTp, KTp[:, 0, :]), (qlatT, qlatT[:, 0, j, :]), True, False)
                fw.mm((bS, bS.t[:, 0:128]), (KTp, KTp[:, 1, :]), (qlatT, qlatT[:, 1, j, :]), False, False)
                fw.mm((bS, bS.t[:, 0:128]), (KTp, KTp[64:96, 2, :]), (qpeT, qpeT[64:96, j, :]), False, True)
                PTs = PTs_ring.next()
                fw.act((PTs, PTs[:]), (bS, bS.t[:, 0:128]), AF.Exp)
                st2[k] = PTs

            def stage3(k, bO=bO, grp=grp, st2=st2):
                i, r = items[k]
                PTs = st2.pop(k)
                kvp, krp = grp[i]
                fw.mm((bO, bO.t[:, 0:KL + 1]), (PTs, PTs[:]), (kvp, kvp[:, r, :]), k == 0, False)

            n_it = len(items) if SUB >= 2 else 0
            for k in range(n_it + 2):
                if k < n_it:
                    stage1(k)
                if 0 <= k - 1 < n_it:
                    stage2(k - 1)
                if 0 <= k - 2 < n_it:
                    stage3(k - 2)
            if SUB < 3:
                continue
            bS = fw.bank()
            fw.mm((bS, bS.t[0:8, 0:128]), (ckvT, ckvT[:, 0, cs8:cs8 + 8]), (qlatT, qlatT[:, 0, j, :]), True, False)
            fw.mm((bS, bS.t[0:8, 0:128]), (ckvT, ckvT[:, 1, cs8:cs8 + 8]), (qlatT, qlatT[:, 1, j, :]), False, False)
            fw.mm((bS, bS.t[0:8, 0:128]), (identb, identb[:, 0:8]), (cbf_sb, cbf_sb[:, 128 + 2048:128 + 2048 + 128]), False, False)
            fw.mm((bS, bS.t[0:8, 0:128]), (kpeT, kpeT[64:96, cs8:cs8 + 8]), (qpeT, qpeT[64:96, j, :]), False, True)
            PTs = PTs_ring.next()
            fw.act((PTs, PTs[0:8, :]), (bS, bS.t[0:8, 0:128]), AF.Exp)
            if SUB < 4:
                continue
            ckn = ckn_ring.next()
            fw.dma("pool", (ckn, ckn[:, 0:KL]), (R["kvlat"], kvlat[cs8:cs8 + 8, :]))
            fw.mm((bO, bO.t[:, 0:KL + 1]), (PTs, PTs[0:8, :]), (ckn, ckn[0:8, 0:KL + 1]), False, True)
            if SUB < 5:
                continue
            ss_ = sst.next()
            fw.recip((ss_, ss_[:, 0:1]), (bO, bO.t[:, KL:KL + 1]))
            ol = ol_ring.next()
            fw.ts("dve", (ol, ol[:]), (bO, bO.t[:, 0:KL]), (ss_, ss_[:, 0:1]), ALU.mult)
            if "dbg_ol" in debug and j == 0:
                dol = nc.dram_tensor("dbg_ol", [128, KL], BF16, kind="ExternalOutput").ap()
                dq = nc.dram_tensor("dbg_q", [128, 2, NSAMP, 128], BF16, kind="ExternalOutput").ap()
                dqp = nc.dram_tensor("dbg_qp", [96, NSAMP, 128], BF16, kind="ExternalOutput").ap()
                dden = nc.dram_tensor("dbg_den", [128, 2], F32, kind="ExternalOutput").ap()
                fw.dma("sp", (R["yout"], dol), (ol, ol[:]), final=True)
                fw.dma("sp", (R["yout"], dq), (qlatT, qlatT[:]), final=True)
                fw.dma("sp", (R["yout"], dqp), (qpeT, qpeT[:]), final=True)
                fw.dma("sp", (R["yout"], dden), (ss_, ss_[:]), final=True)
            bt = fw.bank()
            bv = bt.t[:].bitcast(BF16)
            for kc in range(2):
                fw.tr((bt, bv[:, kc * 128:(kc + 1) * 128]), (ol, ol[:, kc * 128:(kc + 1) * 128]), (identb, identb[:]))
            olT = olT_ring.next()
            fw.cp("dve", (olT, olT[:].rearrange("p a b -> p (a b)")), (bt, bv[:, 0:256]))
            if SUB < 6:
                continue
            b2 = fw.bank()
            for h in range(MH):
                for kc in range(2):
                    fw.mm((b2, b2.t[0:VH, h * 8:(h + 1) * 8]), (wuv, wuv[:, kc, h * VH:(h + 1) * VH]), (olT, olT[:, kc, h * 8:(h + 1) * 8]), kc == 0, kc == 1)
            oS = oS_ring.next()
            fw.cp("dve", (oS, oS[:].rearrange("p a b -> p (a b)")), (b2, b2.t[0:VH, 0:128]))
            fw.dma("sp", (R["oT"], oT_d[:, :, cs8:cs8 + 8].rearrange("h v u -> v h u")), (oS, oS[:]))
        fw.reserve(0)
        fw.pop_scope()
        fw.pop_scope()
        fw.push_scope()
        alloc_norm_rings()
        wbo = fw.sb("wbo", [VH, MH, D], BF16)
        for hq in range(0, MH, 4):
            fw.dma("pool", (wbo, wbo[:, hq:hq + 4, :]), (R["w"], W["w_b_out"][hq * VH:(hq + 4) * VH, :].rearrange("(h v) n -> v h n", v=VH)))
        wo = fw.sb("wo", [128, 8, D], BF16)
        for kk in range(0, 8, 4):
            fw.dma("pool", (wo, wo[:, kk:kk + 4, :]), (R["w"], W["w_o"][kk * 128:(kk + 4) * 128, :].rearrange("(k p) n -> p k n", p=128)))
        gmix = bcast_row("gmix", D, D)
        oT_ring = Ring(fw, "oTr", 2, [VH, MH, 512], BF16)
        gb_ring = Ring(fw, "gbr", 2, [128, 8, 512], BF16)
        mA2_ring = Ring(fw, "mA2", 2, [128, 8, 512], BF16)
        mS_ring = Ring(fw, "mS", 2, [128, 8, 512], BF16)
        tm_ring = Ring(fw, "tmr", 2, [128, 512])
        for (t0, tn) in (tiles_of(0, U, 512) if ASTG >= 5 else []):
            oTt = oT_ring.next()
            for hq in range(0, MH, 8):
                fw.dma("sp", (oTt, oTt[:, hq:hq + 8, 0:tn]), (R["oT"], oT_d[hq:hq + 8, :, t0:t0 + tn].rearrange("h v u -> v h u")))
            gb = gb_ring.next()
            fw.dma("sp", (gb, gb[:, :, 0:tn]), (R["gT"], gT_d[8:16, :, t0:t0 + tn].rearrange("c p u -> p c u")))
            mA2 = mA2_ring.next()
            fw.dma("sp", (mA2, mA2[:, :, 0:tn]), (R["mT"], mT_d[:, :, t0:t0 + tn].rearrange("c p u -> p c u")))
            mS = mS_ring.next()
            for oc in range(8):
                b = fw.bank()
                for h in range(MH):
                    fw.mm((b, b.t[:, 0:tn]), (wbo, wbo[:, h, oc * 128:(oc + 1) * 128]), (oTt, oTt[:, h, 0:tn]), h == 0, h == MH - 1)
                tm = tm_ring.next()
                fw.tt("dve", (tm, tm[:, 0:tn]), (b, b.t[:, 0:tn]), (gb, gb[:, oc, 0:tn]), ALU.mult)
                fw.tt("pool", (mS, mS[:, oc, 0:tn]), (tm, tm[:, 0:tn]), (mA2, mA2[:, oc, 0:tn]), ALU.add)
            for (r0, rn) in tiles_of(t0, t0 + tn, 128):
                bks = [fw.bank(), fw.bank()]
                for half in range(2):
                    for k in range(8):
                        fw.mm((bks[half], bks[half].t[0:rn, :]), (mS, mS[:, k, r0 - t0:r0 - t0 + rn]), (wo, wo[:, k, half * 512:(half + 1) * 512]), k == 0, k == 7)
                postnorm_residual(bks, rn, gmix, res1, R["res1"], res2, R["res2"], r0)
        fw.pop_scope()

    PH = build.phases
    if "ffn1" in PH:
        ffn("ffn1", 0, 0, xin, R["xin"], res1, R["res1"], final=("proj" not in PH))
    if "proj" in PH:
        mix_proj()
    if "ssd" in PH:
        ssd()
    if "attn" in PH:
        attn()
    if "ffn2" in PH:
        src2, reg2 = (res2, R["res2"]) if "attn" in PH else (res1, R["res1"])
        ffn("ffn2", 2, 2 * D, src2, reg2, yout, R["yout"], final=True)
    fw.emit()
    return nc


build.phases = ("ffn1", "proj", "ssd", "attn", "ffn2")


def host_consts(SEQ, PAST):
    U = 3 + NMETA + SEQ + NSAMP * 11
    ident = np.eye(128, dtype=np.float32)
    t = np.arange(128)
    tri = (t[:, None] <= t[None, :]).astype(np.float32)
    tris = (t[:, None] > t[None, :]).astype(np.float32)
    ones = np.ones((128, 128), np.float32)
    sel = np.zeros((128, NH, 128), np.float32)
    for h in range(NH):
        sel[h, h, :] = 1.0
    cf32 = np.concatenate([ident, tri, tris, ones, sel.reshape(128, NH * 128), (t % 32).astype(np.float32)[:, None]], axis=1)
    neg = np.where(t[None, :] < t[:, None], NEGV, 0.0).astype(np.float32)
    q = np.arange(512)
    negm = [np.where(q[None, :] < 128 * r + t[:, None], NEGV, 0.0).astype(np.float32) for r in range(4)]
    neg8 = np.zeros((128, 128), np.float32)
    for s_ in range(8):
        for h in range(MH):
            for tq in range(8):
                if tq < s_:
                    neg8[s_, h * 8 + tq] = NEGV
    cbf = np.concatenate([neg] + negm + [neg8], axis=1)
    pos = np.zeros(U, np.float32)
    pos[3:3 + NMETA + SEQ] = np.arange(NMETA + SEQ)
    for j in range(NSAMP):
        b0 = 3 + NMETA + SEQ + 11 * j
        pos[b0 + 3:b0 + 11] = PAST + np.arange(8)
    inv = (np.float32(10000.0) ** (-np.arange(0, ROPE, 2, dtype=np.float32) / np.float32(ROPE))).astype(np.float32)
    ang = pos[:, None].astype(np.float32) * inv[None, :]
    cos = np.cos(ang).astype(np.float32)
    sin = np.sin(ang).astype(np.float32)
    csU = np.stack([np.concatenate([cos, cos], 1), np.concatenate([-sin, sin], 1)], axis=1)
    cs96 = np.zeros((96, 2, U), np.float32)
    cs96[0:64, 0, :] = SCALE
    cs96[64:80, 0, :] = cos.T * SCALE
    cs96[80:96, 0, :] = cos.T * SCALE
    cs96[64:80, 1, :] = -sin.T * SCALE
    cs96[80:96, 1, :] = sin.T * SCALE
    sel3 = np.zeros((96, NH, 128), np.float32)
    for h in range(NH):
        for part in range(3):
            sel3[32 * part + h, h, :] = 1.0
    return dict(cf32=cf32, cbf=cbf, csU=np.ascontiguousarray(csU), cs96=cs96, sel3=sel3.reshape(96, NH * 128))


def swap_rope_cols(w, head_w, rope_off):
    w2 = w.copy()
    n = w.shape[1] // head_w
    for h in range(n):
        a = h * head_w + rope_off
        w2[:, a:a + 16] = w[:, a + 16:a + 32]
        w2[:, a + 16:a + 32] = w[:, a:a + 16]
    return w2


def host_shared(inp, SEQ, PAST):
    sh = host_consts(SEQ, PAST)
    for nm in ["ffn1_w_gate", "ffn1_w_up", "ffn1_w_down", "ffn2_w_gate", "ffn2_w_up", "ffn2_w_down", "w_in", "w_q_b",
               "w_a_out", "w_b_out", "w_o"]:
        sh[nm] = np.ascontiguousarray(inp[nm][0])
    w_in = inp["w_in"][0]
    kpe0 = DI + CD + NH + QL + KL
    sh["w_kpe_sw"] = np.ascontiguousarray(np.concatenate([w_in[:, kpe0 + 16:kpe0 + 32], w_in[:, kpe0:kpe0 + 16]], axis=1))
    sh["w_q_b_sw"] = swap_rope_cols(inp["w_q_b"][0], 96, 64)
    w_uk = inp["w_uk"][0]
    sh["w_uk"] = np.ascontiguousarray(w_uk.reshape(KL, MH * NOPE))
    sh["w_ukT"] = np.ascontiguousarray(w_uk.transpose(2, 1, 0).reshape(NOPE, MH * KL))
    sh["w_uv"] = np.ascontiguousarray(inp["w_uv"][0].reshape(KL, MH * VH))
    gc = np.stack([inp[k][0].reshape(8, 128).T for k in ["ffn1_pre_g", "mix_pre_g", "ffn2_pre_g"]], axis=1)
    sh["gcols"] = np.ascontiguousarray(gc.astype(np.float32))
    sh["grows"] = np.concatenate([inp[k][0].reshape(-1) for k in
                                  ["ffn1_post_g", "mix_post_g", "ffn2_post_g", "ssm_norm_g", "q_a_norm_g", "kv_a_norm_g",
                                   "dt_bias", "a_log", "d_skip"]]).astype(np.float32)[None, :]
    cw = inp["conv_w"][0]
    sh["convw"] = np.ascontiguousarray(cw.reshape(4, 24, 128).transpose(2, 1, 0).reshape(128, 96))
    sh["convb"] = np.ascontiguousarray(inp["conv_b"][0].reshape(24, 128).T)
    sh["gncol"] = np.ascontiguousarray(inp["ssm_norm_g"][0].reshape(16, 128).T)
    npool = inp["cache_kv_latent"].shape[1]
    sh["cache_kv"] = inp["cache_kv_latent"][0].reshape(npool * 32, 4 * KL)
    sh["cache_kr"] = inp["cache_k_rope"][0].reshape(npool * 32, 4 * ROPE)
    return sh


def host_core(inp, sh, b, SEQ):
    U = 3 + NMETA + SEQ + NSAMP * 11
    m = dict(sh)
    xin = np.zeros((U, D), np.float32)
    xin[3:3 + NMETA] = inp["meta_tokens"]
    xin[3 + NMETA:3 + NMETA + SEQ] = inp["x_prompt"][b]
    stc = np.zeros((NSAMP * 3, CD), np.float32)
    for j in range(NSAMP):
        b0 = 3 + NMETA + SEQ + 11 * j
        xin[b0 + 3:b0 + 11] = inp["x_sample"][NSAMP * b + j]
        stc[3 * j:3 * j + 3] = inp["state_conv"][0, NSAMP * b + j]
    m["xin"] = xin
    m["st_conv"] = stc
    m["st_ssm"] = np.ascontiguousarray(inp["state_ssm"][0, NSAMP * b:NSAMP * b + NSAMP].reshape(NSAMP, DI, DS))
    m["ptab"] = np.ascontiguousarray(inp["page_table"][NSAMP * b:NSAMP * b + NSAMP].reshape(1, -1)).astype(np.int32)
    return m


SEQ_FULL = 2048
NPG_FULL = 128


def kernel(**inputs):
    inp = {k: np.asarray(v) for k, v in inputs.items()}
    nb = inp["x_prompt"].shape[0]
    seq = inp["x_prompt"].shape[1]
    npg = inp["page_table"].shape[1]
    npool = inp["cache_kv_latent"].shape[1]
    past = npg * inp["cache_kv_latent"].shape[2]
    nsamp_total = inp["x_sample"].shape[0]
    U = 3 + NMETA + seq + NSAMP * 11
    PL = NMETA + seq
    nc = build(seq, npg, npool)
    sh = host_shared(inp, seq, past)
    in_maps = [host_core(inp, sh, b, seq) for b in range(nb)]
    res = run_bass_kernel_spmd(nc, in_maps, core_ids=list(range(nb)))
    y_prompt = np.zeros((nb, seq, D), np.float32)
    y_sample = np.zeros((nsamp_total, TS, D), np.float32)
    kvp = np.zeros((1, nb, PL, KL), np.float32)
    krp = np.zeros((1, nb, PL, ROPE), np.float32)
    ssp = np.zeros((1, nb, NH, HD, DS), np.float32)
    cvp = np.zeros((1, nb, CK - 1, CD), np.float32)
    kvs = np.zeros((1, nsamp_total, TS, KL), np.float32)
    krs = np.zeros((1, nsamp_total, TS, ROPE), np.float32)
    sss = np.zeros((1, nsamp_total, NH, HD, DS), np.float32)
    cvs = np.zeros((1, nsamp_total, CK - 1, CD), np.float32)
    for b in range(nb):
        r = res.results[b]
        yo = np.asarray(r["yout"]); kv = np.asarray(r["kvlat"]); kr = np.asarray(r["krope"])
        so = np.asarray(r["ssm_o"]); co = np.asarray(r["conv_o"])
        y_prompt[b] = yo[3 + NMETA:3 + PL]
        kvp[0, b] = kv[3:3 + PL]
        krp[0, b] = kr[3:3 + PL]
        ssp[0, b] = so[0].reshape(NH, HD, DS)
        cvp[0, b] = co[0:3]
        for j in range(NSAMP):
            sb = 3 + PL + 11 * j
            i = NSAMP * b + j
            y_sample[i] = yo[sb + 3:sb + 11]
            kvs[0, i] = kv[sb + 3:sb + 11]
            krs[0, i] = kr[sb + 3:sb + 11]
            sss[0, i] = so[1 + j].reshape(NH, HD, DS)
            cvs[0, i] = co[11 * j + 11:11 * j + 14]
    return (y_prompt, y_sample, kvp, krp, ssp, cvp, kvs, krs, sss, cvs)
```

```python
import os
import numpy as np
import concourse.bass as bass
import concourse.mybir as mybir
from concourse.bass_utils import run_bass_kernel_spmd

F32 = mybir.dt.float32
BF16 = mybir.dt.bfloat16
I32 = mybir.dt.int32
AF = mybir.ActivationFunctionType
ALU = mybir.AluOpType
AX = mybir.AxisListType

N_DMA_SEMS = 24


class TT:
    __slots__ = ("name", "t", "last_w", "reads", "excl")

    def __init__(self, name, t=None):
        self.name = name
        self.t = t
        self.last_w = None
        self.reads = []
        self.excl = False

    def __getitem__(self, k):
        return self.t[k]


class _Op:
    __slots__ = ("fn", "waits", "tok", "dma", "needed")

    def __init__(self, fn, waits, tok, dma):
        self.fn = fn
        self.waits = waits
        self.tok = tok
        self.dma = dma
        self.needed = False


class FW:
    COMPUTE = ("pe", "act", "dve", "pool")

    def __init__(self, nc):
        self.nc = nc
        self.q = {k: [] for k in ("pe", "act", "dve", "pool", "sp")}
        self.cnt = {k: 0 for k in self.COMPUTE}
        self.dma_rr = {"sp": 0, "pool": 0, "act": 0}
        self.dma_cum = {}
        self.dma_last = {}
        self.same_engine_sync = (os.environ.get("SES", "1") == "1")
        self.raw_only = (os.environ.get("RAWONLY", "1") == "1")
        self.final_tokens = []
        self.optok = {}
        self.banks = None
        self.bank_i = 0
        self.scopes = []
        self.pending = {}
        self.reserved = 0

    def sb(self, name, shape, dt=F32):
        if self.scopes:
            return TT(name, self.scopes[-1].enter_context(self.nc.sbuf_tensor(name, list(shape), dt)))
        return TT(name, self.nc.alloc_sbuf_tensor(name, list(shape), dt))

    def push_scope(self):
        import contextlib
        self.scopes.append(contextlib.ExitStack())

    def pop_scope(self):
        self.barrier()
        self.scopes.pop().close()

    def barrier(self):
        w = {}
        for e in self.COMPUTE:
            if self.cnt[e] > 0:
                w[e] = self.cnt[e]
        for key, tok in self.dma_last.items():
            w[key] = tok[1]
        for e in self.q:
            self.pending[e] = dict(w)

    def region(self, name):
        return TT(name, None)

    def init_banks(self):
        self.banks = [TT("bank%d" % i, self.nc.alloc_psum_tensor("bank%d" % i, [128, 512], F32)) for i in range(8)]
        for b in self.banks:
            b.excl = True

    def bank(self):
        n = 8 - self.reserved
        self.bank_i = self.bank_i % n
        b = self.banks[self.bank_i]
        self.bank_i = (self.bank_i + 1) % n
        return b

    def reserve(self, n):
        self.reserved = n
        return [self.banks[8 - 1 - i] for i in range(n)]

    def op(self, eng, fn, r=(), w=(), dma=False, final=False):
        deps = []
        deps2 = []
        for t in r:
            if t.last_w is not None:
                deps.append(t.last_w)
            if t.excl:
                deps.extend(x for x in t.reads if x[0] != eng)
        for t in w:
            if t.last_w is not None:
                deps2.append(t.last_w)
            deps2.extend(t.reads)
        if dma or not self.raw_only:
            deps.extend(deps2)
        else:
            deps.extend(x for x in deps2 if x[0] != eng)
        if dma:
            i = self.dma_rr[eng]
            self.dma_rr[eng] = (i + 1) % N_DMA_SEMS
            key = ("dma", eng, i)
            prev = self.dma_last.get(key)
            if prev is not None:
                deps.append(prev)
            val = self.dma_cum.get(key, 0) + 16
            self.dma_cum[key] = val
            tok = (key, val)
            self.dma_last[key] = tok
        else:
            self.cnt[eng] += 1
            tok = (eng, self.cnt[eng])
        waits = {}
        if self.pending.get(eng):
            for k, v in self.pending[eng].items():
                if k == eng:
                    continue
                waits[k] = v
            self.pending[eng] = None
        for (k, v) in deps:
            if k == eng and not dma:
                if eng == "pe" or not self.same_engine_sync:
                    continue
            if waits.get(k, 0) < v:
                waits[k] = v
        o = _Op(fn, waits, tok, dma)
        self.q[eng].append(o)
        if not dma:
            self.optok[tok] = o
        for t in r:
            t.reads.append(tok)
        for t in w:
            t.last_w = tok
            t.reads = []
        if final:
            self.final_tokens.append(tok)
        return tok

    def mm(self, out, lhsT, rhs, start, stop):
        self.op("pe", lambda e: e.matmul(out=out[1], lhsT=lhsT[1], rhs=rhs[1], start=start, stop=stop),
                r=[lhsT[0], rhs[0]], w=[out[0]])

    def tr(self, out, in_, ident):
        self.op("pe", lambda e: e.transpose(out=out[1], in_=in_[1], identity=ident[1]), r=[in_[0], ident[0]], w=[out[0]])

    def act(self, out, in_, func, bias=None, scale=None, accum=None, eng="act"):
        r = [in_[0]]
        kw = {}
        if bias is not None:
            if isinstance(bias, tuple):
                r.append(bias[0])
                kw["bias"] = bias[1]
            else:
                kw["bias"] = bias
        if scale is not None:
            if isinstance(scale, tuple):
                r.append(scale[0])
                kw["scale"] = scale[1]
            else:
                kw["scale"] = scale
        w = [out[0]]
        if accum is not None:
            w.append(accum[0])
            kw["accum_out"] = accum[1]
        self.op(eng, lambda e: e.activation(out=out[1], in_=in_[1], func=func, **kw), r=r, w=w)

    def ts(self, eng, out, in0, s1, op0, s2=None, op1=None):
        r = [in0[0]]
        a1 = s1
        if isinstance(s1, tuple):
            r.append(s1[0])
            a1 = s1[1]
        a2 = s2
        if isinstance(s2, tuple):
            r.append(s2[0])
            a2 = s2[1]
        if op1 is None:
            self.op(eng, lambda e: e.tensor_scalar(out=out[1], in0=in0[1], scalar1=a1, scalar2=None, op0=op0), r=r, w=[out[0]])
        else:
            self.op(eng, lambda e: e.tensor_scalar(out=out[1], in0=in0[1], scalar1=a1, scalar2=a2, op0=op0, op1=op1), r=r, w=[out[0]])

    def tt(self, eng, out, in0, in1, op):
        self.op(eng, lambda e: e.tensor_tensor(out=out[1], in0=in0[1], in1=in1[1], op=op), r=[in0[0], in1[0]], w=[out[0]])

    def stt(self, out, in0, scalar, in1, op0, op1):
        r = [in0[0], in1[0]]
        a = scalar
        if isinstance(scalar, tuple):
            r.append(scalar[0])
            a = scalar[1]
        self.op("dve", lambda e: e.scalar_tensor_tensor(out=out[1], in0=in0[1], scalar=a, in1=in1[1], op0=op0, op1=op1), r=r, w=[out[0]])

    def cp(self, eng, out, in_):
        if eng == "act":
            self.op(eng, lambda e: e.copy(out=out[1], in_=in_[1]), r=[in_[0]], w=[out[0]])
        else:
            self.op(eng, lambda e: e.tensor_copy(out=out[1], in_=in_[1]), r=[in_[0]], w=[out[0]])

    def memset(self, eng, out, val):
        self.op(eng, lambda e: e.memset(out[1], val), w=[out[0]])

    def recip(self, out, in_):
        self.op("dve", lambda e: e.reciprocal(out=out[1], in_=in_[1]), r=[in_[0]], w=[out[0]])

    def dma(self, q, out, in_, final=False, slow=False):
        if slow:
            self.op(q, lambda e: e.dma_start(out=out[1], in_=in_[1], allow_slow_non_contiguous=True), r=[in_[0]], w=[out[0]], dma=True, final=final)
        else:
            self.op(q, lambda e: e.dma_start(out=out[1], in_=in_[1]), r=[in_[0]], w=[out[0]], dma=True, final=final)

    def gather(self, out, in_ap, in_reg, idx):
        self.op("pool", lambda e: e.indirect_dma_start(out=out[1], out_offset=None, in_=in_ap,
                                                         in_offset=bass.IndirectOffsetOnAxis(ap=idx[1], axis=0)),
                r=[in_reg, idx[0]], w=[out[0]], dma=True)

    def emit(self):
        nc = self.nc
        for qn, ops in self.q.items():
            for o in ops:
                for (k, v) in o.waits.items():
                    if isinstance(k, str):
                        self.optok[(k, v)].needed = True
        for tok in self.final_tokens:
            if isinstance(tok[0], str):
                self.optok[tok].needed = True
        remap = {}
        for e in self.COMPUTE:
            c = 0
            for o in self.q[e]:
                if o.dma:
                    continue
                if o.needed:
                    c += 1
                remap[o.tok] = c
        sems = {e: nc.alloc_semaphore("s_" + e) for e in self.COMPUTE}
        dsems = {}
        for key in self.dma_cum:
            dsems[key] = nc.alloc_semaphore("d_%s_%d" % (key[1], key[2]))
        finals = list(self.final_tokens)

        def run_queue(qn, eng):
            known = {}
            for o in self.q[qn]:
                for (k, v) in o.waits.items():
                    if isinstance(k, str):
                        v2 = remap[(k, v)]
                        s = sems[k]
                    else:
                        v2 = v
                        s = dsems[k]
                    if known.get(k, 0) >= v2:
                        continue
                    known[k] = v2
                    eng.wait_ge(s, v2)
                ins = o.fn(eng)
                if o.dma:
                    ins.then_inc(dsems[o.tok[0]], 16)
                elif o.needed:
                    ins.then_inc(sems[qn], 1)
            if qn == "sp":
                for tok in finals:
                    k, v = tok
                    if isinstance(k, str):
                        eng.wait_ge(sems[k], remap[tok])
                    else:
                        eng.wait_ge(dsems[k], v)

        with nc.Block() as block:
            @block.sync
            def _(e):
                run_queue("sp", e)

            @block.tensor
            def _(e):
                run_queue("pe", e)

            @block.scalar
            def _(e):
                run_queue("act", e)

            @block.vector
            def _(e):
                run_queue("dve", e)

            @block.gpsimd
            def _(e):
                run_queue("pool", e)


class Ring:
    def __init__(self, fw, name, n, shape, dt=F32):
        self.tiles = [fw.sb("%s%d" % (name, i), shape, dt) for i in range(n)]
        self.i = 0

    def next(self):
        t = self.tiles[self.i]
        self.i = (self.i + 1) % len(self.tiles)
        return t


D = 1024
DFF = 2816
NMETA = 16
DI = 2048
NH = 32
HD = 64
NG = 4
DS = 128
CK = 4
CD = 3072
MH = 16
QL = 512
KL = 256
NOPE = 64
ROPE = 32
VH = 64
EPS = 1e-6
SCALE = (NOPE + ROPE) ** -0.5
NIN = 8000
NSAMP = 4
TS = 8
NEGV = -30000.0


def tiles_of(a, b, step):
    out = []
    c = a
    while c < b:
        out.append((c, min(step, b - c)))
        c += step
    return out


def build(SEQ, NPG, NPOOL, debug=()):
    U = 3 + NMETA + SEQ + NSAMP * 11
    P0 = 3
    PL = NMETA + SEQ
    SB0 = 3 + PL
    nc = bass.Bass("TRN2", target_bir_lowering=False)
    fw = FW(nc)
    fw.init_banks()

    def din(name, shape, dt=F32):
        return nc.dram_tensor(name, list(shape), dt, kind="ExternalInput").ap()

    def dout(name, shape, dt=F32):
        return nc.dram_tensor(name, list(shape), dt, kind="ExternalOutput").ap()

    def dscr(name, shape, dt=F32):
        kind = "ExternalOutput" if name in debug else "Internal"
        return nc.dram_tensor(name, list(shape), dt, kind=kind).ap()

    xin = din("xin", [U, D])
    W = {}
    for nm, shp in [("ffn1_w_gate", [D, DFF]), ("ffn1_w_up", [D, DFF]), ("ffn1_w_down", [DFF, D]),
                    ("ffn2_w_gate", [D, DFF]), ("ffn2_w_up", [D, DFF]), ("ffn2_w_down", [DFF, D]),
                    ("w_in", [D, NIN]), ("w_kpe_sw", [D, ROPE]), ("w_q_b", [QL, MH * 96]), ("w_q_b_sw", [QL, MH * 96]),
                    ("w_uk", [KL, MH * NOPE]), ("w_ukT", [NOPE, MH * KL]), ("w_uv", [KL, MH * VH]),
                    ("w_a_out", [DI, D]), ("w_b_out", [MH * VH, D]), ("w_o", [D, D])]:
        W[nm] = din(nm, shp)
    gcols = din("gcols", [128, 3, 8])
    grows = din("grows", [1, 3 * D + DI + QL + KL + 3 * NH])
    gncol = din("gncol", [128, 16])
    convw = din("convw", [128, 24 * 4])
    convb = din("convb", [128, 24])
    cf32 = din("cf32", [128, 128 * 4 + NH * 128 + 1])
    cbf = din("cbf", [128, 128 + 4 * 512 + 128])
    sel3_d = din("sel3", [96, NH * 128])
    cs96 = din("cs96", [96, 2, U])
    csU = din("csU", [U, 2, 32])
    cache_kv = din("cache_kv", [NPOOL * 32, 4 * KL])
    cache_kr = din("cache_kr", [NPOOL * 32, 4 * ROPE])
    ptab = din("ptab", [1, NSAMP * NPG], I32)
    st_ssm = din("st_ssm", [NSAMP, DI, DS])
    st_conv = din("st_conv", [NSAMP * 3, CD])

    yout = dout("yout", [U, D])
    kvlat = dout("kvlat", [U, KL])
    krope = dout("krope", [U, ROPE])
    ssm_o = dout("ssm_o", [1 + NSAMP, DI, DS])
    conv_o = dout("conv_o", [47, CD])

    res1 = dscr("res1", [U, D])
    res2 = dscr("res2", [U, D])
    zs_d = dscr("zs_d", [U, DI], BF16)
    xc_d = dscr("xc_d", [24, 128, U], BF16)
    dtr_d = dscr("dtr_d", [U, NH])
    gT_d = dscr("gT_d", [16, 128, U], BF16)
    mT_d = dscr("mT_d", [8, 128, U], BF16)
    oT_d = dscr("oT_d", [MH, 64, U], BF16)
    R = {k: fw.region(k) for k in ["xin", "w", "res1", "res2", "zs", "xc", "dtr", "gT", "mT", "oT", "yout", "kvlat", "krope", "ssm_o", "conv_o", "consts", "cache"]}

    identf = fw.sb("identf", [128, 128])
    tri = fw.sb("tri", [128, 128])
    tris = fw.sb("tris", [128, 128])
    onesf = fw.sb("onesf", [128, 128])
    identb = fw.sb("identb", [128, 128], BF16)
    cbf_sb = fw.sb("cbf_sb", [128, 128 + 4 * 512 + 128], BF16)
    gcol_sb = fw.sb("gcol_sb", [128, 3, 8])
    eps_sb = fw.sb("eps_sb", [128, 1])
    fw.dma("sp", (identf, identf[:]), (R["consts"], cf32[:, 0:128]))
    fw.dma("sp", (tri, tri[:]), (R["consts"], cf32[:, 128:256]))
    fw.dma("sp", (tris, tris[:]), (R["consts"], cf32[:, 256:384]))
    fw.dma("sp", (onesf, onesf[:]), (R["consts"], cf32[:, 384:512]))
    fw.dma("pool", (cbf_sb, cbf_sb[:]), (R["consts"], cbf))
    fw.dma("sp", (gcol_sb, gcol_sb[:]), (R["consts"], gcols))
    fw.cp("dve", (identb, identb[:]), (identf, identf[:]))
    fw.memset("dve", (eps_sb, eps_sb[:]), EPS)
    negsl = cbf_sb[:, 0:128]
    qaT = fw.sb("qaT", [128, 4, U], BF16)
    ckvT = fw.sb("ckvT", [128, 2, U], BF16)
    kpeT = fw.sb("kpeT", [96, U], BF16)

    def negm(r):
        return cbf_sb[:, 128 + r * 512:128 + (r + 1) * 512]
    neg8 = cbf_sb[0:8, 128 + 2048:128 + 2048 + 128]

    def bcast_row(name, off, n, dt=F32):
        t = fw.sb(name, [128, n], dt)
        fw.dma("sp" if dt == F32 else "pool", (t, t[:]), (R["consts"], grows[0, off:off + n].partition_broadcast(128)))
        return t

    rowtiles = tiles_of(0, U, 128)

    def rstd_from(ss_tt, ss_ap, n, rows, out_tt, out_ap):
        fw.act((out_tt, out_ap), (ss_tt, ss_ap), AF.Sqrt, bias=(eps_sb, eps_sb[0:rows, :]), scale=1.0 / n)
        fw.recip((out_tt, out_ap), (out_tt, out_ap))

    class _H:
        pass
    NR = _H()
    st_ring = Ring(fw, "stat", 4, [128, 8])
    nr_cnt = [0]

    def alloc_norm_rings():
        nr_cnt[0] += 1
        NR.xrow = Ring(fw, "xrow%d_" % nr_cnt[0], 2, [128, D])
        NR.junk = Ring(fw, "junk%d_" % nr_cnt[0], 2, [128, D])
        NR.xn = Ring(fw, "xnb%d_" % nr_cnt[0], 2, [128, D], BF16)

    def norm_transpose(src_ap, src_reg, which, xT, c_lo, c_hi):
        for (r0, rn) in tiles_of(c_lo, c_hi, 128):
            xr = NR.xrow.next()
            fw.dma("sp", (xr, xr[0:rn, :]), (src_reg, src_ap[r0:r0 + rn, :]))
            jk = NR.junk.next()
            st = st_ring.next()
            fw.act((jk, jk[0:rn, :]), (xr, xr[0:rn, :]), AF.Square, accum=(st, st[0:rn, 0:1]))
            rstd_from(st, st[0:rn, 0:1], D, rn, st, st[0:rn, 1:2])
            xn = NR.xn.next()
            fw.ts("dve", (xn, xn[0:rn, :]), (xr, xr[0:rn, :]), (st, st[0:rn, 1:2]), ALU.mult)
            for half in range(2):
                bk = fw.bank()
                bv = bk.t[:].bitcast(BF16)
                for kk in range(4):
                    k = half * 4 + kk
                    fw.tr((bk, bv[:, kk * 128:kk * 128 + rn]), (xn, xn[0:rn, k * 128:(k + 1) * 128]), (identb, identb[0:rn, 0:rn]))
                for kk in range(4):
                    k = half * 4 + kk
                    eng = "dve" if half == 0 else "pool"
                    if eng == "pool":
                        fw.act((xT, xT[:, k, r0 - c_lo:r0 - c_lo + rn]), (bk, bv[:, kk * 128:kk * 128 + rn]), AF.Copy,
                               scale=(gcol_sb, gcol_sb[:, which, k:k + 1]))
                    else:
                        fw.ts("dve", (xT, xT[:, k, r0 - c_lo:r0 - c_lo + rn]), (bk, bv[:, kk * 128:kk * 128 + rn]),
                              (gcol_sb, gcol_sb[:, which, k:k + 1]), ALU.mult)

    def postnorm_residual(banks, rn, g_bc, res_ap, res_reg, dst_ap, dst_reg, r0, final=False):
        st = st_ring.next()
        for half in range(2):
            jk = NR.junk.next()
            fw.act((jk, jk[0:rn, 0:512]), (banks[half], banks[half].t[0:rn, :]), AF.Square, accum=(st, st[0:rn, half:half + 1]))
        fw.tt("dve", (st, st[0:rn, 2:3]), (st, st[0:rn, 0:1]), (st, st[0:rn, 1:2]), ALU.add)
        rstd_from(st, st[0:rn, 2:3], D, rn, st, st[0:rn, 3:4])
        xr = NR.xrow.next()
        fw.dma("sp", (xr, xr[0:rn, :]), (res_reg, res_ap[r0:r0 + rn, :]))
        o = NR.junk.next()
        for half in range(2):
            fw.stt((o, o[0:rn, half * 512:(half + 1) * 512]), (banks[half], banks[half].t[0:rn, :]), (st, st[0:rn, 3:4]),
                   (g_bc, g_bc[0:rn, half * 512:(half + 1) * 512]), ALU.mult, ALU.mult)
        fw.tt("pool", (o, o[0:rn, :]), (o, o[0:rn, :]), (xr, xr[0:rn, :]), ALU.add)
        fw.dma("sp", (dst_reg, dst_ap[r0:r0 + rn, :]), (o, o[0:rn, :]), final=final)

    wring_h = [None]

    def load_w(w_ap, c0, cn, kchunks=8):
        t = wring_h[0].next()
        fw.dma("pool", (t, t[:, 0:kchunks, 0:cn]), (R["w"], w_ap[:, c0:c0 + cn].rearrange("(k p) n -> p k n", p=128)))
        return t

    def ffn(pfx, which_pre, gpost_off, src_ap, src_reg, dst_ap, dst_reg, final):
        fw.push_scope()
        alloc_norm_rings()
        wring_h[0] = Ring(fw, "wst_" + pfx, 4, [128, 8, 512], BF16)
        gpost = bcast_row("gpost_" + pfx, gpost_off, D)
        fw.ts("dve", (gpost, gpost[:]), (gpost, gpost[:]), 0.5, ALU.mult)
        supers = tiles_of(0, U, 768)
        supers = [(a, a + n) for (a, n) in supers]
        maxc = max(b - a for a, b in supers)
        xT = fw.sb("xT_" + pfx, [128, 8, maxc], BF16)
        hT = fw.sb("hT_" + pfx, [128, 22, maxc], BF16)
        wd = fw.sb("wd_" + pfx, [128, 22, D], BF16)
        for kk in range(0, 22, 2):
            fw.dma("pool", (wd, wd[:, kk:kk + 2, :]), (R["w"], W[pfx + "_w_down"][kk * 128:(kk + 2) * 128, :].rearrange("(k p) n -> p k n", p=128)))
        sg_ring = Ring(fw, "sg_" + pfx, 2, [128, 512])
        for (c_lo, c_hi) in supers:
            norm_transpose(src_ap, src_reg, which_pre, xT, c_lo, c_hi)
            for (g0, gn) in tiles_of(0, DFF, 512):
                wg = load_w(W[pfx + "_w_gate"], g0, gn)
                wu = load_w(W[pfx + "_w_up"], g0, gn)
                for jj in range(gn // 128):
                    j = g0 // 128 + jj
                    for (t0, tn) in tiles_of(0, c_hi - c_lo, 512):
                        bg = fw.bank()
                        bu = fw.bank()
                        for k in range(8):
                            fw.mm((bg, bg.t[:, 0:tn]), (wg, wg[:, k, jj * 128:(jj + 1) * 128]), (xT, xT[:, k, t0:t0 + tn]), k == 0, k == 7)
                        for k in range(8):
                            fw.mm((bu, bu.t[:, 0:tn]), (wu, wu[:, k, jj * 128:(jj + 1) * 128]), (xT, xT[:, k, t0:t0 + tn]), k == 0, k == 7)
                        sg = sg_ring.next()
                        fw.act((sg, sg[:, 0:tn]), (bg, bg.t[:, 0:tn]), AF.Silu)
                        fw.tt("dve", (hT, hT[:, j, t0:t0 + tn]), (sg, sg[:, 0:tn]), (bu, bu.t[:, 0:tn]), ALU.mult)
            for (r0, rn) in tiles_of(c_lo, c_hi, 128):
                bks = [fw.bank(), fw.bank()]
                for half in range(2):
                    for j in range(22):
                        fw.mm((bks[half], bks[half].t[0:rn, :]), (hT, hT[:, j, r0 - c_lo:r0 - c_lo + rn]),
                              (wd, wd[:, j, half * 512:(half + 1) * 512]), j == 0, j == 21)
                postnorm_residual(bks, rn, gpost, src_ap, src_reg, dst_ap, dst_reg, r0, final=final)
        fw.pop_scope()


    OFF_Z, OFF_X, OFF_DT, OFF_QA, OFF_KV, OFF_KPE, OFF_GA = 0, DI, DI + CD, DI + CD + NH, DI + CD + NH + QL, DI + CD + NH + QL + KL, DI + CD + NH + QL + KL + ROPE
    GOFF = {"ssm": 3 * D, "qa": 3 * D + DI, "kv": 3 * D + DI + QL, "dtb": 3 * D + DI + QL + KL, "alog": 3 * D + DI + QL + KL + NH, "dsk": 3 * D + DI + QL + KL + 2 * NH}
    SB = [SB0 + 11 * j for j in range(NSAMP)]

    def mix_proj():
        fw.push_scope()
        alloc_norm_rings()
        wring_h[0] = Ring(fw, "wst_mp", 4, [128, 8, 512], BF16)
        xT = fw.sb("xTm", [128, 8, U], BF16)
        norm_transpose(res1, R["res1"], 1, xT, 0, U)
        o512 = Ring(fw, "o512", 3, [128, 512], BF16)
        for (g0, gn) in tiles_of(OFF_Z, OFF_X, 512):
            wt = load_w(W["w_in"], g0, gn)
            for (r0, rn) in rowtiles:
                b = fw.bank()
                for k in range(8):
                    fw.mm((b, b.t[0:rn, :]), (xT, xT[:, k, r0:r0 + rn]), (wt, wt[:, k, :]), k == 0, k == 7)
                o = o512.next()
                fw.act((o, o[0:rn, :]), (b, b.t[0:rn, :]), AF.Silu)
                fw.dma("sp", (R["zs"], zs_d[r0:r0 + rn, g0:g0 + gn]), (o, o[0:rn, :]))
        convw_sb = fw.sb("convw_sb", [128, 96])
        convb_sb = fw.sb("convb_sb", [128, 24])
        fw.dma("sp", (convw_sb, convw_sb[:]), (R["consts"], convw))
        fw.dma("sp", (convb_sb, convb_sb[:]), (R["consts"], convb))
        diag = fw.sb("diag", [128, 24, 4, 128], BF16)
        for c in range(24):
            for k in range(4):
                fw.ts("dve", (diag, diag[:, c, k, :]), (identf, identf[:]), (convw_sb, convw_sb[:, c * 4 + k:c * 4 + k + 1]), ALU.mult)
        stc = fw.sb("stc", [12, CD])
        fw.dma("sp", (stc, stc[:]), (R["consts"], st_conv))
        scT = fw.sb("scT", [128, 24, 12], BF16)
        b = fw.bank()
        for c in range(24):
            fw.tr((b, b.t[:, c * 12:(c + 1) * 12]), (stc, stc[0:12, c * 128:(c + 1) * 128]), (identf, identf[0:12, 0:12]))
        fw.cp("dve", (scT, scT[:].rearrange("p a b -> p (a b)")), (b, b.t[:, 0:288]))
        xpre_ring = Ring(fw, "xpre", 3, [128, U], BF16)
        pend_conv = [None]
        cn_ring = Ring(fw, "cn", 2, [47, 512])
        for gi, (g0, gn) in enumerate(tiles_of(OFF_X, OFF_DT, 512)):
            wt = load_w(W["w_in"], g0, gn)
            b = fw.bank()
            for k in range(8):
                fw.mm((b, b.t[0:47, :]), (xT, xT[:, k, U - 47:U]), (wt, wt[:, k, :]), k == 0, k == 7)
            cn = cn_ring.next()
            fw.cp("dve", (cn, cn[:]), (b, b.t[0:47, :]))
            fw.dma("sp", (R["conv_o"], conv_o[:, gi * 512:(gi + 1) * 512]), (cn, cn[:]), final=True)
            for jj in range(4):
                c = gi * 4 + jj
                xp = xpre_ring.next()
                for ti, (t0, tn) in enumerate(tiles_of(0, U, 512)):
                    b = fw.bank()
                    for k in range(8):
                        fw.mm((b, b.t[:, 0:tn]), (wt, wt[:, k, jj * 128:(jj + 1) * 128]), (xT, xT[:, k, t0:t0 + tn]), k == 0, k == 7)
                    fw.cp("dve" if ti % 2 == 0 else "act", (xp, xp[:, t0:t0 + tn]), (b, b.t[:, 0:tn]))
                for j in range(NSAMP):
                    fw.cp("pool", (xp, xp[:, SB[j]:SB[j] + 3]), (scT, scT[:, c, 3 * j:3 * j + 3]))

                def conv(c=c, xp=xp):
                    for (t0, tn) in tiles_of(3, U, 512):
                        b = fw.bank()
                        for k in range(4):
                            fw.mm((b, b.t[:, 0:tn]), (diag, diag[:, c, k, :]), (xp, xp[:, t0 - 3 + k:t0 - 3 + k + tn]), k == 0, k == 3)
                        o = o512.next()
                        fw.act((o, o[:, 0:tn]), (b, b.t[:, 0:tn]), AF.Silu, bias=(convb_sb, convb_sb[:, c:c + 1]))
                        fw.dma("sp", (R["xc"], xc_d[c, :, t0:t0 + tn]), (o, o[:, 0:tn]))
                if pend_conv[0] is not None:
                    pend_conv[0]()
                pend_conv[0] = conv
        pend_conv[0]()
        wq = load_w(W["w_in"], OFF_QA, QL)
        wk = load_w(W["w_in"], OFF_KV, KL + ROPE)
        fw.dma("pool", (wk, wk[:, :, KL + ROPE:KL + 2 * ROPE]), (R["w"], W["w_kpe_sw"].rearrange("(k p) n -> p k n", p=128)))
        wdt = load_w(W["w_in"], OFF_DT, NH)
        gq = bcast_row("gq_bc", GOFF["qa"], QL)
        gkv = bcast_row("gkv_bc", GOFF["kv"], KL)
        cs_ring = Ring(fw, "csr", 2, [128, 2, 32])
        sm_ring = Ring(fw, "smr", 3, [128, 512])
        kpad = fw.sb("kpad", [128, 96], BF16)
        fw.memset("dve", (kpad, kpad[:]), 0.0)
        def small_mm(r0, rn):
            bq = fw.bank()
            bk2 = fw.bank()
            bd = fw.bank()
            for k in range(8):
                fw.mm((bq, bq.t[0:rn, :]), (xT, xT[:, k, r0:r0 + rn]), (wq, wq[:, k, 0:QL]), k == 0, k == 7)
            for k in range(8):
                fw.mm((bk2, bk2.t[0:rn, 0:320]), (xT, xT[:, k, r0:r0 + rn]), (wk, wk[:, k, 0:320]), k == 0, k == 7)
            for k in range(8):
                fw.mm((bd, bd.t[0:rn, 0:NH]), (xT, xT[:, k, r0:r0 + rn]), (wdt, wdt[:, k, 0:NH]), k == 0, k == 7)
            return bq, bk2, bd

        nxt = small_mm(*rowtiles[0])
        for ri, (r0, rn) in enumerate(rowtiles):
            bq, bk2, bd = nxt
            if ri + 1 < len(rowtiles):
                nxt = small_mm(*rowtiles[ri + 1])
            sm = sm_ring.next()
            fw.cp("dve", (sm, sm[0:rn, 0:NH]), (bd, bd.t[0:rn, 0:NH]))
            fw.dma("sp", (R["dtr"], dtr_d[r0:r0 + rn, :]), (sm, sm[0:rn, 0:NH]))
            st = st_ring.next()
            jk = NR.junk.next()
            fw.act((jk, jk[0:rn, 0:QL]), (bq, bq.t[0:rn, :]), AF.Square, accum=(st, st[0:rn, 0:1]))
            rstd_from(st, st[0:rn, 0:1], QL, rn, st, st[0:rn, 1:2])
            fw.act((jk, jk[0:rn, 512:512 + KL]), (bk2, bk2.t[0:rn, 0:KL]), AF.Square, accum=(st, st[0:rn, 2:3]))
            rstd_from(st, st[0:rn, 2:3], KL, rn, st, st[0:rn, 3:4])
            xn = NR.xn.next()
            fw.stt((xn, xn[0:rn, 0:QL]), (bq, bq.t[0:rn, :]), (st, st[0:rn, 1:2]), (gq, gq[0:rn, :]), ALU.mult, ALU.mult)
            ck = sm_ring.next()
            fw.stt((ck, ck[0:rn, 0:KL]), (bk2, bk2.t[0:rn, 0:KL]), (st, st[0:rn, 3:4]), (gkv, gkv[0:rn, :]), ALU.mult, ALU.mult)
            fw.dma("sp", (R["kvlat"], kvlat[r0:r0 + rn, :]), (ck, ck[0:rn, 0:KL]), final=True)
            fw.cp("pool", (xn, xn[0:rn, 512:512 + KL]), (ck, ck[0:rn, 0:KL]))
            cs = cs_ring.next()
            fw.dma("sp", (cs, cs[0:rn]), (R["consts"], csU[r0:r0 + rn]))
            kr = sm_ring.next()
            fw.tt("dve", (kr, kr[0:rn, 0:32]), (bk2, bk2.t[0:rn, KL:KL + 32]), (cs, cs[0:rn, 0, :]), ALU.mult)
            fw.tt("dve", (kr, kr[0:rn, 32:64]), (bk2, bk2.t[0:rn, KL + 32:KL + 64]), (cs, cs[0:rn, 1, :]), ALU.mult)
            fw.tt("dve", (kr, kr[0:rn, 0:32]), (kr, kr[0:rn, 0:32]), (kr, kr[0:rn, 32:64]), ALU.add)
            fw.dma("sp", (R["krope"], krope[r0:r0 + rn, :]), (kr, kr[0:rn, 0:32]), final=True)
            fw.cp("dve", (kpad, kpad[0:rn, 64:96]), (kr, kr[0:rn, 0:32]))
            bt = fw.bank()
            bv = bt.t[:].bitcast(BF16)
            for kk in range(4):
                fw.tr((bt, bv[:, kk * 128:kk * 128 + rn]), (xn, xn[0:rn, kk * 128:(kk + 1) * 128]), (identb, identb[0:rn, 0:rn]))
            for kk in range(2):
                fw.tr((bt, bv[:, (4 + kk) * 128:(4 + kk) * 128 + rn]), (xn, xn[0:rn, 512 + kk * 128:512 + (kk + 1) * 128]), (identb, identb[0:rn, 0:rn]))
            fw.tr((bt, bv[0:96, 6 * 128:6 * 128 + rn]), (kpad, kpad[0:rn, :]), (identb, identb[0:rn, 0:rn]))
            for kk in range(4):
                fw.cp("dve", (qaT, qaT[:, kk, r0:r0 + rn]), (bt, bv[:, kk * 128:kk * 128 + rn]))
            for kk in range(2):
                fw.cp("dve", (ckvT, ckvT[:, kk, r0:r0 + rn]), (bt, bv[:, (4 + kk) * 128:(4 + kk) * 128 + rn]))
            fw.cp("dve", (kpeT, kpeT[64:96, r0:r0 + rn]), (bt, bv[64:96, 6 * 128:6 * 128 + rn]))
        for gi, (g0, gn) in enumerate(tiles_of(OFF_GA, NIN, 512)):
            wt = load_w(W["w_in"], g0, gn)
            for jj in range(4):
                for (t0, tn) in tiles_of(0, U, 512):
                    b = fw.bank()
                    for k in range(8):
                        fw.mm((b, b.t[:, 0:tn]), (wt, wt[:, k, jj * 128:(jj + 1) * 128]), (xT, xT[:, k, t0:t0 + tn]), k == 0, k == 7)
                    o = o512.next()
                    fw.act((o, o[:, 0:tn]), (b, b.t[:, 0:tn]), AF.Sigmoid)
                    fw.dma("sp", (R["gT"], gT_d[gi * 4 + jj, :, t0:t0 + tn]), (o, o[:, 0:tn]))
        fw.pop_scope()


    def ssd():
        fw.push_scope()
        sel3 = fw.sb("sel3_sb", [96, NH, 128], BF16)
        fw.dma("pool", (sel3, sel3[:].rearrange("p a b -> p (a b)")), (R["consts"], sel3_d))
        s3_ring = Ring(fw, "s3r", 2, [96, 128], BF16)
        r1_ring = Ring(fw, "r1r", 2, [96, 128])
        mt_ring = Ring(fw, "mtr", 2, [96, 128], BF16)
        da3_ring = Ring(fw, "da3r", 2, [128, 3, NH])
        A_bc = bcast_row("A_bc", GOFF["alog"], NH)
        fw.act((A_bc, A_bc[:]), (A_bc, A_bc[:]), AF.Exp)
        fw.ts("dve", (A_bc, A_bc[:]), (A_bc, A_bc[:]), -1.0, ALU.mult)
        dtb_bc = bcast_row("dtb_bc", GOFF["dtb"], NH)
        Drep = bcast_row("Drep", GOFF["dsk"], NH)
        wa = fw.sb("wa", [128, 16, D], BF16)
        for kk in range(0, 16, 4):
            fw.dma("pool", (wa, wa[:, kk:kk + 4, :]), (R["w"], W["w_a_out"][kk * 128:(kk + 4) * 128, :].rearrange("(k p) n -> p k n", p=128)))
        gncol_sb = fw.sb("gncol_sb", [128, 16])
        fw.dma("sp", (gncol_sb, gncol_sb[:]), (R["consts"], gncol))
        for kk in range(16):
            fw.ts("dve", (wa, wa[:, kk, :]), (wa, wa[:, kk, :]), (gncol_sb, gncol_sb[:, kk:kk + 1]), ALU.mult)
        hT = fw.sb("hT", [128, DI])
        hTb = fw.sb("hTb", [128, DI], BF16)
        xc_ring = Ring(fw, "xcr", 2, [128, 24, 128], BF16)
        zs_ring = Ring(fw, "zsr", 2, [128, DI], BF16)
        sm = Ring(fw, "ssm_sm", 2, [128, 8, NH])
        cd_ring = Ring(fw, "cdr", 2, [128, NH])
        xs_ring = Ring(fw, "xstm", 2, [128, DI], BF16)
        xw_ring = Ring(fw, "xwtm", 2, [128, DI], BF16)
        xsD_ring = Ring(fw, "xsD", 2, [128, DI], BF16)
        B_ring = Ring(fw, "Btm", 2, [128, 512], BF16)
        cb_ring = Ring(fw, "cbT", 2, [128, 4, 128], BF16)
        dexp_ring = Ring(fw, "dexp", 3, [128, 128], BF16)
        MT_ring = Ring(fw, "MTg", 2, [128, 8, 128], BF16)
        y_ring = Ring(fw, "yall", 1, [128, DI])
        t_ring = Ring(fw, "ytmp", 2, [128, 512])
        ynb_ring = Ring(fw, "ynb", 1, [128, DI], BF16)
        ynT_ring = Ring(fw, "ynT", 1, [128, 16, 128], BF16)
        ga_ring = Ring(fw, "gar", 2, [128, 8, 128], BF16)
        mA_ring = Ring(fw, "mAr", 2, [128, 8, 128], BF16)
        so_ring = Ring(fw, "sor", 2, [128, 4, 128])

        STG = int(os.environ.get("SSD_STG", "9"))

        def front(c0, cl):
            xc = xc_ring.next()
            for q6 in range(0, 24, 4):
                fw.dma("sp", (xc, xc[:, q6:q6 + 4, 0:cl]), (R["xc"], xc_d[q6:q6 + 4, :, c0:c0 + cl].rearrange("c p u -> p c u")))
            zs = zs_ring.next()
            fw.dma("sp", (zs, zs[0:cl, :]), (R["zs"], zs_d[c0:c0 + cl, :]))
            s_ = sm.next()
            fw.dma("sp", (s_, s_[0:cl, 0, :]), (R["dtr"], dtr_d[c0:c0 + cl, :]))
            fw.tt("dve", (s_, s_[0:cl, 0, :]), (s_, s_[0:cl, 0, :]), (dtb_bc, dtb_bc[0:cl, :]), ALU.add)
            fw.act((s_, s_[0:cl, 0, :]), (s_, s_[0:cl, 0, :]), AF.Exp)
            fw.act((s_, s_[0:cl, 0, :]), (s_, s_[0:cl, 0, :]), AF.Ln, bias=1.0)
            da3 = da3_ring.next()
            fw.tt("dve", (da3, da3[0:cl]), (s_, s_[0:cl, 0:1, :].to_broadcast([cl, 3, NH])), (A_bc, A_bc[0:cl, :].unsqueeze(1).to_broadcast([cl, 3, NH])), ALU.mult)
            bs = fw.bank()
            da = (da3, da3[0:cl, 0, :])
            fw.mm((bs, bs.t[0:cl, 0:32]), (tri, tri[0:cl, 0:cl]), da, True, True)
            fw.mm((bs, bs.t[0:cl, 32:64]), (tris, tris[0:cl, 0:cl]), da, True, True)
            fw.mm((bs, bs.t[:, 64:96]), (onesf, onesf[0:cl, :]), da, True, True)
            fw.mm((bs, bs.t[0:96, 128:128 + cl]), (da3, da3[0:cl].rearrange("p a b -> p (a b)")), (tri, tri[0:cl, 0:cl]), True, True)
            fw.ts("dve", (s_, s_[0:cl, 2, :]), (bs, bs.t[0:cl, 0:32]), -1.0, ALU.mult)
            fw.act((s_, s_[0:cl, 3, :]), (bs, bs.t[0:cl, 0:32]), AF.Exp)
            fw.act((s_, s_[0:cl, 4, :]), (bs, bs.t[0:cl, 32:64]), AF.Exp)
            fw.tt("dve", (s_, s_[0:cl, 4, :]), (s_, s_[0:cl, 4, :]), (s_, s_[0:cl, 0, :]), ALU.mult)
            cd = cd_ring.next()
            fw.act((cd, cd[:]), (bs, bs.t[:, 64:96]), AF.Exp)
            acsT = s3_ring.next()
            r1 = r1_ring.next()
            mtp = mt_ring.next()
            fw.cp("dve", (acsT, acsT[0:96, 0:cl]), (bs, bs.t[0:96, 128:128 + cl]))
            fw.tt("dve", (r1, r1[32:64, 0:cl]), (bs, bs.t[32:64, 128:128 + cl]), (acsT, acsT[32:64, 0:cl]), ALU.subtract)
            fw.tt("dve", (r1, r1[64:96, 0:cl]), (bs, bs.t[64:96, 128:128 + cl]), (acsT, acsT[64:96, 0:cl]), ALU.subtract)
            fw.cp("dve", (acsT, acsT[32:64, 0:cl]), (r1, r1[32:64, 0:cl]))
            fw.cp("dve", (mtp, mtp[64:96, 0:cl]), (r1, r1[64:96, 0:cl]))
            fw.tt("dve", (r1, r1[64:96, 0:cl]), (r1, r1[64:96, 0:cl]), (mtp, mtp[64:96, 0:cl]), ALU.subtract)
            fw.cp("dve", (acsT, acsT[64:96, 0:cl]), (r1, r1[64:96, 0:cl]))
            xs = xs_ring.next()
            for half in range(2):
                bt = fw.bank()
                bv = bt.t[:].bitcast(BF16)
                for kk in range(8):
                    fw.tr((bt, bv[0:cl, kk * 128:(kk + 1) * 128]), (xc, xc[:, half * 8 + kk, 0:cl]), (identb, identb[:]))
                fw.cp("dve" if half == 0 else "act", (xs, xs[0:cl, half * 1024:(half + 1) * 1024]), (bt, bv[0:cl, :]))
            Bt = B_ring.next()
            bt = fw.bank()
            bv = bt.t[:].bitcast(BF16)
            for g in range(4):
                fw.tr((bt, bv[0:cl, g * 128:(g + 1) * 128]), (xc, xc[:, 16 + g, 0:cl]), (identb, identb[:]))
            fw.cp("dve", (Bt, Bt[0:cl, :]), (bt, bv[0:cl, 0:512]))
            xw = xw_ring.next()
            fw.tt("pool", (xw, xw[0:cl, :].rearrange("p (h d) -> p h d", h=NH)), (xs, xs[0:cl, :].rearrange("p (h d) -> p h d", h=NH)),
                  (s_, s_[0:cl, 4, :].unsqueeze(2).to_broadcast([cl, NH, HD])), ALU.mult)
            xsD = xsD_ring.next()
            fw.tt("pool", (xsD, xsD[0:cl, :].rearrange("p (h d) -> p h d", h=NH)), (xs, xs[0:cl, :].rearrange("p (h d) -> p h d", h=NH)),
                  (Drep, Drep[0:cl, :].unsqueeze(2).to_broadcast([cl, NH, HD])), ALU.mult)
            bc = fw.bank()
            for g in range(4):
                fw.mm((bc, bc.t[0:cl, g * 128:g * 128 + cl]), (xc, xc[:, 16 + g, 0:cl]), (xc, xc[:, 20 + g, 0:cl]), True, True)
            cbT = cb_ring.next()
            fw.cp("act", (cbT, cbT[0:cl].rearrange("p a b -> p (a b)")), (bc, bc.t[0:cl, :]))
            C = _H()
            C.c0, C.cl, C.xc, C.zs, C.s_, C.cd, C.acsT, C.xs, C.Bt, C.xw, C.cbT = c0, cl, xc, zs, s_, cd, acsT, xs, Bt, xw, cbT
            C.xsD = xsD
            C.MT = {}
            return C

        def stageA(C, g):
            c0, cl, xc, zs, s_, cd, acsT, xs, Bt, xw, cbT = C.c0, C.cl, C.xc, C.zs, C.s_, C.cd, C.acsT, C.xs, C.Bt, C.xw, C.cbT
            MT = MT_ring.next()
            C.MT[g] = MT
            for half in range(2):
                bm = fw.bank()
                for hh in range(4):
                    h = g * 8 + half * 4 + hh
                    fw.mm((bm, bm.t[0:cl, hh * 128:hh * 128 + cl]), (sel3, sel3[:, h, 0:cl]), (acsT, acsT[0:96, 0:cl]), True, False)
                    fw.mm((bm, bm.t[0:cl, hh * 128:hh * 128 + cl]), (identb, identb[0:cl, 0:cl]), (cbf_sb, negsl[0:cl, 0:cl]), False, True)
                for hh in range(4):
                    h = g * 8 + half * 4 + hh
                    de = dexp_ring.next()
                    fw.act((de, de[0:cl, 0:cl]), (bm, bm.t[0:cl, hh * 128:hh * 128 + cl]), AF.Exp, bias=(s_, s_[0:cl, 2, h:h + 1]))
                    fw.stt((MT, MT[0:cl, half * 4 + hh, 0:cl]), (de, de[0:cl, 0:cl]), (s_, s_[0:cl, 0, h:h + 1]), (cbT, cbT[0:cl, g, 0:cl]), ALU.mult, ALU.mult)

        def stageB(C, g):
            c0, cl, xc, zs, s_, cd, acsT, xs, Bt, xw, cbT = C.c0, C.cl, C.xc, C.zs, C.s_, C.cd, C.acsT, C.xs, C.Bt, C.xw, C.cbT
            MT = C.MT[g]
            if g == 0:
                C.yall = y_ring.next()
            yall = C.yall
            by = fw.bank()
            xsD = C.xsD
            fw.mm((by, by.t[0:cl, :]), (identb, identb[0:cl, 0:cl]), (xsD, xsD[0:cl, g * 512:(g + 1) * 512]), True, False)
            for hh in range(8):
                h = g * 8 + hh
                fw.mm((by, by.t[0:cl, hh * 64:(hh + 1) * 64]), (MT, MT[0:cl, hh, 0:cl]), (xs, xs[0:cl, h * 64:(h + 1) * 64]), False, hh == 7)
            bo = fw.bank()
            fw.mm((bo, bo.t[0:cl, :]), (xc, xc[:, 20 + g, 0:cl]), (hTb, hTb[:, g * 512:(g + 1) * 512]), True, True)
            yg = (yall, yall[0:cl, g * 512:(g + 1) * 512])
            fw.tt("dve", (yall, yall[0:cl, g * 512:(g + 1) * 512].rearrange("p (h d) -> p h d", h=8)), (bo, bo.t[0:cl, :].rearrange("p (h d) -> p h d", h=8)),
                  (s_, s_[0:cl, 3, g * 8:(g + 1) * 8].unsqueeze(2).to_broadcast([cl, 8, HD])), ALU.mult)
            fw.tt("dve", yg, yg, (by, by.t[0:cl, :]), ALU.add)
            tmp = t_ring.next()
            fw.tt("dve", yg, yg, (zs, zs[0:cl, g * 512:(g + 1) * 512]), ALU.mult)
            fw.act((tmp, tmp[0:cl, :]), yg, AF.Square, accum=(s_, s_[0:cl, 5, g:g + 1]))
            bst = fw.bank()
            fw.mm((bst, bst.t[:, :]), (Bt, Bt[0:cl, g * 128:(g + 1) * 128]), (xw, xw[0:cl, g * 512:(g + 1) * 512]), True, True)
            hg = (hT, hT[:, g * 512:(g + 1) * 512])
            fw.tt("dve", (hT, hT[:, g * 512:(g + 1) * 512].rearrange("p (h d) -> p h d", h=8)), (hT, hT[:, g * 512:(g + 1) * 512].rearrange("p (h d) -> p h d", h=8)),
                  (cd, cd[:, g * 8:(g + 1) * 8].unsqueeze(2).to_broadcast([128, 8, HD])), ALU.mult)
            fw.tt("dve", hg, hg, (bst, bst.t[:, :]), ALU.add)
            fw.cp("act", (hTb, hTb[:, g * 512:(g + 1) * 512]), hg)

        def tail(C):
            c0, cl, xc, zs, s_, cd, acsT, xs, Bt, xw, cbT = C.c0, C.cl, C.xc, C.zs, C.s_, C.cd, C.acsT, C.xs, C.Bt, C.xw, C.cbT
            yall = C.yall
            fw.act((s_, s_[0:cl, 5, 4:8]), (s_, s_[0:cl, 5, 0:4]), AF.Sqrt, bias=(eps_sb, eps_sb[0:cl, :]), scale=1.0 / 512)
            fw.recip((s_, s_[0:cl, 5, 4:8]), (s_, s_[0:cl, 5, 4:8]))
            ynb = ynb_ring.next()
            for g in range(4):
                fw.act((ynb, ynb[0:cl, g * 512:(g + 1) * 512]), (yall, yall[0:cl, g * 512:(g + 1) * 512]), AF.Copy, scale=(s_, s_[0:cl, 5, 4 + g:5 + g]))
            ynT = ynT_ring.next()
            for half in range(2):
                bt = fw.bank()
                bv = bt.t[:].bitcast(BF16)
                for kk in range(8):
                    fw.tr((bt, bv[:, kk * 128:kk * 128 + cl]), (ynb, ynb[0:cl, (half * 8 + kk) * 128:(half * 8 + kk + 1) * 128]), (identb, identb[0:cl, 0:cl]))
                fw.cp("dve" if half == 0 else "act", (ynT, ynT[:, half * 8:(half + 1) * 8, 0:cl]), (bt, bv[:, :].rearrange("p (a b) -> p a b", a=8)[:, :, 0:cl]))
            ga = ga_ring.next()
            fw.dma("sp", (ga, ga[:, :, 0:cl]), (R["gT"], gT_d[0:8, :, c0:c0 + cl].rearrange("c p u -> p c u")))
            mA = mA_ring.next()
            for half in range(2):
                bw = fw.bank()
                for oo in range(4):
                    oc = half * 4 + oo
                    for k in range(16):
                        fw.mm((bw, bw.t[:, oo * 128:oo * 128 + cl]), (wa, wa[:, k, oc * 128:(oc + 1) * 128]), (ynT, ynT[:, k, 0:cl]), k == 0, k == 15)
                fw.tt("dve", (mA, mA[:, half * 4:(half + 1) * 4, 0:cl]), (bw, bw.t[:, :].rearrange("p (a b) -> p a b", a=4)[:, :, 0:cl]), (ga, ga[:, half * 4:(half + 1) * 4, 0:cl]), ALU.mult)
            fw.dma("sp", (R["mT"], mT_d[:, :, c0:c0 + cl].rearrange("c p u -> p c u")), (mA, mA[:, :, 0:cl]))


        def run_seq(chunks):
            Cs = {}
            seq = [(ci, g) for ci in range(len(chunks)) for g in range(4)]
            Cs[0] = front(*chunks[0])
            stageA(Cs[0], 0)
            for n, (ci, g) in enumerate(seq):
                if n + 1 < len(seq):
                    ci2, g2 = seq[n + 1]
                    if g2 == 0:
                        Cs[ci2] = front(*chunks[ci2])
                    stageA(Cs[ci2], g2)
                stageB(Cs[ci], g)
                if g == 3:
                    tail(Cs[ci])
                    del Cs[ci]

        def store_state(idx):
            if STG < 1 and STG != -1:
                return
            for q4 in range(4):
                bt = fw.bank()
                for kk in range(4):
                    k = q4 * 4 + kk
                    fw.tr((bt, bt.t[:, kk * 128:(kk + 1) * 128]), (hT, hT[:, k * 128:(k + 1) * 128]), (identf, identf[:]))
                so = so_ring.next()
                fw.cp("dve", (so, so[:].rearrange("p a b -> p (a b)")), (bt, bt.t[:, :]))
                fw.dma("sp", (R["ssm_o"], ssm_o[idx, q4 * 512:(q4 + 1) * 512, :].rearrange("(k p) n -> p k n", p=128)), (so, so[:]), final=True)

        fw.memset("dve", (hT, hT[:]), 0.0)
        fw.memset("pool", (hTb, hTb[:]), 0.0)
        run_seq([(P0, NMETA)] + [(P0 + NMETA + 128 * i, 128) for i in range(SEQ // 128)])
        store_state(0)
        for j in range(NSAMP if STG != -2 else 0):
            for q4 in range(4):
                si = so_ring.next()
                fw.dma("sp", (si, si[:]), (R["consts"], st_ssm[j, q4 * 512:(q4 + 1) * 512, :].rearrange("(k p) n -> p k n", p=128)))
                if STG == -3:
                    continue
                bt = fw.bank()
                for kk in range(4):
                    fw.tr((bt, bt.t[:, kk * 128:(kk + 1) * 128]), (si, si[:, kk, :]), (identf, identf[:]))
                if STG == -4:
                    continue
                fw.cp("dve", (hT, hT[:, q4 * 512:(q4 + 1) * 512]), (bt, bt.t[:, :]))
                if STG == -5:
                    continue
                fw.cp("act", (hTb, hTb[:, q4 * 512:(q4 + 1) * 512]), (bt, bt.t[:, :]))
            run_seq([(SB[j] + 3, TS)])
            store_state(1 + j)
        fw.pop_scope()


    def attn():
        fw.push_scope()
        ASTG = int(os.environ.get("ATT_STG", "9"))
        NST = (PL + 127) // 128
        NPG4 = NPG // 4
        qlatT = fw.sb("qlatT", [128, 2, NSAMP, 128], BF16)
        qpeT = fw.sb("qpeT", [96, NSAMP, 128], BF16)
        wuv = fw.sb("wuv", [128, 2, MH * VH], BF16)
        fw.dma("pool", (wuv, wuv[:]), (R["w"], W["w_uv"].rearrange("(k p) n -> p k n", p=128)))
        fw.push_scope()
        wqb = fw.sb("wqb", [128, 4, MH * 96], BF16)
        wqbs = fw.sb("wqbs", [128, 4, MH * 96], BF16)
        fw.dma("pool", (wqb, wqb[:]), (R["w"], W["w_q_b"].rearrange("(k p) n -> p k n", p=128)))
        fw.dma("pool", (wqbs, wqbs[:]), (R["w"], W["w_q_b_sw"].rearrange("(k p) n -> p k n", p=128)))
        wukp = fw.sb("wukp", [128, 2, MH, 96], BF16)
        fw.memset("dve", (wukp, wukp[:]), 0.0)
        for kc in range(2):
            fw.dma("pool", (wukp, wukp[:, kc, :, 0:NOPE]), (R["w"], W["w_uk"][kc * 128:(kc + 1) * 128, :].rearrange("p (h n) -> p h n", h=MH)))
        wukT = fw.sb("wukT", [NOPE, MH, KL], BF16)
        fw.dma("pool", (wukT, wukT[:]), (R["w"], W["w_ukT"].rearrange("n (h c) -> n h c", h=MH)))
        cs96_sb = fw.sb("cs96_sb", [96, 2, U])
        fw.dma("sp", (cs96_sb, cs96_sb[:]), (R["consts"], cs96))
        V_sb = fw.sb("V_sb", [128, NST, MH, VH + 1], BF16)
        fw.memset("pool", (V_sb, V_sb[:]), 1.0)
        QT = fw.sb("QT", [96, 4, U], BF16)
        KT = fw.sb("KT", [96, 4, U], BF16)
        PT_ring = Ring(fw, "PT", 5, [128, 512], BF16)
        tq_ring = Ring(fw, "tq", 2, [96, 2, 512])
        rd_ring = Ring(fw, "rd", 2, [65, 512])
        bc_ring = Ring(fw, "bcr", 2, [64, 512])
        o_ring = Ring(fw, "osb", 2, [64, 512], BF16)
        for st in range(NST):
            s0 = P0 + 128 * st
            sn = min(128, P0 + PL - s0)
            for half in range(2):
                b = fw.bank()
                for kc in range(2):
                    fw.mm((b, b.t[0:sn, :]), (ckvT, ckvT[:, kc, s0:s0 + sn]), (wuv, wuv[:, kc, half * 512:(half + 1) * 512]), kc == 0, kc == 1)
                fw.cp("dve" if half == 0 else "act", (V_sb, V_sb[0:sn, st, half * 8:(half + 1) * 8, 0:VH]), (b, b.t[0:sn, :].rearrange("p (h v) -> p h v", h=8)))
        accs = fw.reserve(2)
        acc_i = 0
        pend_fin = [None]
        qtiles = tiles_of(0, PL, 512)
        for hg in range(MH // 4):
            for hh in range(4):
                h = hg * 4 + hh
                for (t0, tn) in tiles_of(0, U, 512):
                    bA = fw.bank()
                    bB = fw.bank()
                    for k in range(4):
                        fw.mm((bA, bA.t[0:96, 0:tn]), (wqb, wqb[:, k, h * 96:(h + 1) * 96]), (qaT, qaT[:, k, t0:t0 + tn]), k == 0, k == 3)
                    for k in range(4):
                        fw.mm((bB, bB.t[0:96, 0:tn]), (wqbs, wqbs[:, k, h * 96:(h + 1) * 96]), (qaT, qaT[:, k, t0:t0 + tn]), k == 0, k == 3)
                    tq = tq_ring.next()
                    fw.tt("dve", (tq, tq[:, 0, 0:tn]), (bA, bA.t[0:96, 0:tn]), (cs96_sb, cs96_sb[:, 0, t0:t0 + tn]), ALU.mult)
                    fw.tt("dve", (tq, tq[:, 1, 0:tn]), (bB, bB.t[0:96, 0:tn]), (cs96_sb, cs96_sb[:, 1, t0:t0 + tn]), ALU.mult)
                    fw.tt("pool", (QT, QT[:, hh, t0:t0 + tn]), (tq, tq[:, 0, 0:tn]), (tq, tq[:, 1, 0:tn]), ALU.add)
                    bK = fw.bank()
                    for kc in range(2):
                        fw.mm((bK, bK.t[0:96, 0:tn]), (wukp, wukp[:, kc, h, :]), (ckvT, ckvT[:, kc, t0:t0 + tn]), kc == 0, False)
                    fw.mm((bK, bK.t[0:96, 0:tn]), (identb, identb[64:96, 0:96]), (kpeT, kpeT[64:96, t0:t0 + tn]), False, True)
                    fw.cp("act", (KT, KT[:, hh, t0:t0 + tn]), (bK, bK.t[0:96, 0:tn]))
            for j in range(NSAMP if ASTG >= 2 else 0):
                cs8 = SB[j] + 3
                bq = fw.bank()
                for kc in range(2):
                    for hh in range(4):
                        h = hg * 4 + hh
                        fw.mm((bq, bq.t[:, kc * 32 + hh * 8:kc * 32 + hh * 8 + 8]), (wukT, wukT[:, h, kc * 128:(kc + 1) * 128]), (QT, QT[0:NOPE, hh, cs8:cs8 + 8]), True, True)
                fw.cp("dve", (qlatT, qlatT[:, :, j, hg * 32:(hg + 1) * 32]), (bq, bq.t[:, 0:64].rearrange("p (a b) -> p a b", a=2)))
                fw.cp("pool", (qpeT, qpeT[64:96, j, hg * 32:(hg + 1) * 32].rearrange("p (a b) -> p a b", a=4)), (QT, QT[64:96, :, cs8:cs8 + 8]))
            for hh in range(4 if ASTG >= 3 else 0):
                h = hg * 4 + hh
                for qi, (qp0, qn) in enumerate(qtiles):
                    q0 = P0 + qp0
                    bO = accs[acc_i]
                    acc_i = (acc_i + 1) % 2
                    sts = [st for st in range(NST) if st * 128 <= qp0 + qn - 1]
                    prevq = []

                    def do_pv(pv, bO=bO, qn=qn, h=h, nlast=len(sts) - 1):
                        PT_, st_, si_, sn_ = pv
                        fw.mm((bO, bO.t[0:VH + 1, 0:qn]), (V_sb, V_sb[0:sn_, st_, h, :]), (PT_, PT_[0:sn_, 0:qn]), si_ == 0, si_ == nlast)

                    for si, st in enumerate(sts):
                        s0 = P0 + 128 * st
                        sn = min(128, P0 + PL - s0)
                        diag = (st * 128 + sn - 1) > qp0
                        bS = fw.bank()
                        fw.mm((bS, bS.t[0:sn, 0:qn]), (KT, KT[:, hh, s0:s0 + sn]), (QT, QT[:, hh, q0:q0 + qn]), True, not diag)
                        if diag:
                            r = st - 4 * qi
                            fw.mm((bS, bS.t[0:sn, 0:qn]), (identb, identb[0:sn, 0:sn]), (cbf_sb, negm(r)[0:sn, 0:qn]), False, True)
                        PT = PT_ring.next()
                        fw.act((PT, PT[0:sn, 0:qn]), (bS, bS.t[0:sn, 0:qn]), AF.Exp)
                        if si == 1 and pend_fin[0] is not None:
                            pend_fin[0]()
                            pend_fin[0] = None
                        prevq.append((PT, st, si, sn))
                        if len(prevq) > 2:
                            do_pv(prevq.pop(0))
                    if pend_fin[0] is not None:
                        pend_fin[0]()
                        pend_fin[0] = None
                    while prevq:
                        do_pv(prevq.pop(0))

                    def fin(bO=bO, qn=qn, h=h, q0=q0):
                        rd = rd_ring.next()
                        fw.recip((rd, rd[64:65, 0:qn]), (bO, bO.t[64:65, 0:qn]))
                        bB = fw.bank()
                        fw.mm((bB, bB.t[0:64, 0:qn]), (onesf, onesf[64:65, 0:64]), (rd, rd[64:65, 0:qn]), True, True)
                        bcs = bc_ring.next()
                        fw.cp("act", (bcs, bcs[:, 0:qn]), (bB, bB.t[0:64, 0:qn]))
                        osb = o_ring.next()
                        fw.tt("dve", (osb, osb[:, 0:qn]), (bO, bO.t[0:64, 0:qn]), (bcs, bcs[:, 0:qn]), ALU.mult)
                        fw.dma("sp", (R["oT"], oT_d[h, :, q0:q0 + qn]), (osb, osb[:, 0:qn]))
                    pend_fin[0] = fin
        if pend_fin[0] is not None:
            pend_fin[0]()
            pend_fin[0] = None
        fw.pop_scope()
        fw.push_scope()
        ptb = fw.sb("ptb", [128, NSAMP * NPG], I32)
        fw.dma("sp", (ptb, ptb[:]), (R["consts"], ptab[0, :].partition_broadcast(128)))
        pm = fw.sb("pm", [128, 1])
        fw.dma("sp", (pm, pm[:]), (R["consts"], cf32[:, 512 + NH * 128:512 + NH * 128 + 1]), slow=True)
        idx = fw.sb("idx", [128, NSAMP * NPG4], I32)
        for qd in range(4):
            fw.ts("dve", (idx, idx[32 * qd:32 * qd + 32, :]), (ptb, ptb[32 * qd:32 * qd + 32, :].rearrange("p (i q) -> p i q", q=4)[:, :, qd]),
                  32.0, ALU.mult, s2=(pm, pm[32 * qd:32 * qd + 32, :]), op1=ALU.add)
        kvp_ring = Ring(fw, "kvp", 4, [128, 4, KL + 1], BF16)
        for t_ in kvp_ring.tiles:
            fw.memset("dve", (t_, t_[:]), 1.0)
        krg_ring = Ring(fw, "krg", 3, [128, 4, ROPE])
        kv32_ring = Ring(fw, "kv32", 2, [128, 4, KL])
        krp_ring = Ring(fw, "krp", 4, [128, 4, 96], BF16)
        ones_b = fw.sb("ones_b", [128, 8], BF16)
        fw.memset("dve", (ones_b, ones_b[:]), 1.0)
        for t_ in krp_ring.tiles:
            fw.memset("dve", (t_, t_[:]), 0.0)
        KTp_ring = Ring(fw, "KTp", 4, [128, 3, 128], BF16)
        PTs_ring = Ring(fw, "PTs", 4, [128, 128], BF16)
        ckn_ring = Ring(fw, "ckn", 2, [8, KL + 4], BF16)
        for t_ in ckn_ring.tiles:
            fw.memset("dve", (t_, t_[:]), 1.0)
        ol_ring = Ring(fw, "olat", 2, [128, KL], BF16)
        olT_ring = Ring(fw, "olatT", 2, [128, 2, 128], BF16)
        oS_ring = Ring(fw, "oS", 2, [64, MH, 8], BF16)
        sst = Ring(fw, "sst", 2, [128, 2])
        accs = fw.reserve(2)
        ckr = cache_kr.rearrange("r (a b) -> r a b", a=4)
        SUB = int(os.environ.get("SUB", "9"))
        for j in range(NSAMP if ASTG >= 4 else 0):
            cs8 = SB[j] + 3
            bO = accs[j % 2]
            grp = {}

            def do_gather(i, j=j, grp=grp):
                kvp = kvp_ring.next()
                krp = krp_ring.next()
                krg = krg_ring.next()
                kv32 = kv32_ring.next()
                ic = j * NPG4 + i
                fw.gather((kv32, kv32[:].rearrange("p a b -> p (a b)")), cache_kv, R["cache"], (idx, idx[:, ic:ic + 1]))
                fw.gather((krg, krg[:].rearrange("p a b -> p (a b)")), cache_kr, R["cache"], (idx, idx[:, ic:ic + 1]))
                fw.cp("act", (kvp, kvp[:, :, 0:KL]), (kv32, kv32[:]))
                fw.cp("pool", (krp, krp[:, :, 64:96]), (krg, krg[:]))
                grp[i] = (kvp, krp)

            items = [(i, r) for i in range(NPG4) for r in range(4)]
            st1 = {}
            st2 = {}

            def stage1(k, grp=grp, st1=st1):
                i, r = items[k]
                if r == 0:
                    if i == 0:
                        do_gather(0)
                    if i + 1 < NPG4:
                        do_gather(i + 1)
                kvp, krp = grp[i]
                bt = fw.bank()
                bv = bt.t[:].bitcast(BF16)
                fw.tr((bt, bv[:, 0:128]), (kvp, kvp[:, r, 0:128]), (identb, identb[:]))
                fw.tr((bt, bv[:, 128:256]), (kvp, kvp[:, r, 128:256]), (identb, identb[:]))
                fw.tr((bt, bv[0:96, 256:384]), (krp, krp[:, r, :]), (identb, identb[:]))
                KTp = KTp_ring.next()
                fw.cp("dve", (KTp, KTp[:, 0:2, :]), (bt, bv[:, 0:256].rearrange("p (a b) -> p a b", a=2)))
                fw.cp("dve", (KTp, KTp[0:96, 2, :]), (bt, bv[0:96, 256:384]))
                st1[k] = KTp

            def stage2(k, j=j, st1=st1, st2=st2):
                KTp = st1.pop(k)
                bS = fw.bank()
                fw.mm((bS, bS.t[:, 0:128]), (KTp, KTp[:, 0, :]), (qlatT, qlatT[:, 0, j, :]), True, False)
                fw.mm((bS, bS.t[:, 0:128]), (KTp, KTp[:, 1, :]), (qlatT, qlatT[:, 1, j, :]), False, False)
                fw.mm((bS, bS.t[:, 0:128]), (KTp, KTp[64:96, 2, :]), (qpeT, qpeT[64:96, j, :]), False, True)
                PTs = PTs_ring.next()
                fw.act((PTs, PTs[:]), (bS, bS.t[:, 0:128]), AF.Exp)
                st2[k] = PTs

            def stage3(k, bO=bO, grp=grp, st2=st2):
                i, r = items[k]
                PTs = st2.pop(k)
                kvp, krp = grp[i]
                fw.mm((bO, bO.t[:, 0:KL + 1]), (PTs, PTs[:]), (kvp, kvp[:, r, :]), k == 0, False)

            n_it = len(items) if SUB >= 2 else 0
            for k in range(n_it + 2):
                if k < n_it:
                    stage1(k)
                if 0 <= k - 1 < n_it:
                    stage2(k - 1)
                if 0 <= k - 2 < n_it:
                    stage3(k - 2)
            if SUB < 3:
                continue
            bS = fw.bank()
            fw.mm((bS, bS.t[0:8, 0:128]), (ckvT, ckvT[:, 0, cs8:cs8 + 8]), (qlatT, qlatT[:, 0, j, :]), True, False)
            fw.mm((bS, bS.t[0:8, 0:128]), (ckvT, ckvT[:, 1, cs8:cs8 + 8]), (qlatT, qlatT[:, 1, j, :]), False, False)
            fw.mm((bS, bS.t[0:8, 0:128]), (identb, identb[:, 0:8]), (cbf_sb, cbf_sb[:, 128 + 2048:128 + 2048 + 128]), False, False)
            fw.mm((bS, bS.t[0:8, 0:128]), (kpeT, kpeT[64:96, cs8:cs8 + 8]), (qpeT, qpeT[64:96, j, :]), False, True)
            PTs = PTs_ring.next()
            fw.act((PTs, PTs[0:8, :]), (bS, bS.t[0:8, 0:128]), AF.Exp)
            if SUB < 4:
                continue
            ckn = ckn_ring.next()
            fw.dma("pool", (ckn, ckn[:, 0:KL]), (R["kvlat"], kvlat[cs8:cs8 + 8, :]))
            fw.mm((bO, bO.t[:, 0:KL + 1]), (PTs, PTs[0:8, :]), (ckn, ckn[0:8, 0:KL + 1]), False, True)
            if SUB < 5:
                continue
            ss_ = sst.next()
            fw.recip((ss_, ss_[:, 0:1]), (bO, bO.t[:, KL:KL + 1]))
            ol = ol_ring.next()
            fw.ts("dve", (ol, ol[:]), (bO, bO.t[:, 0:KL]), (ss_, ss_[:, 0:1]), ALU.mult)
            if "dbg_ol" in debug and j == 0:
                dol = nc.dram_tensor("dbg_ol", [128, KL], BF16, kind="ExternalOutput").ap()
                dq = nc.dram_tensor("dbg_q", [128, 2, NSAMP, 128], BF16, kind="ExternalOutput").ap()
                dqp = nc.dram_tensor("dbg_qp", [96, NSAMP, 128], BF16, kind="ExternalOutput").ap()
                dden = nc.dram_tensor("dbg_den", [128, 2], F32, kind="ExternalOutput").ap()
                fw.dma("sp", (R["yout"], dol), (ol, ol[:]), final=True)
                fw.dma("sp", (R["yout"], dq), (qlatT, qlatT[:]), final=True)
                fw.dma("sp", (R["yout"], dqp), (qpeT, qpeT[:]), final=True)
                fw.dma("sp", (R["yout"], dden), (ss_, ss_[:]), final=True)
            bt = fw.bank()
            bv = bt.t[:].bitcast(BF16)
            for kc in range(2):
                fw.tr((bt, bv[:, kc * 128:(kc + 1) * 128]), (ol, ol[:, kc * 128:(kc + 1) * 128]), (identb, identb[:]))
            olT = olT_ring.next()
            fw.cp("dve", (olT, olT[:].rearrange("p a b -> p (a b)")), (bt, bv[:, 0:256]))
            if SUB < 6:
                continue
            b2 = fw.bank()
            for h in range(MH):
                for kc in range(2):
                    fw.mm((b2, b2.t[0:VH, h * 8:(h + 1) * 8]), (wuv, wuv[:, kc, h * VH:(h + 1) * VH]), (olT, olT[:, kc, h * 8:(h + 1) * 8]), kc == 0, kc == 1)
            oS = oS_ring.next()
            fw.cp("dve", (oS, oS[:].rearrange("p a b -> p (a b)")), (b2, b2.t[0:VH, 0:128]))
            fw.dma("sp", (R["oT"], oT_d[:, :, cs8:cs8 + 8].rearrange("h v u -> v h u")), (oS, oS[:]))
        fw.reserve(0)
        fw.pop_scope()
        fw.pop_scope()
        fw.push_scope()
        alloc_norm_rings()
        wbo = fw.sb("wbo", [VH, MH, D], BF16)
        for hq in range(0, MH, 4):
            fw.dma("pool", (wbo, wbo[:, hq:hq + 4, :]), (R["w"], W["w_b_out"][hq * VH:(hq + 4) * VH, :].rearrange("(h v) n -> v h n", v=VH)))
        wo = fw.sb("wo", [128, 8, D], BF16)
        for kk in range(0, 8, 4):
            fw.dma("pool", (wo, wo[:, kk:kk + 4, :]), (R["w"], W["w_o"][kk * 128:(kk + 4) * 128, :].rearrange("(k p) n -> p k n", p=128)))
        gmix = bcast_row("gmix", D, D)
        oT_ring = Ring(fw, "oTr", 2, [VH, MH, 512], BF16)
        gb_ring = Ring(fw, "gbr", 2, [128, 8, 512], BF16)
        mA2_ring = Ring(fw, "mA2", 2, [128, 8, 512], BF16)
        mS_ring = Ring(fw, "mS", 2, [128, 8, 512], BF16)
        tm_ring = Ring(fw, "tmr", 2, [128, 512])
        for (t0, tn) in (tiles_of(0, U, 512) if ASTG >= 5 else []):
            oTt = oT_ring.next()
            for hq in range(0, MH, 8):
                fw.dma("sp", (oTt, oTt[:, hq:hq + 8, 0:tn]), (R["oT"], oT_d[hq:hq + 8, :, t0:t0 + tn].rearrange("h v u -> v h u")))
            gb = gb_ring.next()
            fw.dma("sp", (gb, gb[:, :, 0:tn]), (R["gT"], gT_d[8:16, :, t0:t0 + tn].rearrange("c p u -> p c u")))
            mA2 = mA2_ring.next()
            fw.dma("sp", (mA2, mA2[:, :, 0:tn]), (R["mT"], mT_d[:, :, t0:t0 + tn].rearrange("c p u -> p c u")))
            mS = mS_ring.next()
            for oc in range(8):
                b = fw.bank()
                for h in range(MH):
                    fw.mm((b, b.t[:, 0:tn]), (wbo, wbo[:, h, oc * 128:(oc + 1) * 128]), (oTt, oTt[:, h, 0:tn]), h == 0, h == MH - 1)
                tm = tm_ring.next()
                fw.tt("dve", (tm, tm[:, 0:tn]), (b, b.t[:, 0:tn]), (gb, gb[:, oc, 0:tn]), ALU.mult)
                fw.tt("pool", (mS, mS[:, oc, 0:tn]), (tm, tm[:, 0:tn]), (mA2, mA2[:, oc, 0:tn]), ALU.add)
            for (r0, rn) in tiles_of(t0, t0 + tn, 128):
                bks = [fw.bank(), fw.bank()]
                for half in range(2):
                    for k in range(8):
                        fw.mm((bks[half], bks[half].t[0:rn, :]), (mS, mS[:, k, r0 - t0:r0 - t0 + rn]), (wo, wo[:, k, half * 512:(half + 1) * 512]), k == 0, k == 7)
                postnorm_residual(bks, rn, gmix, res1, R["res1"], res2, R["res2"], r0)
        fw.pop_scope()

    PH = build.phases
    if "ffn1" in PH:
        ffn("ffn1", 0, 0, xin, R["xin"], res1, R["res1"], final=("proj" not in PH))
    if "proj" in PH:
        mix_proj()
    if "ssd" in PH:
        ssd()
    if "attn" in PH:
        attn()
    if "ffn2" in PH:
        src2, reg2 = (res2, R["res2"]) if "attn" in PH else (res1, R["res1"])
        ffn("ffn2", 2, 2 * D, src2, reg2, yout, R["yout"], final=True)
    fw.emit()
    return nc


build.phases = ("ffn1", "proj", "ssd", "attn", "ffn2")


def host_consts(SEQ, PAST):
    U = 3 + NMETA + SEQ + NSAMP * 11
    ident = np.eye(128, dtype=np.float32)
    t = np.arange(128)
    tri = (t[:, None] <= t[None, :]).astype(np.float32)
    tris = (t[:, None] > t[None, :]).astype(np.float32)
    ones = np.ones((128, 128), np.float32)
    sel = np.zeros((128, NH, 128), np.float32)
    for h in range(NH):
        sel[h, h, :] = 1.0
    cf32 = np.concatenate([ident, tri, tris, ones, sel.reshape(128, NH * 128), (t % 32).astype(np.float32)[:, None]], axis=1)
    neg = np.where(t[None, :] < t[:, None], NEGV, 0.0).astype(np.float32)
    q = np.arange(512)
    negm = [np.where(q[None, :] < 128 * r + t[:, None], NEGV, 0.0).astype(np.float32) for r in range(4)]
    neg8 = np.zeros((128, 128), np.float32)
    for s_ in range(8):
        for h in range(MH):
            for tq in range(8):
                if tq < s_:
                    neg8[s_, h * 8 + tq] = NEGV
    cbf = np.concatenate([neg] + negm + [neg8], axis=1)
    pos = np.zeros(U, np.float32)
    pos[3:3 + NMETA + SEQ] = np.arange(NMETA + SEQ)
    for j in range(NSAMP):
        b0 = 3 + NMETA + SEQ + 11 * j
        pos[b0 + 3:b0 + 11] = PAST + np.arange(8)
    inv = (np.float32(10000.0) ** (-np.arange(0, ROPE, 2, dtype=np.float32) / np.float32(ROPE))).astype(np.float32)
    ang = pos[:, None].astype(np.float32) * inv[None, :]
    cos = np.cos(ang).astype(np.float32)
    sin = np.sin(ang).astype(np.float32)
    csU = np.stack([np.concatenate([cos, cos], 1), np.concatenate([-sin, sin], 1)], axis=1)
    cs96 = np.zeros((96, 2, U), np.float32)
    cs96[0:64, 0, :] = SCALE
    cs96[64:80, 0, :] = cos.T * SCALE
    cs96[80:96, 0, :] = cos.T * SCALE
    cs96[64:80, 1, :] = -sin.T * SCALE
    cs96[80:96, 1, :] = sin.T * SCALE
    sel3 = np.zeros((96, NH, 128), np.float32)
    for h in range(NH):
        for part in range(3):
            sel3[32 * part + h, h, :] = 1.0
    return dict(cf32=cf32, cbf=cbf, csU=np.ascontiguousarray(csU), cs96=cs96, sel3=sel3.reshape(96, NH * 128))


def swap_rope_cols(w, head_w, rope_off):
    w2 = w.copy()
    n = w.shape[1] // head_w
    for h in range(n):
        a = h * head_w + rope_off
        w2[:, a:a + 16] = w[:, a + 16:a + 32]
        w2[:, a + 16:a + 32] = w[:, a:a + 16]
    return w2


def host_shared(inp, SEQ, PAST):
    sh = host_consts(SEQ, PAST)
    for nm in ["ffn1_w_gate", "ffn1_w_up", "ffn1_w_down", "ffn2_w_gate", "ffn2_w_up", "ffn2_w_down", "w_in", "w_q_b",
               "w_a_out", "w_b_out", "w_o"]:
        sh[nm] = np.ascontiguousarray(inp[nm][0])
    w_in = inp["w_in"][0]
    kpe0 = DI + CD + NH + QL + KL
    sh["w_kpe_sw"] = np.ascontiguousarray(np.concatenate([w_in[:, kpe0 + 16:kpe0 + 32], w_in[:, kpe0:kpe0 + 16]], axis=1))
    sh["w_q_b_sw"] = swap_rope_cols(inp["w_q_b"][0], 96, 64)
    w_uk = inp["w_uk"][0]
    sh["w_uk"] = np.ascontiguousarray(w_uk.reshape(KL, MH * NOPE))
    sh["w_ukT"] = np.ascontiguousarray(w_uk.transpose(2, 1, 0).reshape(NOPE, MH * KL))
    sh["w_uv"] = np.ascontiguousarray(inp["w_uv"][0].reshape(KL, MH * VH))
    gc = np.stack([inp[k][0].reshape(8, 128).T for k in ["ffn1_pre_g", "mix_pre_g", "ffn2_pre_g"]], axis=1)
    sh["gcols"] = np.ascontiguousarray(gc.astype(np.float32))
    sh["grows"] = np.concatenate([inp[k][0].reshape(-1) for k in
                                  ["ffn1_post_g", "mix_post_g", "ffn2_post_g", "ssm_norm_g", "q_a_norm_g", "kv_a_norm_g",
                                   "dt_bias", "a_log", "d_skip"]]).astype(np.float32)[None, :]
    cw = inp["conv_w"][0]
    sh["convw"] = np.ascontiguousarray(cw.reshape(4, 24, 128).transpose(2, 1, 0).reshape(128, 96))
    sh["convb"] = np.ascontiguousarray(inp["conv_b"][0].reshape(24, 128).T)
    sh["gncol"] = np.ascontiguousarray(inp["ssm_norm_g"][0].reshape(16, 128).T)
    npool = inp["cache_kv_latent"].shape[1]
    sh["cache_kv"] = inp["cache_kv_latent"][0].reshape(npool * 32, 4 * KL)
    sh["cache_kr"] = inp["cache_k_rope"][0].reshape(npool * 32, 4 * ROPE)
    return sh


def host_core(inp, sh, b, SEQ):
    U = 3 + NMETA + SEQ + NSAMP * 11
    m = dict(sh)
    xin = np.zeros((U, D), np.float32)
    xin[3:3 + NMETA] = inp["meta_tokens"]
    xin[3 + NMETA:3 + NMETA + SEQ] = inp["x_prompt"][b]
    stc = np.zeros((NSAMP * 3, CD), np.float32)
    for j in range(NSAMP):
        b0 = 3 + NMETA + SEQ + 11 * j
        xin[b0 + 3:b0 + 11] = inp["x_sample"][NSAMP * b + j]
        stc[3 * j:3 * j + 3] = inp["state_conv"][0, NSAMP * b + j]
    m["xin"] = xin
    m["st_conv"] = stc
    m["st_ssm"] = np.ascontiguousarray(inp["state_ssm"][0, NSAMP * b:NSAMP * b + NSAMP].reshape(NSAMP, DI, DS))
    m["ptab"] = np.ascontiguousarray(inp["page_table"][NSAMP * b:NSAMP * b + NSAMP].reshape(1, -1)).astype(np.int32)
    return m


SEQ_FULL = 2048
NPG_FULL = 128


def kernel(**inputs):
    inp = {k: np.asarray(v) for k, v in inputs.items()}
    nb = inp["x_prompt"].shape[0]
    seq = inp["x_prompt"].shape[1]
    npg = inp["page_table"].shape[1]
    npool = inp["cache_kv_latent"].shape[1]
    past = npg * inp["cache_kv_latent"].shape[2]
    nsamp_total = inp["x_sample"].shape[0]
    U = 3 + NMETA + seq + NSAMP * 11
    PL = NMETA + seq
    nc = build(seq, npg, npool)
    sh = host_shared(inp, seq, past)
    in_maps = [host_core(inp, sh, b, seq) for b in range(nb)]
    res = run_bass_kernel_spmd(nc, in_maps, core_ids=list(range(nb)))
    y_prompt = np.zeros((nb, seq, D), np.float32)
    y_sample = np.zeros((nsamp_total, TS, D), np.float32)
    kvp = np.zeros((1, nb, PL, KL), np.float32)
    krp = np.zeros((1, nb, PL, ROPE), np.float32)
    ssp = np.zeros((1, nb, NH, HD, DS), np.float32)
    cvp = np.zeros((1, nb, CK - 1, CD), np.float32)
    kvs = np.zeros((1, nsamp_total, TS, KL), np.float32)
    krs = np.zeros((1, nsamp_total, TS, ROPE), np.float32)
    sss = np.zeros((1, nsamp_total, NH, HD, DS), np.float32)
    cvs = np.zeros((1, nsamp_total, CK - 1, CD), np.float32)
    for b in range(nb):
        r = res.results[b]
        yo = np.asarray(r["yout"]); kv = np.asarray(r["kvlat"]); kr = np.asarray(r["krope"])
        so = np.asarray(r["ssm_o"]); co = np.asarray(r["conv_o"])
        y_prompt[b] = yo[3 + NMETA:3 + PL]
        kvp[0, b] = kv[3:3 + PL]
        krp[0, b] = kr[3:3 + PL]
        ssp[0, b] = so[0].reshape(NH, HD, DS)
        cvp[0, b] = co[0:3]
        for j in range(NSAMP):
            sb = 3 + PL + 11 * j
            i = NSAMP * b + j
            y_sample[i] = yo[sb + 3:sb + 11]
            kvs[0, i] = kv[sb + 3:sb + 11]
            krs[0, i] = kr[sb + 3:sb + 11]
            sss[0, i] = so[1 + j].reshape(NH, HD, DS)
            cvs[0, i] = co[11 * j + 11:11 * j + 14]
    return (y_prompt, y_sample, kvp, krp, ssp, cvp, kvs, krs, sss, cvs)
```
